# Optimizing a Trainium2 kernel written in Bass

```python
import math
import jax, jax.numpy as jnp
from jax import lax
import numpy as np

D_MODEL = 1024
BATCH = 16
SEQ = 2048
DEPTH = 2

CTX_LEN = 256
GRID_W = 64
f32 = jnp.float32
RMS_EPS = 1e-6
ROPE_BASE = 10000.0
N_MOD = 9
D_FF = (11 * D_MODEL) // 4
MIX_W = D_MODEL
GROUP_W = MIX_W // 4

MLSTM_HEAD_DIM = 64
MLSTM_HEADS = GROUP_W // MLSTM_HEAD_DIM
MLSTM_CHUNK = 64
MLA_NOPE = 64
MLA_ROPE_DIM = 32
MLA_V = 64
MLA_HEADS = GROUP_W // MLA_V
MLA_Q_LORA = GROUP_W
MLA_KV_LORA = GROUP_W // 2
ATTN_BLK = 128
SWA_HEAD_DIM = 64
SWA_HEADS = GROUP_W // SWA_HEAD_DIM
SWA_KV_HEADS = SWA_HEADS // 2
SWA_WINDOW = 128
SWA_BLK = 128
SSD_HEAD_DIM = 64
SSD_HEADS = GROUP_W // SSD_HEAD_DIM
SSD_STATE = 64
SSD_GROUPS = 2
SSD_CONV = 5
SSD_CHUNK = 128
SSD_NORM_GROUP = GROUP_W // SSD_GROUPS
SSD_XBC = GROUP_W + 2 * SSD_GROUPS * SSD_STATE

IN_SEGMENTS = (
    ("m_q", GROUP_W), ("m_k", GROUP_W), ("m_v", GROUP_W), ("m_o", GROUP_W), ("m_gates", 4 * MLSTM_HEADS),
    ("a_q", MLA_Q_LORA), ("a_kv", MLA_KV_LORA), ("a_kr", MLA_ROPE_DIM),
    ("w_q", SWA_HEADS * SWA_HEAD_DIM), ("w_k", SWA_KV_HEADS * SWA_HEAD_DIM), ("w_v", SWA_KV_HEADS * SWA_HEAD_DIM),
    ("s_z", GROUP_W), ("s_xbc", SSD_XBC), ("s_dt", 2 * SSD_HEADS),
)
IN_COLS = sum(n for _, n in IN_SEGMENTS)

kernel_name = "hybrid_parallel_heads_diffusion_block"


def rms_norm(x, g, eps=RMS_EPS):
    xf = x.astype(f32)
    y = xf * lax.rsqrt(jnp.mean(xf * xf, axis=-1, keepdims=True) + eps)
    return (y * g.astype(f32)).astype(x.dtype)


def modulate(x, shift, scale):
    return x * (1 + scale) + shift


def swiglu(x, wi, wo):
    g, u = jnp.split(x @ wi, 2, axis=-1)
    return (jax.nn.silu(g) * u) @ wo


def split_cols(u):
    parts, o = {}, 0
    for name, n in IN_SEGMENTS:
        parts[name] = u[..., o:o + n]
        o += n
    return parts


def axial_rope(rows, rot_dim):
    row = jnp.repeat(jnp.arange(rows), GRID_W).astype(f32)
    col = jnp.tile(jnp.arange(GRID_W), rows).astype(f32)
    nf = rot_dim // 4
    inv = ROPE_BASE ** (-jnp.arange(nf, dtype=f32) / nf)
    ar, ac = row[:, None] * inv, col[:, None] * inv
    ang = jnp.concatenate([ar, ar, ac, ac], axis=-1)
    return jnp.cos(ang)[:, None, :], jnp.sin(ang)[:, None, :]


def apply_rope(x, cos, sin):
    r = x.shape[-1]
    x4 = x.reshape(x.shape[:-1] + (2, 2, r // 4))
    rot = jnp.stack([-x4[..., 1, :], x4[..., 0, :]], axis=-2).reshape(x.shape)
    return (x * cos + rot * sin).astype(x.dtype)


def to_chunks(a, L):
    b, h, t = a.shape[:3]
    return jnp.moveaxis(a.reshape((b, h, t // L, L) + a.shape[3:]), 2, 0)


def from_chunks(a):
    a = jnp.moveaxis(a, 0, 2)
    return a.reshape(a.shape[:2] + (a.shape[2] * a.shape[3],) + a.shape[4:])


def bidirectional(scan_fn, ctx_f, lat_f, ctx_b, lat_b, init):
    flip = lambda t: tuple(jnp.flip(a, axis=2) for a in t)
    h_cf, s_f = scan_fn(*ctx_f, init)
    h_lf, _ = scan_fn(*lat_f, s_f)
    h_cb, s_b = scan_fn(*flip(ctx_b), init)
    h_lb, _ = scan_fn(*flip(lat_b), s_b)
    return h_cf + jnp.flip(h_cb, axis=2), h_lf + jnp.flip(h_lb, axis=2)


def mlstm_scan(q, k, v, logi, logf, state):
    L = MLSTM_CHUNK
    tril = jnp.tril(jnp.ones((L, L), bool))

    def body(carry, inp):
        C, n, m = carry
        qc, kc, vc, ic, fc = inp
        b = jnp.cumsum(fc, axis=-1)
        D = jnp.where(tril, b[..., :, None] - b[..., None, :] + ic[..., None, :], -jnp.inf)
        m_inter = b + m[..., None]
        m_t = jnp.maximum(m_inter, jnp.max(D, axis=-1))
        w_inter = jnp.exp(m_inter - m_t)
        S = jnp.einsum('bhtd,bhsd->bhts', qc, kc) * jnp.exp(D - m_t[..., None])
        num = w_inter[..., None] * jnp.einsum('bhtd,bhde->bhte', qc, C) + jnp.einsum('bhts,bhse->bhte', S, vc)
        den = w_inter * jnp.einsum('bhtd,bhd->bht', qc, n) + jnp.sum(S, axis=-1)
        h = num / jnp.maximum(jnp.abs(den), jnp.exp(-m_t))[..., None]
        bL = b[..., -1]
        g = bL[..., None] - b + ic
        m_new = jnp.maximum(bL + m, jnp.max(g, axis=-1))
        a = jnp.exp(bL + m - m_new)
        w = jnp.exp(g - m_new[..., None])
        C_new = a[..., None, None] * C + jnp.einsum('bhs,bhsd,bhse->bhde', w, kc, vc)
        n_new = a[..., None] * n + jnp.einsum('bhs,bhsd->bhd', w, kc)
        return (C_new, n_new, m_new), h

    state, h = lax.scan(body, state, tuple(to_chunks(a, L) for a in (q, k, v, logi, logf)))
    return from_chunks(h), state


def mlstm_mixer(p_lat, p_ctx, gate_b, out_norm, need_ctx):
    H, dh = MLSTM_HEADS, MLSTM_HEAD_DIM

    def prep(p):
        b, t = p["m_q"].shape[:2]
        heads = lambda a: a.reshape(b, t, H, dh).transpose(0, 2, 1, 3).astype(f32)
        q, k, v = heads(p["m_q"]), heads(p["m_k"]) * dh ** -0.5, heads(p["m_v"])
        g = (p["m_gates"] + gate_b).astype(f32).reshape(b, t, 4, H).transpose(2, 0, 3, 1)
        return (q, k, v, g[0], jax.nn.log_sigmoid(g[1])), (q, k, v, g[2], jax.nn.log_sigmoid(g[3]))

    cf, cb = prep(p_ctx)
    lf, lb = prep(p_lat)
    b = p_lat["m_q"].shape[0]
    init = (jnp.zeros((b, H, dh, dh), f32), jnp.zeros((b, H, dh), f32), jnp.zeros((b, H), f32))
    h_c, h_l = bidirectional(mlstm_scan, cf, lf, cb, lb, init)

    def finish(h, p):
        bb, hh, t, d = h.shape
        hn = rms_norm(h.transpose(0, 2, 1, 3), out_norm.reshape(hh, d)).reshape(bb, t, hh * d)
        return (jax.nn.sigmoid(p["m_o"].astype(f32)) * hn).astype(p["m_o"].dtype)

    return finish(h_l, p_lat), (finish(h_c, p_ctx) if need_ctx else None)


def block_attention(q, k, v):
    b, t, h, dq = q.shape
    nb = t // ATTN_BLK
    scale = dq ** -0.5
    qb = q.reshape(b, nb, ATTN_BLK, h, dq).transpose(1, 0, 2, 3, 4)

    def one(qblk):
        s = jnp.einsum('bqhd,bkhd->bhqk', qblk, k).astype(f32) * scale
        p = jax.nn.softmax(s, axis=-1).astype(v.dtype)
        return jnp.einsum('bhqk,bkhd->bqhd', p, v)

    o = lax.map(one, qb)
    return o.transpose(1, 0, 2, 3, 4).reshape(b, t, h * v.shape[-1])


def mla_mixer(p_lat, p_ctx, q_norm, kv_norm, wq_b, wkv_b, q_gain, k_gain, rope, need_ctx):
    H = MLA_HEADS

    def proj(p, rope):
        b, t = p["a_q"].shape[:2]
        qh = (rms_norm(p["a_q"], q_norm) @ wq_b).reshape(b, t, H, MLA_NOPE + MLA_ROPE_DIM)
        q_nope = rms_norm(qh[..., :MLA_NOPE], q_gain[:MLA_NOPE])
        q_rope = rms_norm(qh[..., MLA_NOPE:], q_gain[MLA_NOPE:])
        kv = (rms_norm(p["a_kv"], kv_norm) @ wkv_b).reshape(b, t, H, MLA_NOPE + MLA_V)
        k_nope = rms_norm(kv[..., :MLA_NOPE], k_gain[:MLA_NOPE])
        v = kv[..., MLA_NOPE:]
        k_rope = rms_norm(p["a_kr"], k_gain[MLA_NOPE:])[:, :, None, :]
        if rope is not None:
            q_rope = apply_rope(q_rope, *rope)
            k_rope = apply_rope(k_rope, *rope)
        q = jnp.concatenate([q_nope, q_rope], axis=-1)
        k = jnp.concatenate([k_nope, jnp.broadcast_to(k_rope, (b, t, H, MLA_ROPE_DIM))], axis=-1)
        return q, k, v

    ql, kl, vl = proj(p_lat, rope)
    qc, kc, vc = proj(p_ctx, None)
    out_l = block_attention(ql, jnp.concatenate([kc, kl], axis=1), jnp.concatenate([vc, vl], axis=1))
    out_c = block_attention(qc, kc, vc) if need_ctx else None
    return out_l, out_c


def swa_mixer(p_lat, p_ctx, q_gain, k_gain, sink, rope, need_ctx):
    H, KV, dh = SWA_HEADS, SWA_KV_HEADS, SWA_HEAD_DIM
    G = H // KV
    scale = dh ** -0.5

    def proj(p, rope):
        b, t = p["w_q"].shape[:2]
        q = rms_norm(p["w_q"].reshape(b, t, H, dh), q_gain)
        k = rms_norm(p["w_k"].reshape(b, t, KV, dh), k_gain)
        v = p["w_v"].reshape(b, t, KV, dh)
        if rope is not None:
            q, k = apply_rope(q, *rope), apply_rope(k, *rope)
        return q, k, v

    ql, kl, vl = proj(p_lat, rope)
    qc, kc, vc = proj(p_ctx, None)
    b, t = ql.shape[:2]
    nb = t // SWA_BLK
    qb = ql.reshape(b, nb, SWA_BLK, KV, G, dh)

    def band(a):
        ap = jnp.pad(a, ((0, 0), (SWA_BLK, SWA_BLK), (0, 0), (0, 0))).reshape(b, nb + 2, SWA_BLK, KV, dh)
        return jnp.concatenate([ap[:, :-2], ap[:, 1:-1], ap[:, 2:]], axis=2)

    kb, vb = band(kl), band(vl)
    s_loc = jnp.einsum('bnqkgd,bnskd->bkgnqs', qb, kb).astype(f32) * scale
    blk = jnp.arange(nb)[:, None, None]
    qpos = blk * SWA_BLK + jnp.arange(SWA_BLK)[None, :, None]
    kpos = (blk - 1) * SWA_BLK + jnp.arange(3 * SWA_BLK)[None, None, :]
    valid = (kpos >= 0) & (kpos < t) & (jnp.abs(qpos - kpos) <= SWA_WINDOW)
    s_loc = jnp.where(valid, s_loc, -1e30)
    s_ctx = jnp.einsum('bnqkgd,bckd->bkgnqc', qb, kc).astype(f32) * scale
    s_sink = jnp.broadcast_to(sink.reshape(1, KV, G, 1, 1, 1).astype(f32), s_loc.shape[:-1] + (1,))
    p = jax.nn.softmax(jnp.concatenate([s_loc, s_ctx, s_sink], axis=-1), axis=-1).astype(vl.dtype)
    nl, ncx = 3 * SWA_BLK, kc.shape[1]
    o = (jnp.einsum('bkgnqs,bnskd->bnqkgd', p[..., :nl], vb)
         + jnp.einsum('bkgnqc,bckd->bnqkgd', p[..., nl:nl + ncx], vc))
    out_l = o.reshape(b, t, H * dh)
    out_c = None
    if need_ctx:
        cl = qc.shape[1]
        qg = qc.reshape(b, cl, KV, G, dh)
        s = jnp.einsum('bqkgd,bckd->bkgqc', qg, kc).astype(f32) * scale
        ss = jnp.broadcast_to(sink.reshape(1, KV, G, 1, 1).astype(f32), s.shape[:-1] + (1,))
        pc = jax.nn.softmax(jnp.concatenate([s, ss], axis=-1), axis=-1)[..., :-1].astype(vc.dtype)
        out_c = jnp.einsum('bkgqc,bckd->bqkgd', pc, vc).reshape(b, cl, H * dh)
    return out_l, out_c


def depthwise_conv(x, w, bias):
    ch = x.shape[-1]
    y = lax.conv_general_dilated(x, w[:, None, :].astype(x.dtype), window_strides=(1,),
                                 padding=[(SSD_CONV // 2, SSD_CONV // 2)],
                                 dimension_numbers=("NWC", "WIO", "NWC"), feature_group_count=ch)
    return y + bias


def ssd_scan(x, dt, a, Bm, Cm, state):
    L = SSD_CHUNK
    tril = jnp.tril(jnp.ones((L, L), bool))

    def body(S, inp):
        xc, dtc, ac, Bc, Cc = inp
        cum = jnp.cumsum(ac, axis=-1)
        seg = jnp.exp(jnp.where(tril, cum[..., :, None] - cum[..., None, :], -jnp.inf))
        w = jnp.einsum('bhtn,bhsn->bhts', Cc, Bc) * seg * dtc[..., None, :]
        y = jnp.einsum('bhts,bhsp->bhtp', w, xc) + jnp.exp(cum)[..., None] * jnp.einsum('bhtn,bhpn->bhtp', Cc, S)
        dec = jnp.exp(cum[..., -1:] - cum) * dtc
        S_new = jnp.exp(cum[..., -1])[..., None, None] * S + jnp.einsum('bhs,bhsp,bhsn->bhpn', dec, xc, Bc)
        return S_new, y

    state, y = lax.scan(body, state, tuple(to_chunks(t_, L) for t_ in (x, dt, a, Bm, Cm)))
    return from_chunks(y), state


def ssd_mixer(p_lat, p_ctx, conv_w, conv_b, dt_bias, a_log, d_skip, norm_g, need_ctx):
    H, P, N, G = SSD_HEADS, SSD_HEAD_DIM, SSD_STATE, SSD_GROUPS
    A = -jnp.exp(a_log.astype(f32)).reshape(2, H)

    def prep(p):
        b, t = p["s_xbc"].shape[:2]
        xbc = jax.nn.silu(depthwise_conv(p["s_xbc"], conv_w, conv_b)).astype(f32)
        xs, Bm, Cm = jnp.split(xbc, [GROUP_W, GROUP_W + G * N], axis=-1)
        xh = xs.reshape(b, t, H, P)
        rep = lambda m: jnp.repeat(m.reshape(b, t, G, N), H // G, axis=2).transpose(0, 2, 1, 3)
        Bh, Ch = rep(Bm), rep(Cm)
        dt = jax.nn.softplus((p["s_dt"] + dt_bias).astype(f32)).reshape(b, t, 2, H).transpose(2, 0, 3, 1)
        xt = xh.transpose(0, 2, 1, 3)
        fwd = (xt, dt[0], dt[0] * A[0][None, :, None], Bh, Ch)
        bwd = (xt, dt[1], dt[1] * A[1][None, :, None], Bh, Ch)
        return fwd, bwd, xh

    cf, cb, xc = prep(p_ctx)
    lf, lb, xl = prep(p_lat)
    b = xl.shape[0]
    init = jnp.zeros((b, H, P, N), f32)
    y_c, y_l = bidirectional(ssd_scan, cf, lf, cb, lb, init)

    def finish(y, xh, p):
        bb, t = xh.shape[:2]
        yy = (y.transpose(0, 2, 1, 3) + d_skip[:, None] * xh).reshape(bb, t, GROUP_W)
        g = (yy * jax.nn.silu(p["s_z"].astype(f32))).reshape(bb, t, GROUP_W // SSD_NORM_GROUP, SSD_NORM_GROUP)
        out = rms_norm(g, norm_g.reshape(GROUP_W // SSD_NORM_GROUP, SSD_NORM_GROUP)).reshape(bb, t, GROUP_W)
        return out.astype(p["s_z"].dtype)

    return finish(y_l, xl, p_lat), (finish(y_c, xc, p_ctx) if need_ctx else None)


def setup_inputs(seed: int = 0) -> dict:
    key = jax.random.key(seed)
    ks = iter(jax.random.split(key, 64))
    L = DEPTH

    def nrm(shape, scale=1.0):
        return jax.random.normal(next(ks), shape, f32) * scale

    def gain(shape):
        return 1.0 + nrm(shape, 0.05)

    fb = jnp.linspace(3.0, 6.0, MLSTM_HEADS, dtype=f32)
    mlstm_gate_b = jnp.concatenate([nrm((L, MLSTM_HEADS), 0.1), fb + nrm((L, MLSTM_HEADS), 0.1),
                                    nrm((L, MLSTM_HEADS), 0.1), fb + nrm((L, MLSTM_HEADS), 0.1)], axis=-1)
    dt0 = jnp.exp(jax.random.uniform(next(ks), (L, 2 * SSD_HEADS), f32, math.log(1e-3), math.log(1e-1)))
    ssd_dt_bias = dt0 + jnp.log(-jnp.expm1(-dt0))
    ssd_a_log = jnp.log(jax.random.uniform(next(ks), (L, 2 * SSD_HEADS), f32, 1.0, 16.0))
    qk_mla = MLA_NOPE + MLA_ROPE_DIM
    return {
        "x": nrm((BATCH, SEQ, D_MODEL)),
        "c": nrm((BATCH, D_MODEL)),
        "ctx": nrm((BATCH, CTX_LEN, D_MODEL)),
        "c_ctx": nrm((D_MODEL,)),
        "w_mod": nrm((L, D_MODEL, N_MOD * D_MODEL), 0.02),
        "b_mod": nrm((L, N_MOD * D_MODEL), 0.02),
        "ffn1_norm": gain((L, D_MODEL)),
        "ffn1_wi": nrm((L, D_MODEL, 2 * D_FF), D_MODEL ** -0.5),
        "ffn1_wo": nrm((L, D_FF, D_MODEL), D_FF ** -0.5),
        "mix_norm": gain((L, D_MODEL)),
        "w_in": nrm((L, D_MODEL, IN_COLS), D_MODEL ** -0.5),
        "w_out": nrm((L, MIX_W, D_MODEL), MIX_W ** -0.5),
        "mlstm_gate_b": mlstm_gate_b,
        "mlstm_out_norm": gain((L, GROUP_W)),
        "mla_q_norm": gain((L, MLA_Q_LORA)),
        "mla_kv_norm": gain((L, MLA_KV_LORA)),
        "mla_wq_b": nrm((L, MLA_Q_LORA, MLA_HEADS * qk_mla), MLA_Q_LORA ** -0.5),
        "mla_wkv_b": nrm((L, MLA_KV_LORA, MLA_HEADS * (MLA_NOPE + MLA_V)), MLA_KV_LORA ** -0.5),
        "mla_q_gain": gain((L, qk_mla)),
        "mla_k_gain": gain((L, qk_mla)),
        "swa_q_gain": gain((L, SWA_HEAD_DIM)),
        "swa_k_gain": gain((L, SWA_HEAD_DIM)),
        "swa_sink": nrm((L, SWA_HEADS), 0.5),
        "ssd_conv_w": nrm((L, SSD_CONV, SSD_XBC), SSD_CONV ** -0.5),
        "ssd_conv_b": nrm((L, SSD_XBC), 0.02),
        "ssd_dt_bias": ssd_dt_bias,
        "ssd_a_log": ssd_a_log,
        "ssd_d": 1.0 + nrm((L, SSD_HEADS), 0.1),
        "ssd_norm": gain((L, GROUP_W)),
        "ffn2_norm": gain((L, D_MODEL)),
        "ffn2_wi": nrm((L, D_MODEL, 2 * D_FF), D_MODEL ** -0.5),
        "ffn2_wo": nrm((L, D_FF, D_MODEL), D_FF ** -0.5),
    }


def reference(x, c, ctx, c_ctx, w_mod, b_mod, ffn1_norm, ffn1_wi, ffn1_wo, mix_norm, w_in, w_out,
              mlstm_gate_b, mlstm_out_norm, mla_q_norm, mla_kv_norm, mla_wq_b, mla_wkv_b, mla_q_gain,
              mla_k_gain, swa_q_gain, swa_k_gain, swa_sink, ssd_conv_w, ssd_conv_b, ssd_dt_bias, ssd_a_log,
              ssd_d, ssd_norm, ffn2_norm, ffn2_wi, ffn2_wo):
    b, t, d = x.shape
    rows = t // GRID_W
    rope_mla = axial_rope(rows, MLA_ROPE_DIM)
    rope_swa = axial_rope(rows, SWA_HEAD_DIM)
    sc, scc = jax.nn.silu(c), jax.nn.silu(c_ctx)
    h, hc = x, ctx
    for l in range(DEPTH):
        need_ctx = l < DEPTH - 1
        mod = (sc @ w_mod[l] + b_mod[l]).reshape(b, N_MOD, 1, d)
        modc = (scc @ w_mod[l] + b_mod[l]).reshape(1, N_MOD, 1, d)
        h = h + 0.5 * mod[:, 2] * swiglu(modulate(rms_norm(h, ffn1_norm[l]), mod[:, 0], mod[:, 1]),
                                         ffn1_wi[l], ffn1_wo[l])
        hc = hc + 0.5 * modc[:, 2] * swiglu(modulate(rms_norm(hc, ffn1_norm[l]), modc[:, 0], modc[:, 1]),
                                            ffn1_wi[l], ffn1_wo[l])
        u = split_cols(modulate(rms_norm(h, mix_norm[l]), mod[:, 3], mod[:, 4]) @ w_in[l])
        uc = split_cols(modulate(rms_norm(hc, mix_norm[l]), modc[:, 3], modc[:, 4]) @ w_in[l])
        a_l, a_c = mlstm_mixer(u, uc, mlstm_gate_b[l], mlstm_out_norm[l], need_ctx)
        m_l, m_c = mla_mixer(u, uc, mla_q_norm[l], mla_kv_norm[l], mla_wq_b[l], mla_wkv_b[l],
                             mla_q_gain[l], mla_k_gain[l], rope_mla, need_ctx)
        w_l, w_c = swa_mixer(u, uc, swa_q_gain[l], swa_k_gain[l], swa_sink[l], rope_swa, need_ctx)
        s_l, s_c = ssd_mixer(u, uc, ssd_conv_w[l], ssd_conv_b[l], ssd_dt_bias[l], ssd_a_log[l],
                             ssd_d[l], ssd_norm[l], need_ctx)
        h = h + mod[:, 5] * (jnp.concatenate([a_l, m_l, w_l, s_l], axis=-1) @ w_out[l])
        if need_ctx:
            hc = hc + modc[:, 5] * (jnp.concatenate([a_c, m_c, w_c, s_c], axis=-1) @ w_out[l])
            hc = hc + 0.5 * modc[:, 8] * swiglu(modulate(rms_norm(hc, ffn2_norm[l]), modc[:, 6], modc[:, 7]),
                                                ffn2_wi[l], ffn2_wo[l])
        h = h + 0.5 * mod[:, 8] * swiglu(modulate(rms_norm(h, ffn2_norm[l]), mod[:, 6], mod[:, 7]),
                                         ffn2_wi[l], ffn2_wo[l])
    return h
```

```python
import math
import numpy as np
import concourse.bass as bass
import concourse.mybir as mybir
from concourse.bass_utils import run_bass_kernel_spmd

F32 = mybir.dt.float32
BF16 = mybir.dt.bfloat16
AF = mybir.ActivationFunctionType
ALU = mybir.AluOpType
AX = mybir.AxisListType

D = 1024; KC = 8; T = 2304; TCX = 256; TL = 2048; NT = 18; FF = 2816; FC = 22
ST = [(0, 256), (256, 512), (768, 512), (1280, 512), (1792, 512)]
EPS = 1e-6
NEG = -30000.0
SEM_LIMIT = 30000
NPP = 246
NREP = 872
R_GATEB, R_ONORM, R_QG, R_KG, R_SQG, R_SKG, R_SINK, R_DTB, R_ALOG, R_SSDD, R_SNORM = 0, 16, 272, 368, 464, 528, 592, 596, 604, 612, 616
P_BMOD, P_NORM, P_QN, P_KVN, P_CW, P_CB = 0, 144, 192, 196, 198, 238
SEG_A, SEG_M, SEG_W, SEG_S = 0, 1040, 1456, 1968


class _Op:
    __slots__ = ("eng", "fn", "deps", "need_sig", "sig", "is_dma", "semkey", "idx", "vc")

    def __init__(self, eng, fn, is_dma=False, semkey=None):
        self.eng = eng; self.fn = fn; self.deps = []; self.need_sig = False; self.sig = None
        self.is_dma = is_dma; self.semkey = semkey


class Prog:
    ENGS = ("pe", "act", "dve", "pool", "sp")

    def __init__(self, nc):
        self.nc = nc
        self.ops = []
        self.state = {}
        self.final_dmas = []
        self.last_op = {}
        self.dmas_open = []
        self.pending_barrier = {}
        self.capture = None

    def _eng(self, name):
        nc = self.nc
        return {"pe": nc.tensor, "act": nc.scalar, "dve": nc.vector, "pool": nc.gpsimd, "sp": nc.sync}[name]

    @staticmethod
    def _norm(k):
        if not isinstance(k, tuple):
            return (k, None)
        if len(k) == 2:
            return k
        return (k[0], tuple(k[1:]))

    def _cells(self, key, create):
        name, sub = key
        st = self.state.setdefault(name, {})
        if sub is None:
            if create and None not in st:
                st[None] = [None, []]
            return list(st.values())
        cells = []
        if None in st:
            cells.append(st[None])
        if sub not in st and create:
            st[sub] = [None, []]
        if sub in st:
            cells.append(st[sub])
        return cells

    def _add(self, op, rd, wr):
        rd = [self._norm(k) for k in rd]
        wr = [self._norm(k) for k in wr]
        deps = {}

        def conflict(o):
            return o.is_dma or op.is_dma or o.eng != op.eng or op.eng != "pe"

        for k in rd:
            for cell in self._cells(k, True):
                w = cell[0]
                if w is not None and w is not op:
                    deps[id(w)] = w
                if k[0] == "ps":
                    for r in cell[1]:
                        if r is not op and r.eng != op.eng:
                            deps[id(r)] = r
        for k in wr:
            for cell in self._cells(k, True):
                w = cell[0]
                if w is not None and w is not op and conflict(w):
                    deps[id(w)] = w
                for r in cell[1]:
                    if r is not op and conflict(r):
                        deps[id(r)] = r
        if op.eng in self.pending_barrier:
            for o in self.pending_barrier.pop(op.eng):
                deps[id(o)] = o
        op.deps = list(deps.values())
        for d_ in op.deps:
            d_.need_sig = True
        for k in wr:
            name, sub = k
            st = self.state[name]
            if sub is None:
                for s_ in list(st.keys()):
                    if s_ is not None:
                        del st[s_]
                st[None] = [op, []]
            else:
                st[sub] = [op, []]
        for k in rd:
            if k in wr:
                continue
            name, sub = k
            st = self.state[name]
            if sub is None:
                st[None][1].append(op)
                for s_, cell in st.items():
                    if s_ is not None:
                        cell[1].append(op)
            else:
                st[sub][1].append(op)
        op.idx = len(self.ops)
        self.ops.append(op)
        if op.is_dma:
            self.dmas_open.append(op)
        else:
            self.last_op[op.eng] = op
        return op

    def op(self, eng, fn, rd=(), wr=()):
        if self.capture is not None:
            self.capture.append(("op", eng, fn, tuple(rd), tuple(wr)))
            return None
        return self._add(_Op(eng, fn), tuple(rd), tuple(wr))

    def interleave(self, bodies):
        lists = []
        for body in bodies:
            self.capture = []
            body()
            lists.append(self.capture)
            self.capture = None
        idx = [0] * len(lists)
        while any(idx[i] < len(lists[i]) for i in range(len(lists))):
            for i, lst in enumerate(lists):
                if idx[i] < len(lst):
                    it = lst[idx[i]]; idx[i] += 1
                    if it[0] == "op":
                        self.op(*it[1:])
                    else:
                        self.dma(*it[1:])

    def dma(self, eng, fn, rd=(), wr=(), semkey=None, final=False):
        if self.capture is not None:
            self.capture.append(("dma", eng, fn, tuple(rd), tuple(wr), semkey, final))
            return None
        o = _Op(eng, fn, is_dma=True, semkey=semkey)
        o.need_sig = True
        self._add(o, tuple(rd), tuple(wr))
        if final:
            self.final_dmas.append(o)
        return o

    def barrier(self):
        lasts = list(self.last_op.values()) + list(self.dmas_open)
        self.dmas_open = []
        for e in self.ENGS:
            self.pending_barrier[e] = list(self.pending_barrier.get(e, [])) + lasts
        self.state = {}

    def emit(self):
        nc = self.nc
        sem_count = [0]

        def new_sem(nm):
            sem_count[0] += 1
            return nc.alloc_semaphore(f"s_{nm}_{sem_count[0]}")

        eng_sem = {}; eng_cnt = {}; dma_sem = {}; dma_cnt = {}
        for o in self.ops:
            if o.is_dma:
                k = o.semkey
                if k not in dma_sem or dma_cnt[k] + 16 > SEM_LIMIT:
                    dma_sem[k] = new_sem("d"); dma_cnt[k] = 0
                dma_cnt[k] += 16
                o.sig = (dma_sem[k], dma_cnt[k])
            elif o.need_sig:
                e = o.eng
                if e not in eng_sem or eng_cnt[e] + 1 > SEM_LIMIT:
                    eng_sem[e] = new_sem(e); eng_cnt[e] = 0
                eng_cnt[e] += 1
                o.sig = (eng_sem[e], eng_cnt[e])
        waited = {e: {} for e in self.ENGS}
        nwait = 0
        sem_by_id = {}
        for o in self.ops:
            eng = self._eng(o.eng)
            wt = waited[o.eng]
            for d_ in sorted(o.deps, key=lambda x: -x.idx):
                sem, val = d_.sig
                key = id(sem)
                if wt.get(key, 0) >= val:
                    continue
                eng.wait_ge(sem, val)
                nwait += 1
                wt[key] = val
                for k2, v2 in d_.vc.items():
                    if wt.get(k2, 0) < v2:
                        wt[k2] = v2
            ins = o.fn(eng)
            if o.sig is not None:
                ins.then_inc(o.sig[0], 16 if o.is_dma else 1)
            if o.need_sig:
                vc = dict(wt)
                if o.is_dma:
                    pass
                else:
                    vc[id(o.sig[0])] = max(vc.get(id(o.sig[0]), 0), o.sig[1])
                o.vc = vc
        for o in self.final_dmas:
            sem, val = o.sig
            if waited["sp"].get(id(sem), 0) < val:
                waited["sp"][id(sem)] = val
                nc.sync.wait_ge(sem, val)
        self.stats = dict(n_ops=len(self.ops), n_wait=nwait, n_sem=sem_count[0], eng_cnt=dict(eng_cnt), dma_max=max(dma_cnt.values()) if dma_cnt else 0)
        return self.stats


def _const_tables():
    r = np.arange(128)[:, None]; c = np.arange(128)[None, :]
    cst = np.zeros((128, 8, 128), np.float32)
    cst[:, 0] = (r == c)
    cst[:, 1] = 1.0
    cst[:, 2] = (r <= c)
    cst[:, 3] = (r >= c)
    cst[:, 4] = (r > c)
    cst[:, 5] = (r < c)
    cst[:, 6] = NEG * (r > c)
    cst[:, 7] = NEG * (r < c)

    def rope(rot):
        t = np.arange(TL)
        row = (t // 64).astype(np.float32); col = (t % 64).astype(np.float32)
        nf = rot // 4
        inv = (10000.0 ** (-np.arange(nf, dtype=np.float32) / nf)).astype(np.float32)
        ar = row[:, None] * inv; ac = col[:, None] * inv
        ang = np.concatenate([ar, ar, ac, ac], -1).astype(np.float32)
        cos = np.cos(ang).astype(np.float32); sin = np.sin(ang).astype(np.float32)
        sgn = np.concatenate([-np.ones(nf), np.ones(nf), -np.ones(nf), np.ones(nf)]).astype(np.float32)
        tab = np.stack([cos, sin * sgn], 1)
        return np.ascontiguousarray(tab.reshape(16, 128, 2, rot).transpose(1, 0, 2, 3))
    return cst, rope(32), rope(64)


def _fm(v):
    v = np.asarray(v, np.float32)
    n = v.shape[-1] // 128
    return np.moveaxis(v.reshape(v.shape[:-1] + (n, 128)), -1, 0)


def _pack_pp(inp):
    pp = np.zeros((128, NPP), np.float32)
    pp[:, P_BMOD:P_BMOD + 144] = _fm(inp["b_mod"].reshape(2, 9, 1024)).reshape(128, 144)
    norms = np.stack([inp["ffn1_norm"], inp["mix_norm"], inp["ffn2_norm"]], 0)
    pp[:, P_NORM:P_NORM + 48] = _fm(norms).reshape(128, 48)
    pp[:, P_QN:P_QN + 4] = _fm(inp["mla_q_norm"]).reshape(128, 4)
    pp[:, P_KVN:P_KVN + 2] = _fm(inp["mla_kv_norm"]).reshape(128, 2)
    cw = np.asarray(inp["ssd_conv_w"], np.float32)
    pp[:, P_CW:P_CW + 40] = np.moveaxis(_fm(cw), -2, -1).reshape(128, 40)
    pp[:, P_CB:P_CB + 8] = _fm(inp["ssd_conv_b"]).reshape(128, 8)
    return pp


def _pack_rep(inp):
    rows = []
    for l in range(2):
        rows.append(np.concatenate([np.asarray(inp[k], np.float32)[l].reshape(-1) for k in (
            "mlstm_gate_b", "mlstm_out_norm", "mla_q_gain", "mla_k_gain", "swa_q_gain", "swa_k_gain", "swa_sink",
            "ssd_dt_bias", "ssd_a_log", "ssd_d", "ssd_norm")]))
    rep = np.stack(rows, 0)
    assert rep.shape == (2, NREP)
    return np.ascontiguousarray(np.broadcast_to(rep[None], (128, 2, NREP))).astype(np.float32)


class Builder:
    def __init__(self, cfg):
        self.cfg = cfg
        self.n_seq = cfg.get("n_seq", 2)
        self.n_layers = cfg.get("n_layers", 2)
        self.dbg = cfg.get("dbg", False)
        nc = self.nc = bass.Bass("TRN2", target_bir_lowering=False)
        self.P = Prog(nc)
        self.uid = 0
        ns = self.n_seq

        def din(name, shape):
            return nc.dram_tensor(name, list(shape), F32, kind="ExternalInput").ap()
        self.x = din("x", [ns, TL, D]); self.ctx = din("ctx", [ns, TCX, D])
        self.cT = din("cT", [128, 8, 3])
        self.w_mod = din("w_mod", [2, D, 9 * D])
        self.wi = [din("ffn1_wi", [2, D, 2 * FF]), din("ffn2_wi", [2, D, 2 * FF])]
        self.wo = [din("ffn1_wo", [2, FF, D]), din("ffn2_wo", [2, FF, D])]
        self.w_in = din("w_in", [2, D, 2744]); self.w_out = din("w_out", [2, D, D])
        self.wq_b = din("mla_wq_b", [2, 256, 384]); self.wkv_b = din("mla_wkv_b", [2, 128, 512])
        self.pp_d = din("pp", [128, NPP]); self.rep_d = din("rep", [128, 2, NREP])
        self.cst_d = din("cst", [128, 8, 128]); self.ropeM_d = din("ropeM", [128, 16, 2, 32]); self.ropeS_d = din("ropeS", [128, 16, 2, 64])
        self.out = nc.dram_tensor("out", [ns, TL, D], F32, kind="ExternalOutput").ap()
        if self.dbg:
            self.dbg_out = nc.dram_tensor("dbg", [ns, 4, T, 256], BF16, kind="ExternalOutput").ap()
        self.hT = nc.alloc_sbuf_tensor("sb_hT", [128, KC, T], F32)
        self.cst = nc.alloc_sbuf_tensor("sb_cst", [128, 8, 128], F32)
        self.cstb = nc.alloc_sbuf_tensor("sb_cstb", [128, 8, 128], BF16)
        self.pp = nc.alloc_sbuf_tensor("sb_pp", [128, NPP], F32)
        self.rep = nc.alloc_sbuf_tensor("sb_rep", [128, NREP], F32)
        self.modT = nc.alloc_sbuf_tensor("sb_modT", [128, 2, 9, KC, 3], F32)
        self.DER = nc.alloc_sbuf_tensor("sb_DER", [128, 2, 3, 3, KC, 3], F32)
        self.eps_t = nc.alloc_sbuf_tensor("sb_eps_t", [128, 4], F32)
        rem = nc.sbuf_bytes_remaining
        self.arena_size = (rem - 64) // 32 * 32
        self.arena_t = nc.alloc_sbuf_tensor("arena", [128, self.arena_size // 4], F32)
        self.arena_base = nc.lookup_mloc(self.arena_t).addr
        self.acur = 0
        self.PS = [nc.alloc_psum_tensor(f"psb{i}", [128, 512], F32) for i in range(8)]
        self.PSB = [p.bitcast(BF16) for p in self.PS]
        self.rotc = {}
        self.last_rg = {}
        c = self.cst
        self.I_f, self.ONES_f, self.TRI_F, self.TRI_B, self.U_F, self.U_B, self.NEG_F, self.NEG_B = [c[:, i, :] for i in range(8)]
        cb = self.cstb
        self.I_b, self.ONES_b, self.TRI_Fb, self.TRI_Bb, self.U_Fb, self.U_Bb, self.NEG_Fb, self.NEG_Bb = [cb[:, i, :] for i in range(8)]

    def A(self, name, shape, dtype):
        sz = (2 if dtype == BF16 else 4) * int(np.prod(shape[1:]))
        off = (self.acur + 31) // 32 * 32
        assert off + sz <= self.arena_size, f"arena overflow {name}: {off + sz} > {self.arena_size}"
        self.acur = off + sz
        self.uid += 1
        return self.nc.alloc_sbuf_tensor_at(f"{name}_{self.uid}", list(shape), dtype, offset=self.arena_base + off)

    def mark(self):
        return self.acur

    def release(self, m):
        self.P.barrier()
        self.acur = m

    def rot(self, name, banks):
        i = self.rotc.get(name, 0)
        self.rotc[name] = i + 1
        return banks[i % len(banks)]

    def mm(self, out, lhsT, rhs, start=True, stop=True, rd=(), wr=()):
        rd = list(rd); wr = list(wr)
        bank = [k[1] for k in wr if isinstance(k, tuple) and k[0] == "ps"][0]
        rg = (lhsT.base_partition(), lhsT.partition_size())
        prev = self.last_rg.get(bank)
        if prev is not None and (prev[0] + prev[1] <= rg[0] or rg[0] + rg[1] <= prev[0]):
            rd.append(("pe_rg", bank))
        wr.append(("pe_rg", bank))
        self.last_rg[bank] = rg
        return self.P.op("pe", lambda e: e.matmul(out, lhsT=lhsT, rhs=rhs, start=start, stop=stop), rd, wr)

    def tr(self, out, in_, ident, rd=(), wr=()):
        return self.P.op("pe", lambda e: e.transpose(out, in_, ident), rd, wr)

    def act(self, out, in_, func, rd=(), wr=(), bias=None, scale=None):
        kw = {}
        if bias is not None:
            kw["bias"] = bias
        if scale is not None:
            kw["scale"] = scale
        return self.P.op("act", lambda e: e.activation(out=out, in_=in_, func=func, **kw), rd, wr)

    def v(self, eng, meth, rd, wr, **kw):
        return self.P.op(eng, lambda e: getattr(e, meth)(**kw), rd, wr)

    def dma(self, eng, out, in_, rd, wr, semkey, final=False):
        return self.P.dma(eng, lambda e: e.dma_start(out=out, in_=in_), rd, wr, semkey, final)

    def ln_exp(self, out, in_, bias_ap, scale, rd, wr, tmpkey, tmp):
        self.act(tmp, in_, AF.Ln, rd=rd, wr=[tmpkey], bias=bias_ap)
        self.act(out, tmp, AF.Exp, rd=[tmpkey], wr=wr, scale=scale)

    def prologue(self):
        P = self.P
        self.dma("sp", self.cst[:], self.cst_d, [], ["cst"], "cst")
        self.dma("pool", self.cstb[:], self.cst_d, [], ["cstb"], "cstb0")
        self.dma("sp", self.pp[:], self.pp_d, [], ["pp"], "pp")
        self.v("dve", "memset", [], ["eps_t"], ap=self.eps_t[:, 0:1], constant=float(D * EPS))
        self.v("dve", "memset", [], ["eps_t"], ap=self.eps_t[:, 1:2], constant=float(EPS))
        self.v("dve", "memset", [], ["eps_t"], ap=self.eps_t[:, 2:3], constant=1.0)
        self.v("dve", "memset", [], ["eps_t"], ap=self.eps_t[:, 3:4], constant=0.0)
        m = self.mark()
        scT = self.A("scT", [128, 8, 3], F32)
        cTs = self.A("cTs", [128, 8, 3], F32)
        self.dma("sp", cTs[:], self.cT, [], ["cTs"], "cTs")
        self.act(scT[:], cTs[:], AF.Silu, rd=["cTs"], wr=["scT"])
        wmb = [self.A(f"wmb{i}", [128, 8, 1024], F32) for i in range(2)]
        for l in range(self.n_layers):
            for i in range(9):
                sl = (l * 9 + i) % 2
                src = self.w_mod[l].rearrange("(kc p) c -> p kc c", p=128)[:, :, i * 1024:(i + 1) * 1024]
                self.dma("sp", wmb[sl][:, 0:4, :], src[:, 0:4, :], [], [(f"wmb{sl}", 0)], ("wmb", sl, 0))
                self.dma("sp", wmb[sl][:, 4:8, :], src[:, 4:8, :], [], [(f"wmb{sl}", 1)], ("wmb", sl, 1))
                pb = self.rot("mod", [0, 1])
                ps = self.PS[pb]
                for kc in range(8):
                    for k2 in range(8):
                        self.mm(ps[:, kc * 3:(kc + 1) * 3], wmb[sl][:, k2, kc * 128:(kc + 1) * 128], scT[:, k2, :],
                                start=(k2 == 0), stop=(k2 == 7), rd=[f"wmb{sl}", "scT"], wr=[("ps", pb)])
                bm = self.pp[:, P_BMOD + (l * 9 + i) * 8: P_BMOD + (l * 9 + i + 1) * 8].unsqueeze(2).broadcast_to([128, 8, 3])
                self.v("dve", "tensor_tensor", [("ps", pb), "pp"], [("modT", l)],
                       out=self.modT[:, l, i, :, :], in0=ps[:, 0:24].rearrange("p (k r) -> p k r", r=3), in1=bm, op=ALU.add)
            for w_, (i_sh, i_sc, i_g, gmul) in enumerate([(0, 1, 2, 0.5), (3, 4, 5, 1.0), (6, 7, 8, 0.5)]):
                nrm = self.pp[:, P_NORM + (w_ * 2 + l) * 8: P_NORM + (w_ * 2 + l + 1) * 8].unsqueeze(2).broadcast_to([128, 8, 3])
                self.v("dve", "scalar_tensor_tensor", [("modT", l), "pp"], [("DER", l)],
                       out=self.DER[:, l, w_, 0, :, :], in0=self.modT[:, l, i_sc, :, :], scalar=1.0, in1=nrm, op0=ALU.add, op1=ALU.mult)
                self.v("dve", "tensor_scalar", [("DER", l)], [("DER", l)],
                       out=self.DER[:, l, w_, 0, :, :], in0=self.DER[:, l, w_, 0, :, :], scalar1=32.0, scalar2=None, op0=ALU.mult)
                self.v("dve", "tensor_copy", [("modT", l)], [("DER", l)], out=self.DER[:, l, w_, 1, :, :], in_=self.modT[:, l, i_sh, :, :])
                self.v("dve", "tensor_scalar", [("modT", l)], [("DER", l)],
                       out=self.DER[:, l, w_, 2, :, :], in0=self.modT[:, l, i_g, :, :], scalar1=float(gmul), scalar2=None, op0=ALU.mult)
        self.release(m)

    def load_seq(self, s):
        m = self.mark()
        xin = [self.A(f"xin{i}", [128, D], F32) for i in range(2)]
        for i in range(NT):
            sl = i % 2
            src = self.ctx[s, i * 128:(i + 1) * 128, :] if i < 2 else self.x[s, (i - 2) * 128:(i - 1) * 128, :]
            self.dma("sp", xin[sl][:], src, [], [("xin", sl)], ("xin", sl))
            for half in range(2):
                pb = self.rot("ld", [0, 1, 2, 3])
                for q in range(4):
                    kc = half * 4 + q
                    self.tr(self.PS[pb][:, q * 128:(q + 1) * 128], xin[sl][:, kc * 128:(kc + 1) * 128], self.I_f,
                            rd=[("xin", sl), "cst"], wr=[("ps", pb)])
                eng = "act" if half == 0 else "dve"
                dst = self.hT[:, half * 4:(half + 1) * 4, i * 128:(i + 1) * 128]
                srcp = self.PS[pb][:, :].rearrange("p (q t) -> p q t", q=4)
                if eng == "act":
                    self.act(dst, srcp, AF.Identity, rd=[("ps", pb)], wr=[("hT", i)])
                else:
                    self.v("dve", "tensor_copy", [("ps", pb)], [("hT", i)], out=dst, in_=srcp)
        self.release(m)

    def store_seq(self, s):
        m = self.mark()
        ob = [self.A(f"ob{i}", [128, D], F32) for i in range(2)]
        for i in range(2, NT):
            sl = i % 2
            for half in range(2):
                pb = self.rot("ld", [0, 1, 2, 3])
                for q in range(4):
                    kc = half * 4 + q
                    self.tr(self.PS[pb][:, q * 128:(q + 1) * 128], self.hT[:, kc, i * 128:(i + 1) * 128], self.I_f,
                            rd=[("hT", i), "cst"], wr=[("ps", pb)])
                dst = ob[sl][:, half * 512:(half + 1) * 512]
                if half == 0:
                    self.act(dst, self.PS[pb][:, :], AF.Identity, rd=[("ps", pb)], wr=[("ob", sl)])
                else:
                    self.v("dve", "tensor_copy", [("ps", pb)], [("ob", sl)], out=dst, in_=self.PS[pb][:, :])
            self.dma("sp", self.out[s, (i - 2) * 128:(i - 1) * 128, :], ob[sl][:], [("ob", sl)], [], ("ob", sl), final=True)
        self.release(m)

    def alloc_norm_tmps(self):
        self.n_sq = self.A("n_sq", [128, KC, 512], BF16)
        self.n_rstd = self.A("n_rstd", [128, 512], F32)
        self.n_ln = self.A("n_ln", [128, 512], F32)
        self.n_tmp = [self.A(f"n_tmp{i}", [128, 512], F32) for i in range(2)]

    def norm_mod(self, l, which, sti, s, xn, xnkey):
        t0, n = ST[sti]
        r = 2 if sti == 0 else s
        tiles = [("hT", t0 // 128 + j) for j in range(n // 128)]
        self.act(self.n_sq[:, :, 0:n], self.hT[:, :, t0:t0 + n], AF.Square, rd=tiles, wr=["n_sq"])
        ps = self.PS[7]
        for kc in range(KC):
            self.mm(ps[:, 0:n], self.ONES_b, self.n_sq[:, kc, 0:n], start=(kc == 0), stop=(kc == KC - 1),
                    rd=["n_sq", "cstb"], wr=[("ps", 7)])
        self.ln_exp(self.n_rstd[:, 0:n], ps[:, 0:n], self.eps_t[:, 0:1], -0.5, [("ps", 7), "eps_t"], ["n_rstd"], "n_ln", self.n_ln[:, 0:n])
        for kc in range(KC):
            tb = self.rot("ntmp", [0, 1])
            self.v("dve", "scalar_tensor_tensor", tiles + ["n_rstd", ("DER", l)], [("n_tmp", tb)],
                   out=self.n_tmp[tb][:, 0:n], in0=self.hT[:, kc, t0:t0 + n], scalar=self.DER[:, l, which, 0, kc, r:r + 1],
                   in1=self.n_rstd[:, 0:n], op0=ALU.mult, op1=ALU.mult)
            self.act(xn[:, kc, 0:n], self.n_tmp[tb][:, 0:n], AF.Identity, rd=[("n_tmp", tb), ("DER", l)], wr=[xnkey],
                     bias=self.DER[:, l, which, 1, kc, r:r + 1])

    def ffn(self, l, which, s, skip_ctx):
        fi = 0 if which == 0 else 1
        m = self.mark()
        self.alloc_norm_tmps()
        xnp = self.A("xnp", [128, KC, 1280], BF16)
        h1 = self.A("h1", [128, FC, 1280], BF16)
        wib = [self.A(f"wib{i}", [128, 2, KC, 128], BF16) for i in range(2)]
        wob = [self.A(f"wob{i}", [128, FC, 128], BF16) for i in range(2)]
        sg = [self.A(f"sg{i}", [128, 512], F32) for i in range(2)]
        wi_v = self.wi[fi][l].rearrange("(kc p) c -> p kc c", p=128)
        wo_v = self.wo[fi][l].rearrange("(jc p) c -> p jc c", p=128)
        passes = [[0, 1, 2], [3, 4]]
        if skip_ctx:
            passes = [[1, 2], [3, 4]]
        for pi, pas in enumerate(passes):
            offs = {}
            o = 0
            for sti in pas:
                offs[sti] = o
                o += ST[sti][1]
            for sti in pas:
                self.norm_mod(l, which, sti, s, xnp[:, :, offs[sti]:offs[sti] + ST[sti][1]], "xnp")
            for j in range(FC):
                sl = self.rot("wib", [0, 1])
                self.dma("pool", wib[sl][:, 0, :, :], wi_v[:, :, j * 128:(j + 1) * 128], [], [(f"wib{sl}", 0)], ("wib", sl, 0))
                self.dma("pool", wib[sl][:, 1, :, :], wi_v[:, :, FF + j * 128:FF + (j + 1) * 128], [], [(f"wib{sl}", 1)], ("wib", sl, 1))
                for sti in pas:
                    n = ST[sti][1]; o = offs[sti]
                    pg = self.rot("ffg", [0, 1]); pu = self.rot("ffu", [2, 3])
                    for gu, pb in ((0, pg), (1, pu)):
                        for kc in range(KC):
                            self.mm(self.PS[pb][:, 0:n], wib[sl][:, gu, kc, :], xnp[:, kc, o:o + n], start=(kc == 0), stop=(kc == KC - 1),
                                    rd=[(f"wib{sl}", gu), ("xnp", sti)], wr=[("ps", pb)])
                    sb = self.rot("sg", [0, 1])
                    self.act(sg[sb][:, 0:n], self.PS[pg][:, 0:n], AF.Silu, rd=[("ps", pg)], wr=[("sg", sb)])
                    self.v("dve", "tensor_tensor", [("sg", sb), ("ps", pu)], [("h1", j)],
                           out=h1[:, j, o:o + n], in0=sg[sb][:, 0:n], in1=self.PS[pu][:, 0:n], op=ALU.mult)
            for mo in range(KC):
                sl = self.rot("wob", [0, 1])
                self.dma("pool", wob[sl][:], wo_v[:, :, mo * 128:(mo + 1) * 128], [], [("wob", sl)], ("wob", sl))
                for sti in pas:
                    t0, n = ST[sti]; o = offs[sti]
                    r = 2 if sti == 0 else s
                    pb = self.rot("ffo", [4, 5])
                    for j in range(FC):
                        self.mm(self.PS[pb][:, 0:n], wob[sl][:, j, :], h1[:, j, o:o + n], start=(j == 0), stop=(j == FC - 1),
                                rd=[("wob", sl), ("h1", j)], wr=[("ps", pb)])
                    tiles = [("hT", t0 // 128 + q) for q in range(n // 128)]
                    self.v("dve", "scalar_tensor_tensor", [("ps", pb), ("DER", l)] + tiles, tiles,
                           out=self.hT[:, mo, t0:t0 + n], in0=self.PS[pb][:, 0:n], scalar=self.DER[:, l, which, 2, mo, r:r + 1],
                           in1=self.hT[:, mo, t0:t0 + n], op0=ALU.mult, op1=ALU.add)
        self.release(m)

    def mixer_phase_begin(self, l, s):
        self.xn2 = self.A("xn2", [128, KC, T], BF16)
        m = self.mark()
        self.alloc_norm_tmps()
        for sti in range(5):
            t0, n = ST[sti]
            self.norm_mod(l, 1, sti, s, self.xn2[:, :, t0:t0 + n], ("xn2", sti))
        self.release(m)

    def mixer_common_alloc(self, l, seg0, ncols, mixer_idx, pad=None, defer_wmix=False):
        self.woutm = self.A("woutm", [128, 2, D], BF16)
        srco = self.w_out[l].rearrange("(c p) n -> p c n", p=128)[:, mixer_idx * 2:(mixer_idx + 1) * 2, :]
        self.dma("pool", self.woutm[:], srco, [], ["woutm"], "woutm")
        if not defer_wmix:
            self.alloc_wmix(l, seg0, ncols, pad)

    def alloc_wmix(self, l, seg0, ncols, pad=None):
        self.wmix = self.A("wmix", [128, KC, pad or ncols], BF16)
        src = self.w_in[l].rearrange("(kc p) c -> p kc c", p=128)[:, :, seg0:seg0 + ncols]
        self.dma("pool", self.wmix[:, 0:4, 0:ncols], src[:, 0:4, :], [], [("wmix", 0)], "wmix0")
        self.dma("pool", self.wmix[:, 4:8, 0:ncols], src[:, 4:8, :], [], [("wmix", 1)], "wmix1")

    def load_gate_w(self, l, col0, name):
        wg = self.A(name, [128, KC, 256], BF16)
        src = self.w_in[l].rearrange("(kc p) c -> p kc c", p=128)[:, :, col0:col0 + 256]
        self.dma("pool", wg[:], src, [], [name], name)
        return wg

    def proj_fm(self, col0, m_rows, t0, n, pb):
        for kc in range(KC):
            self.mm(self.PS[pb][0:m_rows, 0:n], self.wmix[:, kc, col0:col0 + m_rows], self.xn2[:, kc, t0:t0 + n], start=(kc == 0), stop=(kc == KC - 1),
                    rd=["wmix", "xn2"], wr=[("ps", pb)])

    def proj_tm(self, col0, ncol, gi, pb, pcol0=0, w=None, wkey="wmix"):
        w = self.wmix if w is None else w
        for kc in range(KC):
            self.mm(self.PS[pb][:, pcol0:pcol0 + ncol], self.xn2[:, kc, gi * 128:(gi + 1) * 128], w[:, kc, col0:col0 + ncol],
                    start=(kc == 0), stop=(kc == KC - 1), rd=[wkey, "xn2"], wr=[("ps", pb)])

    def out_proj(self, l, s, sti, mixT, mixkey, n):
        t0, _ = ST[sti]
        r = 2 if sti == 0 else s
        tiles = [("hT", t0 // 128 + q) for q in range(n // 128)]
        for mo in range(KC):
            pb = self.rot("op", [6, 7])
            for c in range(2):
                self.mm(self.PS[pb][:, 0:n], self.woutm[:, c, mo * 128:(mo + 1) * 128], mixT[:, c, 0:n], start=(c == 0), stop=(c == 1),
                        rd=["woutm", mixkey], wr=[("ps", pb)])
            self.v("dve", "scalar_tensor_tensor", [("ps", pb), ("DER", l)] + tiles, tiles,
                   out=self.hT[:, mo, t0:t0 + n], in0=self.PS[pb][:, 0:n], scalar=self.DER[:, l, 1, 2, mo, r:r + 1],
                   in1=self.hT[:, mo, t0:t0 + n], op0=ALU.mult, op1=ALU.add)

    def tok_to_mixT(self, otok_ap, otkey, mixT, mixkey, q, s, mixer_idx, gtile, pb=6):
        if self.dbg:
            self.dma("sp", self.dbg_out[s, mixer_idx, gtile * 128:(gtile + 1) * 128, :], otok_ap, [otkey], [], ("dbg", mixer_idx, gtile % 4))
        for c in range(2):
            self.mm(self.PS[pb][:, c * 128:(c + 1) * 128], otok_ap[:, c * 128:(c + 1) * 128], self.I_b, rd=[otkey, "cstb"], wr=[("ps", pb)])
        self.act(mixT[:, :, q * 128:(q + 1) * 128], self.PS[pb][:, 0:256].rearrange("p (c t) -> p c t", c=2), AF.Identity,
                 rd=[("ps", pb)], wr=[(mixkey, q)])

    def linattn_alloc(self, Pv):
        self.la_sets = []
        for si in range(2):
            L = {"Pv": Pv, "id": si}
            L["A"] = self.A(f"laA{si}", [128, 4, 2, 128], BF16)
            L["E"] = self.A(f"laE{si}", [128, 4, 128], F32)
            L["G"] = self.A(f"laG{si}", [128, 4, 128], F32)
            L["W"] = [self.A(f"laW{si}_{i}", [128, 128], BF16) for i in range(4)]
            L["tc"] = self.A(f"latc{si}", [128, 4, Pv], F32)
            L["ydir"] = self.A(f"laydir{si}", [128, 4, Pv], F32)
            L["vs"] = self.A(f"lavs{si}", [128, 4, 80], BF16)
            L["S"] = self.A(f"laS{si}", [64, 4, Pv], F32)
            L["Sbf"] = self.A(f"laSbf{si}", [128, 4, 80], BF16)
            self.la_sets.append(L)
        self.la_gat = self.A("la_gat", [128, NT, 2, 8], F32)
        self.la_etot = self.A("la_etot", [128, NT, 2, 4], F32)
        self.la_ecum = self.A("la_ecum", [128, NT, 2, 4], F32)
        self.la_dec = self.A("la_dec", [128, NT, 2, 4], F32)

    def split_gates(self, a_all, akey, pfx):
        n = NT * 8
        sp = {}
        af = a_all[:].rearrange("p c k -> p (c k)")
        r1 = self.A(pfx + "_r1", [128, n], F32); r2 = self.A(pfx + "_r2", [128, n], F32)
        for nm in ("hi", "mid", "lo"):
            sp[nm + "_b"] = self.A(f"{pfx}_{nm}b", [128, NT, 8], BF16)
            if nm != "lo":
                sp[nm + "_f"] = self.A(f"{pfx}_{nm}f", [128, NT, 8], F32)
        fl = lambda t: t[:].rearrange("p c k -> p (c k)")
        kk = pfx + "_split"
        self.v("dve", "tensor_copy", [akey], [kk], out=fl(sp["hi_b"]), in_=af)
        self.v("dve", "tensor_copy", [kk], [kk], out=fl(sp["hi_f"]), in_=fl(sp["hi_b"]))
        self.v("dve", "tensor_tensor", [akey, kk], [kk], out=r1[:], in0=af, in1=fl(sp["hi_f"]), op=ALU.subtract)
        self.v("dve", "tensor_copy", [kk], [kk], out=fl(sp["mid_b"]), in_=r1[:])
        self.v("dve", "tensor_copy", [kk], [kk], out=fl(sp["mid_f"]), in_=fl(sp["mid_b"]))
        self.v("dve", "tensor_tensor", [kk], [kk], out=r2[:], in0=r1[:], in1=fl(sp["mid_f"]), op=ALU.subtract)
        self.v("dve", "tensor_copy", [kk], [kk], out=fl(sp["lo_b"]), in_=r2[:])
        hm = self.A(pfx + "_hm", [128, NT, 8, 2], F32)
        self.v("dve", "tensor_copy", [kk], [kk], out=hm[:, :, :, 0], in_=sp["hi_f"][:])
        self.v("dve", "tensor_copy", [kk], [kk], out=hm[:, :, :, 1], in_=sp["mid_f"][:])
        sp["hm"] = hm
        sp["key"] = kk
        return sp

    def linattn_gates(self, sp, inp_all, akey):
        skey = sp["key"]
        for c in range(NT):
            for d in range(2):
                TRI = self.TRI_Fb if d == 0 else self.TRI_Bb
                parts = [sp[nm + "_b"][:, c, d * 4:(d + 1) * 4] for nm in ("hi", "mid", "lo")]
                for lhs, off in ((self.ONES_b, 0), (TRI, 4)):
                    col = c * 16 + d * 8 + off
                    for pi, pa in enumerate(parts):
                        self.mm(self.PS[5][:, col:col + 4], lhs, pa, start=(pi == 0), stop=(pi == 2), rd=["cstb", skey], wr=[("ps", 5)])
        gat = self.la_gat
        self.v("dve", "tensor_copy", [("ps", 5)], ["la_gat"], out=gat[:].rearrange("p c d k -> p (c d k)"), in_=self.PS[5][:, 0:NT * 16])
        g3 = gat[:].rearrange("p c d k -> p (c d) k")
        fl = lambda t: t[:].rearrange("p c d k -> p (c d) k")
        self.act(fl(self.la_etot), g3[:, :, 0:4], AF.Exp, rd=["la_gat"], wr=["la_etot"])
        self.act(fl(self.la_ecum), g3[:, :, 4:8], AF.Exp, rd=["la_gat"], wr=["la_ecum"])
        self.v("dve", "tensor_tensor", ["la_gat"], ["la_dec"], out=fl(self.la_dec), in0=g3[:, :, 0:4], in1=g3[:, :, 4:8], op=ALU.subtract)
        self.act(fl(self.la_dec), fl(self.la_dec), AF.Exp, rd=["la_dec"], wr=["la_dec"])
        self.v("dve", "tensor_tensor", ["la_dec", akey], ["la_dec"], out=fl(self.la_dec), in0=fl(self.la_dec),
               in1=inp_all[:].rearrange("p c (d k) -> p (c d) k", d=2), op=ALU.mult)

    def linattn_init(self, L):
        si = L["id"]
        self.v("dve", "memset", [], [f"laS{si}"], ap=L["S"][:], constant=0.0)
        self.v("dve", "memset", [], [f"laSbf{si}"], ap=L["Sbf"][:], constant=0.0)

    def linattn_step(self, L, d, c, want_out, qT, kT, ktok, vtok, gidx, qbase, sp, inp_all, akey, on_out, prep_chunk=None):
        Pv = L["Pv"]; si = L["id"]
        K = lambda nm, j=None: (f"la{nm}{si}", j) if j is not None else f"la{nm}{si}"
        TRI = self.TRI_Fb if d == 0 else self.TRI_Bb
        U = self.U_Fb if d == 0 else self.U_Bb
        NEGm = self.NEG_Fb if d == 0 else self.NEG_Bb
        S, Sbf = L["S"], L["Sbf"]
        skey = sp["key"]
        bA, bB, bC = (0, 1, 2) if si == 0 else (3, 4, 7)
        Dreg = lambda jj: self.PS[bA][:, jj * 128:(jj + 1) * 128]
        Greg = lambda jj: self.PS[bA][:, 256 + jj * 128:256 + (jj + 1) * 128]
        Yreg = lambda j: self.PS[bB][:, j * Pv:(j + 1) * Pv]
        Creg = lambda j: self.PS[bC][:, j * Pv:(j + 1) * Pv]
        Sbank = lambda j: bB if j < 2 else bC
        Sreg = lambda j: self.PS[Sbank(j)][0:64, 260 + (j % 2) * Pv:260 + (j % 2 + 1) * Pv]
        if prep_chunk is not None:
            prep_chunk(d, c)
            yield
        if want_out:
            gs = sorted(set(gidx(j) for j in range(4)))
            jrep = {g: [j for j in range(4) if gidx(j) == g][0] for g in gs}
            for g0 in range(0, len(gs), 2):
                for g in gs[g0:g0 + 2]:
                    self.mm(Greg(g % 2), kT(c, jrep[g]), qT(c, jrep[g]), rd=["laqk"], wr=[("ps", bA)])
                ng = len(gs[g0:g0 + 2])
                self.act(L["G"][:, g0:g0 + ng, :], self.PS[bA][:, 256:256 + ng * 128].rearrange("p (g t) -> p g t", g=ng), AF.Identity,
                         rd=[("ps", bA)], wr=[K("G", g) for g in gs[g0:g0 + ng]])
                yield
            self.v("dve", "tensor_tensor", ["cstb", skey], [K("A")], out=L["A"][:],
                   in0=TRI.unsqueeze(1).unsqueeze(1).broadcast_to([128, 4, 2, 128]),
                   in1=sp["hm"][:, c, d * 4:(d + 1) * 4, :].unsqueeze(3).broadcast_to([128, 4, 2, 128]), op=ALU.mult)
            yield
            for hp in range(2):
                for j in (2 * hp, 2 * hp + 1):
                    o = Dreg(j % 2)
                    self.mm(o, U, L["A"][:, j, 0, :], start=True, stop=False, rd=["cstb", K("A")], wr=[("ps", bA)])
                    self.mm(o, U, L["A"][:, j, 1, :], start=False, stop=False, rd=["cstb", K("A")], wr=[("ps", bA)])
                    self.mm(o, self.I_b, NEGm, start=False, stop=True, rd=["cstb"], wr=[("ps", bA)])
                self.act(L["E"][:, 2 * hp:2 * hp + 2, :], self.PS[bA][:, 0:256].rearrange("p (g t) -> p g t", g=2), AF.Exp,
                         rd=[("ps", bA)], wr=[K("E", 2 * hp), K("E", 2 * hp + 1)])
                yield
            for j in range(4):
                self.v("dve", "scalar_tensor_tensor", [K("E", j), K("G", gidx(j)), akey], [K("W", j)], out=L["W"][j][:], in0=L["E"][:, j, :],
                       scalar=inp_all[:, c, d * 4 + j:d * 4 + j + 1], in1=L["G"][:, gidx(j), :], op0=ALU.mult, op1=ALU.mult)
            yield
            for j in range(4):
                self.mm(Yreg(j), L["W"][j][:], vtok(c, j), rd=[K("W", j), "lav"], wr=[("ps", bB)])
            for j in range(4):
                b0 = qbase(j)
                self.mm(Creg(j), qT(c, j), Sbf[b0:b0 + 64, j, 0:Pv], rd=["laqk", K("Sbf")], wr=[("ps", bC)])
            yield
            for j in range(4):
                self.act(L["tc"][:, j, :], Creg(j), AF.Identity, rd=[("ps", bC), "la_ecum"], wr=[K("tc", j)], scale=self.la_ecum[:, c, d, j:j + 1])
            yield
            self.v("dve", "tensor_tensor", [K("tc", j) for j in range(4)] + [("ps", bB)], [K("ydir", j) for j in range(4)], out=L["ydir"][:],
                   in0=L["tc"][:], in1=self.PS[bB][:, 0:4 * Pv].rearrange("p (j v) -> p j v", j=4), op=ALU.add)
            yield
            on_out(c, d, L["ydir"], [K("ydir", j) for j in range(4)])
            yield
        for j in range(4):
            self.act(L["vs"][:, j, 0:Pv], vtok(c, j), AF.Identity, rd=["lav", "la_dec"], wr=[K("vs", j)], scale=self.la_dec[:, c, d, j:j + 1])
        yield
        for j in range(4):
            self.mm(Sreg(j), ktok(d, c, j), L["vs"][:, j, 0:Pv], rd=["laktok%d" % d, K("vs", j)], wr=[("ps", Sbank(j))])
        yield
        for j in range(4):
            self.v("dve", "scalar_tensor_tensor", [K("S"), "la_etot", ("ps", Sbank(j))], [K("S")], out=S[:, j, :], in0=S[:, j, :],
                   scalar=self.la_etot[0:64, c, d, j:j + 1], in1=Sreg(j), op0=ALU.mult, op1=ALU.add)
        yield
        for b0 in (0, 64):
            hs = [j for j in range(4) if qbase(j) == b0]
            j0, st = hs[0], hs[1] - hs[0]
            self.act(Sbf[b0:b0 + 64, j0:j0 + st + 1:st, 0:Pv], S[:, j0:j0 + st + 1:st, :], AF.Identity, rd=[K("S")], wr=[K("Sbf")])

    def linattn_run(self, skip_out, **args):
        orders = [list(range(NT)), [1, 0] + list(range(NT - 1, 1, -1))]
        for L in self.la_sets:
            self.linattn_init(L)
        for i in range(NT):
            gens = [self.linattn_step(self.la_sets[d], d, orders[d][i], orders[d][i] not in skip_out, **args) for d in range(2)]
            while gens:
                for g in list(gens):
                    try:
                        next(g)
                    except StopIteration:
                        gens.remove(g)

    def mixer_mlstm(self, l, s, need_ctx):
        m = self.mark()
        self.mixer_common_alloc(l, SEG_A, 1040, 0, defer_wmix=True)
        wog = self.load_gate_w(l, SEG_A + 768, "a_wog")
        qT = self.A("a_qT", [128, 2, T], BF16); kT = self.A("a_kT", [128, 2, T], BF16)
        vaug = self.A("a_vaug", [128, NT, 4, 80], BF16)
        a_all = self.A("a_a", [128, NT, 8], F32); inp_all = self.A("a_inp", [128, NT, 8], F32)
        hsum = self.A("a_hsum", [128, NT, 256], F32 if self.cfg.get("acc_f32") else BF16)
        gt = self.A("a_gt", [128, 16], F32); gt2 = self.A("a_gt2", [128, 2, 4], F32)
        m_w = self.mark()
        self.alloc_wmix(l, SEG_A, 1040)
        self.v("dve", "memset", [], ["a_vaug"], ap=vaug[:], constant=1.0)
        for sti in range(5):
            t0, n = ST[sti]
            for c in range(2):
                pb = self.rot("pj", [0, 1, 2, 3])
                self.proj_fm(c * 128, 128, t0, n, pb)
                self.act(qT[:, c, t0:t0 + n], self.PS[pb][:, 0:n], AF.Identity, rd=[("ps", pb)], wr=["laqk"])
                pb = self.rot("pj", [0, 1, 2, 3])
                self.proj_fm(256 + c * 128, 128, t0, n, pb)
                self.act(kT[:, c, t0:t0 + n], self.PS[pb][:, 0:n], AF.Identity, rd=[("ps", pb)], wr=["laqk"], scale=0.125)
            for tt in range(n // 128):
                gi = t0 // 128 + tt
                pb = self.rot("pj", [0, 1, 2, 3])
                self.proj_tm(512, 256, gi, pb)
                self.v("dve", "tensor_copy", [("ps", pb), "a_vaug"], ["lav"], out=vaug[:, gi, :, 0:64],
                       in_=self.PS[pb][:, 0:256].rearrange("p (h d) -> p h d", h=4))
                pb = self.rot("pj", [0, 1, 2, 3])
                self.proj_tm(1024, 16, gi, pb)
                self.v("dve", "tensor_tensor", [("ps", pb), "rep"], ["a_gt"], out=gt[:], in0=self.PS[pb][:, 0:16], in1=self.rep[:, R_GATEB:R_GATEB + 16], op=ALU.add)
                g4 = gt[:].rearrange("p (k h) -> p k h", k=4)
                self.act(inp_all[:, gi, :].rearrange("p (k h) -> p k h", k=2), g4[:, 0:4:2, :], AF.Exp, rd=["a_gt"], wr=["a_gates"])
                self.act(gt2[:], g4[:, 1:4:2, :], AF.Exp, rd=["a_gt"], wr=["a_gt2"], scale=-1.0)
                self.act(gt2[:], gt2[:], AF.Ln, rd=["a_gt2"], wr=["a_gt2"], bias=self.eps_t[:, 2:3])
                self.v("dve", "tensor_scalar", ["a_gt2"], ["a_gates"], out=a_all[:, gi, :].rearrange("p (k h) -> p k h", k=2), in0=gt2[:], scalar1=-1.0, scalar2=None, op0=ALU.mult)
        self.release(m_w)
        m_la = self.mark()
        self.linattn_alloc(65)
        sp = self.split_gates(a_all, "a_gates", "a_sp")
        self.linattn_gates(sp, inp_all, "a_gates")
        hd = [self.A(f"a_hd{i}", [128, 4], F32) for i in range(2)]; hr = [self.A(f"a_hr{i}", [128, 4], F32) for i in range(2)]
        hdir = [self.A(f"a_hdir{i}", [128, 4, 64], F32) for i in range(2)]
        ktmp = [self.A(f"a_ktmp{i}", [128, 256], BF16) for i in range(2)]
        visited = set()

        def prep_chunk(d, c):
            for b_ in range(2):
                self.mm(self.PS[6][:, (d * 2 + b_) * 128:(d * 2 + b_ + 1) * 128], kT[:, b_, c * 128:(c + 1) * 128], self.I_b, rd=["laqk", "cstb"], wr=[("ps", 6)])
            self.act(ktmp[d][:], self.PS[6][:, d * 256:(d + 1) * 256], AF.Identity, rd=[("ps", 6)], wr=["laktok%d" % d])

        def on_out(c, d, ydir, ykeys):
            self.act(hd[d][:], ydir[:, :, 64], AF.Abs, rd=ykeys, wr=[f"a_hd{d}"])
            self.v("dve", "tensor_scalar", [f"a_hd{d}"], [f"a_hd{d}"], out=hd[d][:], in0=hd[d][:], scalar1=1.0, scalar2=None, op0=ALU.max)
            self.v("dve", "reciprocal", [f"a_hd{d}"], [f"a_hr{d}"], out=hr[d][:], in_=hd[d][:])
            hs = hsum[:, c, :].rearrange("p (h d) -> p h d", h=4)
            rb = hr[d][:].unsqueeze(2).broadcast_to([128, 4, 64])
            if c not in visited:
                visited.add(c)
                self.v("dve", "tensor_tensor", ykeys + [f"a_hr{d}"], [("a_hsum", c)], out=hs, in0=ydir[:, :, 0:64], in1=rb, op=ALU.mult)
            else:
                self.v("dve", "tensor_tensor", ykeys + [f"a_hr{d}"], [f"a_hdir{d}"], out=hdir[d][:], in0=ydir[:, :, 0:64], in1=rb, op=ALU.mult)
                self.v("dve", "tensor_tensor", [f"a_hdir{d}", ("a_hsum", c)], [("a_hsum", c)], out=hs, in0=hs, in1=hdir[d][:], op=ALU.add)

        skip = set() if need_ctx else {0, 1}
        args = dict(qT=lambda c, j: qT[(j % 2) * 64:(j % 2) * 64 + 64, j // 2, c * 128:(c + 1) * 128],
                    kT=lambda c, j: kT[(j % 2) * 64:(j % 2) * 64 + 64, j // 2, c * 128:(c + 1) * 128],
                    ktok=lambda d, c, j: ktmp[d][:, j * 64:(j + 1) * 64], vtok=lambda c, j: vaug[:, c, j, 0:65],
                    gidx=lambda j: j, qbase=lambda j: (j % 2) * 64, sp=sp, inp_all=inp_all, akey="a_gates", on_out=on_out, prep_chunk=prep_chunk)
        self.linattn_run(skip, **args)
        self.release(m_la)
        F = []
        for sl in range(2):
            F.append(dict(sq=self.A(f"a_sq{sl}", [128, 256], F32), ss=self.A(f"a_ss{sl}", [128, 4], F32), lnb=self.A(f"a_lnb{sl}", [128, 4], F32),
                          sig=self.A(f"a_sig{sl}", [128, 256], F32), hn=self.A(f"a_hn{sl}", [128, 256], F32), otok=self.A(f"a_otok{sl}", [128, 256], BF16)))
        mixT = self.A("a_mixT", [128, 2, 512], BF16)

        def fin_tile(gi, tt, sl):
            f = F[sl]; k = lambda nm: f"a_{nm}{sl}"
            pb = sl
            self.proj_tm(0, 256, gi, pb, w=wog, wkey="a_wog")
            self.act(f["sig"][:], self.PS[pb][:, 0:256], AF.Sigmoid, rd=[("ps", pb)], wr=[k("sig")])
            self.act(f["sq"][:], hsum[:, gi, :], AF.Square, rd=[("a_hsum", gi)], wr=[k("sq")])
            self.v("dve", "tensor_reduce", [k("sq")], [k("ss")], out=f["ss"][:], in_=f["sq"][:].rearrange("p (h d) -> p h d", h=4), axis=AX.X, op=ALU.add)
            self.v("dve", "tensor_scalar", [k("ss")], [k("ss")], out=f["ss"][:], in0=f["ss"][:], scalar1=1.0 / 64, scalar2=None, op0=ALU.mult)
            self.ln_exp(f["ss"][:], f["ss"][:], self.eps_t[:, 1:2], -0.5, [k("ss"), "eps_t"], [k("ss")], k("lnb"), f["lnb"][:])
            self.v("dve", "tensor_tensor", [("a_hsum", gi), k("ss")], [k("hn")], out=f["hn"][:].rearrange("p (h d) -> p h d", h=4),
                   in0=hsum[:, gi, :].rearrange("p (h d) -> p h d", h=4), in1=f["ss"][:].unsqueeze(2).broadcast_to([128, 4, 64]), op=ALU.mult)
            self.v("dve", "tensor_tensor", [k("hn"), "rep"], [k("hn")], out=f["hn"][:], in0=f["hn"][:], in1=self.rep[:, R_ONORM:R_ONORM + 256], op=ALU.mult)
            self.v("dve", "tensor_tensor", [k("hn"), k("sig")], [k("otok")], out=f["otok"][:], in0=f["hn"][:], in1=f["sig"][:], op=ALU.mult)
            self.tok_to_mixT(f["otok"][:], k("otok"), mixT, "a_mixT", tt, s, 0, gi, pb=6 + sl)

        for sti in range(5):
            if sti == 0 and not need_ctx:
                continue
            t0, n = ST[sti]
            for tp_ in range(0, n // 128, 2):
                self.P.interleave([lambda tt=tt: fin_tile(t0 // 128 + tt, tt, tt % 2) for tt in (tp_, tp_ + 1)])
            self.out_proj(l, s, sti, mixT, "a_mixT", n)
        self.release(m)

    def mixer_ssd(self, l, s, need_ctx):
        m = self.mark()
        self.mixer_common_alloc(l, SEG_S, 776, 3, defer_wmix=True)
        wz = self.load_gate_w(l, SEG_S, "s_wz")
        xbcT = self.A("s_xbcT", [128, 4, T], BF16)
        a_all = self.A("s_a", [128, NT, 8], F32); inp_all = self.A("s_inp", [128, NT, 8], F32)
        Arep = self.A("s_Arep", [128, 8], F32); dtt = self.A("s_dtt", [128, 8], F32)
        self.act(Arep[:], self.rep[:, R_ALOG:R_ALOG + 8], AF.Exp, rd=["rep"], wr=["s_Arep"])
        self.v("dve", "tensor_scalar", ["s_Arep"], ["s_Arep"], out=Arep[:], in0=Arep[:], scalar1=-1.0, scalar2=None, op0=ALU.mult)
        m2 = self.mark()
        self.alloc_wmix(l, SEG_S, 776, pad=784)
        pre_c = self.A("s_prec", [128, 4, TCX + 4], F32); pre_l = self.A("s_prel", [128, 4, TL + 4], F32)
        acc = self.A("s_acc", [128, TL], F32)
        for pre, n_ in ((pre_c, TCX), (pre_l, TL)):
            self.v("dve", "memset", [], ["s_pre"], ap=pre[:, :, 0:2], constant=0.0)
            self.v("dve", "memset", [], ["s_pre"], ap=pre[:, :, n_ + 2:n_ + 4], constant=0.0)
        for sti in range(5):
            t0, n = ST[sti]
            for c in range(4):
                pb = self.rot("pj", [0, 1, 2, 3])
                self.proj_fm(256 + c * 128, 128, t0, n, pb)
                dst = pre_c[:, c, 2:2 + n] if sti == 0 else pre_l[:, c, 2 + t0 - TCX:2 + t0 - TCX + n]
                self.act(dst, self.PS[pb][:, 0:n], AF.Identity, rd=[("ps", pb), "s_pre"], wr=[("s_pre2", sti, c)])
            for tt in range(n // 128):
                gi = t0 // 128 + tt
                pb = self.rot("pj", [0, 1, 2, 3])
                self.proj_tm(768, 8, gi, pb)
                self.v("dve", "tensor_tensor", [("ps", pb), "rep"], ["s_dtt"], out=dtt[:], in0=self.PS[pb][:, 0:8], in1=self.rep[:, R_DTB:R_DTB + 8], op=ALU.add)
                self.act(dtt[:], dtt[:], AF.Exp, rd=["s_dtt"], wr=["s_dtt"])
                self.act(inp_all[:, gi, :], dtt[:], AF.Ln, rd=["s_dtt"], wr=["s_gates"], bias=self.eps_t[:, 2:3])
                self.v("dve", "tensor_tensor", ["s_gates", "s_Arep"], ["s_gates"], out=a_all[:, gi, :], in0=inp_all[:, gi, :], in1=Arep[:], op=ALU.mult)
        for pre, n_, tb, stis in ((pre_c, TCX, 0, [0]), (pre_l, TL, TCX, [1, 2, 3, 4])):
            for c in range(4):
                rdp = [("s_pre2", sti, c) for sti in stis] + ["s_pre", "pp"]
                cw = lambda k: self.pp[:, P_CW + (l * 4 + c) * 5 + k: P_CW + (l * 4 + c) * 5 + k + 1]
                self.v("dve", "tensor_scalar", rdp, ["s_acc"], out=acc[:, 0:n_], in0=pre[:, c, 0:n_], scalar1=cw(0),
                       scalar2=self.pp[:, P_CB + l * 4 + c:P_CB + l * 4 + c + 1], op0=ALU.mult, op1=ALU.add)
                for k in range(1, 5):
                    self.v("dve", "scalar_tensor_tensor", rdp + ["s_acc"], ["s_acc"], out=acc[:, 0:n_], in0=pre[:, c, k:k + n_], scalar=cw(k),
                           in1=acc[:, 0:n_], op0=ALU.mult, op1=ALU.add)
                self.act(xbcT[:, c, tb:tb + n_], acc[:, 0:n_], AF.Silu, rd=["s_acc"], wr=["s_xbcT", "laqk"])
        self.release(m2)
        sstage = self.cfg.get("s_stage", 9)
        if sstage < 2:
            self.release(m)
            return
        xtok = self.A("s_xtok", [128, NT, 256], BF16); btok = self.A("s_btok", [128, NT, 128], BF16)
        yacc = self.A("s_yacc", [128, NT, 256], F32 if self.cfg.get("acc_f32") else BF16)
        m_la = self.mark()
        self.linattn_alloc(64)
        for gi in range(NT):
            pb = self.rot("pj", [0, 1, 2, 3])
            for c in range(3):
                self.mm(self.PS[pb][:, c * 128:(c + 1) * 128], xbcT[:, c, gi * 128:(gi + 1) * 128], self.I_b, rd=["s_xbcT", "cstb"], wr=[("ps", pb)])
            self.act(xtok[:, gi, :], self.PS[pb][:, 0:256], AF.Identity, rd=[("ps", pb)], wr=["lav"])
            self.act(btok[:, gi, :], self.PS[pb][:, 256:384], AF.Identity, rd=[("ps", pb)], wr=["laktok0", "laktok1"])

        if sstage < 3:
            self.release(m)
            return

        sp = self.split_gates(a_all, "s_gates", "s_sp")
        self.linattn_gates(sp, inp_all, "s_gates")
        visited = set()

        def on_out(c, d, ydir, ykeys):
            ys = yacc[:, c, :].rearrange("p (h d) -> p h d", h=4)
            if c not in visited:
                visited.add(c)
                self.v("dve", "tensor_copy", ykeys, [("s_yacc", c)], out=ys, in_=ydir[:])
            else:
                self.v("dve", "tensor_tensor", ykeys + [("s_yacc", c)], [("s_yacc", c)], out=ys, in0=ys, in1=ydir[:], op=ALU.add)

        skip = set() if need_ctx else {0, 1}
        args = dict(qT=lambda c, j: xbcT[(j // 2) * 64:(j // 2) * 64 + 64, 3, c * 128:(c + 1) * 128],
                    kT=lambda c, j: xbcT[(j // 2) * 64:(j // 2) * 64 + 64, 2, c * 128:(c + 1) * 128],
                    ktok=lambda d, c, j: btok[:, c, (j // 2) * 64:(j // 2) * 64 + 64], vtok=lambda c, j: xtok[:, c, j * 64:(j + 1) * 64],
                    gidx=lambda j: j // 2, qbase=lambda j: (j // 2) * 64, sp=sp, inp_all=inp_all, akey="s_gates", on_out=on_out)
        self.linattn_run(skip, **args)
        self.release(m_la)
        F = []
        for sl in range(2):
            F.append(dict(sz=self.A(f"s_sz{sl}", [128, 256], F32), yy=self.A(f"s_yy{sl}", [128, 256], F32), sq=self.A(f"s_sq{sl}", [128, 256], F32),
                          ss=self.A(f"s_ss{sl}", [128, 2], F32), lnb=self.A(f"s_lnb{sl}", [128, 2], F32), otok=self.A(f"s_otok{sl}", [128, 256], BF16)))
        mixT = self.A("s_mixT", [128, 2, 512], BF16)

        def fin_tile(gi, tt, sl):
            f = F[sl]; k = lambda nm: f"s_{nm}{sl}"
            pb = sl
            yy = f["yy"]
            self.proj_tm(0, 256, gi, pb, w=wz, wkey="s_wz")
            self.act(f["sz"][:], self.PS[pb][:, 0:256], AF.Silu, rd=[("ps", pb)], wr=[k("sz")])
            y4 = yy[:].rearrange("p (h d) -> p h d", h=4)
            self.v("dve", "tensor_tensor", ["lav", "rep"], [k("yy")], out=y4, in0=xtok[:, gi, :].rearrange("p (h d) -> p h d", h=4),
                   in1=self.rep[:, R_SSDD:R_SSDD + 4].unsqueeze(2).broadcast_to([128, 4, 64]), op=ALU.mult)
            self.v("dve", "tensor_tensor", [k("yy"), ("s_yacc", gi)], [k("yy")], out=yy[:], in0=yy[:], in1=yacc[:, gi, :], op=ALU.add)
            self.v("dve", "tensor_tensor", [k("yy"), k("sz")], [k("yy")], out=yy[:], in0=yy[:], in1=f["sz"][:], op=ALU.mult)
            self.act(f["sq"][:], yy[:], AF.Square, rd=[k("yy")], wr=[k("sq")])
            self.v("dve", "tensor_reduce", [k("sq")], [k("ss")], out=f["ss"][:], in_=f["sq"][:].rearrange("p (g d) -> p g d", g=2), axis=AX.X, op=ALU.add)
            self.v("dve", "tensor_scalar", [k("ss")], [k("ss")], out=f["ss"][:], in0=f["ss"][:], scalar1=1.0 / 128, scalar2=None, op0=ALU.mult)
            self.ln_exp(f["ss"][:], f["ss"][:], self.eps_t[:, 1:2], -0.5, [k("ss"), "eps_t"], [k("ss")], k("lnb"), f["lnb"][:])
            self.v("dve", "tensor_tensor", [k("yy"), k("ss")], [k("yy")], out=yy[:].rearrange("p (g d) -> p g d", g=2),
                   in0=yy[:].rearrange("p (g d) -> p g d", g=2), in1=f["ss"][:].unsqueeze(2).broadcast_to([128, 2, 128]), op=ALU.mult)
            self.v("dve", "tensor_tensor", [k("yy"), "rep"], [k("otok")], out=f["otok"][:], in0=yy[:], in1=self.rep[:, R_SNORM:R_SNORM + 256], op=ALU.mult)
            self.tok_to_mixT(f["otok"][:], k("otok"), mixT, "s_mixT", tt, s, 3, gi, pb=6 + sl)

        for sti in range(5):
            if sti == 0 and not need_ctx:
                continue
            t0, n = ST[sti]
            for tp_ in range(0, n // 128, 2):
                self.P.interleave([lambda tt=tt: fin_tile(t0 // 128 + tt, tt, tt % 2) for tt in (tp_, tp_ + 1)])
            self.out_proj(l, s, sti, mixT, "s_mixT", n)
        self.release(m)

    def rope(self, x3, H, q, tab, gi, tmp1, tmp2, rd, key):
        lt = gi - 2
        cos = tab[:, lt, 0, :].unsqueeze(1).broadcast_to([128, H, 4 * q])
        xv = x3.rearrange("p h (a b d) -> p h a b d", a=2, b=2)
        tv = tmp2.rearrange("p h (a b d) -> p h a b d", a=2, b=2)
        sv = tab[:, lt, 1, :].rearrange("p (a b d) -> p a b d", a=2, b=2)
        for b_ in range(2):
            self.v("dve", "tensor_tensor", rd + ["rope"], [key + "_t2"], out=tv[:, :, :, b_, :], in0=xv[:, :, :, 1 - b_, :],
                   in1=sv[:, :, b_, :].unsqueeze(1).broadcast_to([128, H, 2, q]), op=ALU.mult)
        self.v("dve", "tensor_tensor", rd + ["rope"], [key + "_t1"], out=tmp1, in0=x3, in1=cos, op=ALU.mult)
        self.v("dve", "tensor_tensor", [key + "_t1", key + "_t2"] + rd, rd, out=x3, in0=tmp1, in1=tmp2, op=ALU.add)

    def mixer_swa(self, l, s, need_ctx):
        m = self.mark()
        self.mixer_common_alloc(l, SEG_W, 512, 2)
        qTs = self.A("w_qT", [128, 2, T], BF16)
        kTs = self.A("w_kT", [128, T], BF16)
        vaug = self.A("w_vaug", [128, NT, 2, 80], BF16)
        tab = self.A("w_rope", [128, 16, 2, 64], F32)
        self.dma("sp", tab[:], self.ropeS_d, [], ["rope"], "rope")
        gain6 = self.A("w_gain6", [128, 6, 64], F32)
        self.v("dve", "tensor_copy", ["rep"], ["w_gain6"], out=gain6[:, 0:4, :], in_=self.rep[:, R_SQG:R_SQG + 64].unsqueeze(1).broadcast_to([128, 4, 64]))
        self.v("dve", "tensor_copy", ["rep", "w_gain6"], ["w_gain6"], out=gain6[:, 4:6, :], in_=self.rep[:, R_SKG:R_SKG + 64].unsqueeze(1).broadcast_to([128, 2, 64]))
        esink = self.A("w_esink", [128, 4], F32)
        self.act(esink[:], self.rep[:, R_SINK:R_SINK + 4], AF.Exp, rd=["rep"], wr=["w_esink"])
        self.v("dve", "memset", [], ["w_vaug"], ap=vaug[:], constant=1.0)
        W_ = []
        for sl in range(2):
            W_.append(dict(sq=self.A(f"w_sq{sl}", [128, 384], F32), ss=self.A(f"w_ss{sl}", [128, 6], F32), lnb=self.A(f"w_lnb{sl}", [128, 6], F32),
                           qk=self.A(f"w_qk{sl}", [128, 6, 64], F32), t1=self.A(f"w_t1{sl}", [128, 6, 64], F32), t2=self.A(f"w_t2{sl}", [128, 6, 64], F32),
                           qkb=self.A(f"w_qkb{sl}", [128, 384], BF16)))

        def prep_tile(gi, sl):
            f = W_[sl]; k = lambda nm: f"w_{nm}{sl}"
            sq, ss, lnb, qk, qkb = f["sq"], f["ss"], f["lnb"], f["qk"], f["qkb"]
            pb = 2 * sl
            self.proj_tm(0, 512, gi, pb)
            ps = self.PS[pb]
            self.act(sq[:], ps[:, 0:384], AF.Square, rd=[("ps", pb)], wr=[k("sq")])
            self.v("dve", "tensor_reduce", [k("sq")], [k("ss")], out=ss[:], in_=sq[:].rearrange("p (h d) -> p h d", h=6), axis=AX.X, op=ALU.add)
            self.v("dve", "tensor_scalar", [k("ss")], [k("ss")], out=ss[:], in0=ss[:], scalar1=1.0 / 64, scalar2=None, op0=ALU.mult)
            self.ln_exp(ss[:], ss[:], self.eps_t[:, 1:2], -0.5, [k("ss"), "eps_t"], [k("ss")], k("lnb"), lnb[:])
            self.v("dve", "tensor_tensor", [("ps", pb), k("ss")], [k("qk")], out=qk[:], in0=ps[:, 0:384].rearrange("p (h d) -> p h d", h=6),
                   in1=ss[:].unsqueeze(2).broadcast_to([128, 6, 64]), op=ALU.mult)
            self.v("dve", "tensor_tensor", [k("qk"), "w_gain6"], [k("qk")], out=qk[:], in0=qk[:], in1=gain6[:], op=ALU.mult)
            self.v("dve", "tensor_copy", [("ps", pb), "w_vaug"], [("w_vaug", gi)], out=vaug[:, gi, :, 0:64], in_=ps[:, 384:512].rearrange("p (h d) -> p h d", h=2))
            if gi >= 2:
                self.rope(qk[:], 6, 16, tab, gi, f["t1"][:], f["t2"][:], [k("qk")], k("r"))
            self.v("dve", "tensor_copy", [k("qk")], [k("qkb")], out=qkb[:, 0:256].rearrange("p (b a d) -> p a b d", b=2, a=2),
                   in_=qk[:, 0:4, :].rearrange("p (a b) d -> p a b d", a=2))
            self.v("dve", "tensor_copy", [k("qk")], [k("qkb")], out=qkb[:, 256:384], in_=qk[:, 4:6, :].rearrange("p h d -> p (h d)"))
            pt = 2 * sl + 1
            for c in range(3):
                self.mm(self.PS[pt][:, c * 128:(c + 1) * 128], qkb[:, c * 128:(c + 1) * 128], self.I_b, rd=[k("qkb"), "cstb"], wr=[("ps", pt)])
            self.act(qTs[:, :, gi * 128:(gi + 1) * 128], self.PS[pt][:, 0:256].rearrange("p (c t) -> p c t", c=2), AF.Identity, rd=[("ps", pt)], wr=[("w_qT", gi)])
            self.act(kTs[:, gi * 128:(gi + 1) * 128], self.PS[pt][:, 256:384], AF.Identity, rd=[("ps", pt)], wr=[("w_kT", gi)])

        for g0 in range(0, NT, 2):
            self.P.interleave([lambda gi=gi: prep_tile(gi, gi % 2) for gi in (g0, g0 + 1)])
        PT = [self.A(f"w_PT{i}", [128, 5, 128], BF16) for i in range(2)]
        otok = [self.A(f"w_otok{i}", [128, 256], BF16) for i in range(2)]
        den = self.A("w_den", [128, 4], F32)
        mixT = self.A("w_mixT", [128, 2, 512], BF16)
        for sti in range(5):
            if sti == 0 and not need_ctx:
                continue
            t0, n = ST[sti]
            def keys_of(gi):
                if gi < 2:
                    return [(0, None), (1, None)]
                keys = []
                if gi - 1 >= 2:
                    keys.append((gi - 1, self.NEG_Bb))
                keys.append((gi, None))
                if gi + 1 < NT:
                    keys.append((gi + 1, self.NEG_Fb))
                return keys + [(0, None), (1, None)]

            items = [(tt, h) for tt in range(n // 128) for h in range(4)]

            def s_stage(it):
                tt, h = it
                gi = t0 // 128 + tt
                keys = keys_of(gi)
                a_, b_ = h // 2, h % 2
                q_ap = qTs[a_ * 64:(a_ + 1) * 64, b_, gi * 128:(gi + 1) * 128]
                pa = self.rot("w_s", [0, 1]); pbk = self.rot("w_s2", [2, 3])
                for ki, (kt, msk) in enumerate(keys):
                    pbank = pa if ki < 4 else pbk
                    col = (ki % 4) * 128
                    self.mm(self.PS[pbank][:, col:col + 128], kTs[a_ * 64:(a_ + 1) * 64, kt * 128:(kt + 1) * 128], q_ap,
                            start=True, stop=(msk is None), rd=["w_kT", "w_qT"], wr=[("ps", pbank)])
                    if msk is not None:
                        self.mm(self.PS[pbank][:, col:col + 128], self.I_b, msk, start=False, stop=True, rd=["cstb"], wr=[("ps", pbank)])
                return pa, pbk

            banks_next = s_stage(items[0])
            ob = 0
            for idx, (tt, h) in enumerate(items):
                gi = t0 // 128 + tt
                keys = keys_of(gi)
                a_ = h // 2
                pa, pbk = banks_next
                if idx + 1 < len(items):
                    banks_next = s_stage(items[idx + 1])
                if h == 0:
                    ob = self.rot("w_otok", [0, 1])
                pto = self.rot("w_pt", [0, 1])
                nk = len(keys)
                n1 = min(nk, 4)
                self.act(PT[pto][:, 0:n1, :], self.PS[pa][:, 0:n1 * 128].rearrange("p (k t) -> p k t", k=n1), AF.Exp, rd=[("ps", pa)], wr=[("w_PT", pto)], scale=0.125)
                if nk > 4:
                    self.act(PT[pto][:, 4:5, :], self.PS[pbk][:, 0:128].rearrange("p (k t) -> p k t", k=1), AF.Exp, rd=[("ps", pbk)], wr=[("w_PT", pto)], scale=0.125)
                po = self.rot("w_o", [4, 5])
                for ki, (kt, msk) in enumerate(keys):
                    self.mm(self.PS[po][:, 0:65], PT[pto][:, ki, :], vaug[:, kt, a_, 0:65], start=(ki == 0), stop=(ki == nk - 1),
                            rd=[("w_PT", pto), "w_vaug"], wr=[("ps", po)])
                self.v("dve", "tensor_tensor", [("ps", po), "w_esink"], [("w_den", h)], out=den[:, h:h + 1], in0=self.PS[po][:, 64:65], in1=esink[:, h:h + 1], op=ALU.add)
                self.v("dve", "reciprocal", [("w_den", h)], [("w_den", h)], out=den[:, h:h + 1], in_=den[:, h:h + 1])
                self.act(otok[ob][:, h * 64:(h + 1) * 64], self.PS[po][:, 0:64], AF.Identity, rd=[("ps", po), ("w_den", h)], wr=[("w_otok", ob)], scale=den[:, h:h + 1])
                if h == 3:
                    self.tok_to_mixT(otok[ob][:], ("w_otok", ob), mixT, "w_mixT", tt, s, 2, gi)
            self.out_proj(l, s, sti, mixT, "w_mixT", n)
        self.release(m)

    def mixer_mla(self, l, s, need_ctx):
        m = self.mark()
        self.mixer_common_alloc(l, SEG_M, 416, 1)
        qT = self.A("m_qT", [128, 4, T], BF16); kT = self.A("m_kT", [128, 4, T], BF16)
        vaug = self.A("m_vaug", [128, NT, 4, 80], BF16)
        wqb = self.A("m_wqb", [128, 2, 384], BF16); wkvb = self.A("m_wkvb", [128, 512], BF16)
        self.dma("pool", wqb[:], self.wq_b[l].rearrange("(c p) n -> p c n", p=128), [], ["m_wqb"], "m_wqb")
        self.dma("pool", wkvb[:], self.wkv_b[l], [], ["m_wkvb"], "m_wkvb")
        self.v("dve", "memset", [], ["m_vaug"], ap=vaug[:], constant=1.0)
        gq16 = self.A("m_gq16", [128, 2], F32); gkv = self.A("m_gkv", [128, 1], F32)
        self.v("dve", "tensor_scalar", ["pp"], ["m_g"], out=gq16[:], in0=self.pp[:, P_QN + l * 2:P_QN + l * 2 + 2], scalar1=16.0, scalar2=None, op0=ALU.mult)
        self.v("dve", "tensor_scalar", ["pp"], ["m_g"], out=gkv[:], in0=self.pp[:, P_KVN + l:P_KVN + l + 1], scalar1=float(math.sqrt(128.0)), scalar2=None, op0=ALU.mult)
        epsq = self.A("m_epsq", [128, 2], F32)
        self.v("dve", "memset", [], ["m_epsq"], ap=epsq[:, 0:1], constant=float(256 * EPS))
        self.v("dve", "memset", [], ["m_epsq"], ap=epsq[:, 1:2], constant=float(128 * EPS))
        m2 = self.mark()
        tab = self.A("m_rope", [128, 16, 2, 32], F32)
        self.dma("sp", tab[:], self.ropeM_d, [], ["rope"], "rope")
        aq = self.A("m_aq", [128, 2, 256], F32); aqsq = self.A("m_aqsq", [128, 2, 256], BF16); aqn = self.A("m_aqn", [128, 2, 256], BF16)
        akv = self.A("m_akv", [128, 256], F32); akvsq = self.A("m_akvsq", [128, 256], BF16); akvn = self.A("m_akvn", [128, 256], BF16)
        rs1 = self.A("m_rs1", [128, 256], F32); rs2 = self.A("m_rs2", [128, 256], F32); lnt = self.A("m_lnt", [128, 256], F32)
        sqq = self.A("m_sqq", [128, 4, 96], F32); ssq = self.A("m_ssq", [128, 12], F32); lnq = self.A("m_lnq", [128, 12], F32)
        sqk = self.A("m_sqk", [128, 4, 64], F32)
        qn = self.A("m_qn", [128, 4, 96], F32); kn = self.A("m_kn", [128, 4, 96], F32)
        qnb = self.A("m_qnb", [128, 4, 96], BF16); knb = self.A("m_knb", [128, 4, 96], BF16)
        t1 = self.A("m_t1", [128, 4, 32], F32); t2 = self.A("m_t2", [128, 4, 32], F32)
        kr = self.A("m_kr", [128, 1, 32], F32); sqr = self.A("m_sqr", [128, 32], F32)
        qr = self.A("m_qr", [128, 4, 32], F32); ssr = self.A("m_ssr", [128, 4], F32)
        ssk = self.A("m_ssk", [128, 4], F32); lnk = self.A("m_lnk", [128, 4], F32)
        t1k = self.A("m_t1k", [128, 1, 32], F32); t2k = self.A("m_t2k", [128, 1, 32], F32)
        QG = self.rep[:, R_QG:R_QG + 96]; KG = self.rep[:, R_KG:R_KG + 96]
        for piece in range(T // 256):
            t0, n = piece * 256, 256
            for c in range(2):
                pb = self.rot("pj", [0, 1, 2, 3])
                self.proj_fm(c * 128, 128, t0, n, pb)
                self.act(aq[:, c, 0:n], self.PS[pb][:, 0:n], AF.Identity, rd=[("ps", pb)], wr=["m_aq"])
            self.act(aqsq[:, :, 0:n], aq[:, :, 0:n], AF.Square, rd=["m_aq"], wr=["m_aqsq"])
            pb = self.rot("pj", [0, 1, 2, 3])
            for c in range(2):
                self.mm(self.PS[pb][:, 0:n], self.ONES_b, aqsq[:, c, 0:n], start=(c == 0), stop=(c == 1), rd=["m_aqsq", "cstb"], wr=[("ps", pb)])
            self.ln_exp(rs1[:, 0:n], self.PS[pb][:, 0:n], epsq[:, 0:1], -0.5, [("ps", pb), "m_epsq"], ["m_rs1"], "m_lnt", lnt[:, 0:n])
            for c in range(2):
                self.v("dve", "scalar_tensor_tensor", ["m_aq", "m_rs1", "m_g"], ["m_aqn"], out=aqn[:, c, 0:n], in0=aq[:, c, 0:n], scalar=gq16[:, c:c + 1],
                       in1=rs1[:, 0:n], op0=ALU.mult, op1=ALU.mult)
            pb = self.rot("pj", [0, 1, 2, 3])
            self.proj_fm(256, 128, t0, n, pb)
            self.act(akv[:, 0:n], self.PS[pb][:, 0:n], AF.Identity, rd=[("ps", pb)], wr=["m_akv"])
            self.act(akvsq[:, 0:n], akv[:, 0:n], AF.Square, rd=["m_akv"], wr=["m_akvsq"])
            pb = self.rot("pj", [0, 1, 2, 3])
            self.mm(self.PS[pb][:, 0:n], self.ONES_b, akvsq[:, 0:n], rd=["m_akvsq", "cstb"], wr=[("ps", pb)])
            self.ln_exp(rs2[:, 0:n], self.PS[pb][:, 0:n], epsq[:, 1:2], -0.5, [("ps", pb), "m_epsq"], ["m_rs2"], "m_lnt", lnt[:, 0:n])
            self.v("dve", "scalar_tensor_tensor", ["m_akv", "m_rs2", "m_g"], ["m_akvn"], out=akvn[:, 0:n], in0=akv[:, 0:n], scalar=gkv[:, 0:1],
                   in1=rs2[:, 0:n], op0=ALU.mult, op1=ALU.mult)
            for tt in range(n // 128):
                gi = t0 // 128 + tt
                tsl = slice(tt * 128, (tt + 1) * 128)

                def q_body(gi=gi, tsl=tsl):
                    pq = 0
                    for c in range(2):
                        self.mm(self.PS[pq][:, 0:384], aqn[:, c, tsl], wqb[:, c, :], start=(c == 0), stop=(c == 1), rd=["m_aqn", "m_wqb"], wr=[("ps", pq)])
                    psq = self.PS[pq][:, 0:384].rearrange("p (h d) -> p h d", h=4)
                    self.act(sqq[:], psq, AF.Square, rd=[("ps", pq)], wr=["m_sqq"])
                    self.v("dve", "tensor_reduce", ["m_sqq"], ["m_ssq"], out=ssq[:, 0:4], in_=sqq[:, :, 0:64], axis=AX.X, op=ALU.add)
                    self.v("dve", "tensor_reduce", ["m_sqq"], ["m_ssq"], out=ssq[:, 4:8], in_=sqq[:, :, 64:96], axis=AX.X, op=ALU.add)
                    self.v("dve", "tensor_scalar", ["m_ssq"], ["m_ssq"], out=ssq[:, 0:4], in0=ssq[:, 0:4], scalar1=1.0 / 64, scalar2=None, op0=ALU.mult)
                    self.v("dve", "tensor_scalar", ["m_ssq"], ["m_ssq"], out=ssq[:, 4:8], in0=ssq[:, 4:8], scalar1=1.0 / 32, scalar2=None, op0=ALU.mult)
                    self.ln_exp(ssq[:, 0:8], ssq[:, 0:8], self.eps_t[:, 1:2], -0.5, ["m_ssq", "eps_t"], ["m_ssq"], "m_lnq", lnq[:, 0:8])
                    self.v("dve", "tensor_tensor", [("ps", pq), "m_ssq"], ["m_qn"], out=qn[:, :, 0:64], in0=psq[:, :, 0:64], in1=ssq[:, 0:4].unsqueeze(2).broadcast_to([128, 4, 64]), op=ALU.mult)
                    self.v("dve", "tensor_tensor", ["m_qn", "rep"], ["m_qn"], out=qn[:, :, 0:64], in0=qn[:, :, 0:64], in1=QG[:, 0:64].unsqueeze(1).broadcast_to([128, 4, 64]), op=ALU.mult)
                    self.v("dve", "tensor_tensor", [("ps", pq), "m_ssq"], ["m_qr"], out=qr[:], in0=psq[:, :, 64:96], in1=ssq[:, 4:8].unsqueeze(2).broadcast_to([128, 4, 32]), op=ALU.mult)
                    self.v("dve", "tensor_tensor", ["m_qr", "rep"], ["m_qr"], out=qr[:], in0=qr[:], in1=QG[:, 64:96].unsqueeze(1).broadcast_to([128, 4, 32]), op=ALU.mult)
                    if gi >= 2:
                        self.rope(qr[:], 4, 8, tab, gi, t1[:], t2[:], ["m_qr"], "mq")
                    self.v("dve", "tensor_copy", ["m_qr", "m_qn"], ["m_qn"], out=qn[:, :, 64:96], in_=qr[:])
                    self.v("dve", "tensor_copy", ["m_qn"], ["m_qnb"], out=qnb[:], in_=qn[:])
                    pt = 2
                    for h in range(4):
                        self.mm(self.PS[pt][0:96, h * 128:(h + 1) * 128], qnb[:, h, :], self.I_b, rd=["m_qnb", "cstb"], wr=[("ps", pt)])
                    self.act(qT[0:96, :, gi * 128:(gi + 1) * 128], self.PS[pt][0:96, 0:512].rearrange("p (h t) -> p h t", h=4), AF.Identity,
                             rd=[("ps", pt)], wr=[("m_qT", gi)])

                def k_body(gi=gi, tsl=tsl):
                    pk = 1
                    self.mm(self.PS[pk][:, 0:512], akvn[:, tsl], wkvb[:], rd=["m_akvn", "m_wkvb"], wr=[("ps", pk)])
                    psk = self.PS[pk][:, 0:512].rearrange("p (h d) -> p h d", h=4)
                    self.act(sqk[:], psk[:, :, 0:64], AF.Square, rd=[("ps", pk)], wr=["m_sqk"])
                    self.v("dve", "tensor_reduce", ["m_sqk"], ["m_ssk"], out=ssk[:, 0:4], in_=sqk[:], axis=AX.X, op=ALU.add)
                    self.v("dve", "tensor_copy", [("ps", pk), "m_vaug"], [("m_vaug", gi)], out=vaug[:, gi, :, 0:64], in_=psk[:, :, 64:128])
                    self.v("dve", "tensor_scalar", ["m_ssk"], ["m_ssk"], out=ssk[:, 0:4], in0=ssk[:, 0:4], scalar1=1.0 / 64, scalar2=None, op0=ALU.mult)
                    self.ln_exp(ssk[:, 0:4], ssk[:, 0:4], self.eps_t[:, 1:2], -0.5, ["m_ssk", "eps_t"], ["m_ssk"], "m_lnk", lnk[:, 0:4])
                    self.v("dve", "tensor_tensor", [("ps", pk), "m_ssk"], ["m_kn"], out=kn[:, :, 0:64], in0=psk[:, :, 0:64], in1=ssk[:, 0:4].unsqueeze(2).broadcast_to([128, 4, 64]), op=ALU.mult)
                    self.v("dve", "tensor_tensor", ["m_kn", "rep"], ["m_kn"], out=kn[:, :, 0:64], in0=kn[:, :, 0:64], in1=KG[:, 0:64].unsqueeze(1).broadcast_to([128, 4, 64]), op=ALU.mult)
                    pr = 3
                    self.proj_tm(384, 32, gi, pr)
                    self.act(sqr[:], self.PS[pr][:, 0:32], AF.Square, rd=[("ps", pr)], wr=["m_sqr"])
                    self.v("dve", "tensor_reduce", ["m_sqr"], [("m_ssr", 0)], out=ssr[:, 0:1], in_=sqr[:], axis=AX.X, op=ALU.add)
                    self.v("dve", "tensor_scalar", [("m_ssr", 0)], [("m_ssr", 0)], out=ssr[:, 0:1], in0=ssr[:, 0:1], scalar1=1.0 / 32, scalar2=None, op0=ALU.mult)
                    self.ln_exp(ssr[:, 2:3], ssr[:, 0:1], self.eps_t[:, 1:2], -0.5, [("m_ssr", 0), "eps_t"], [("m_ssr", 2)], ("m_ssr", 1), ssr[:, 1:2])
                    self.v("dve", "scalar_tensor_tensor", [("ps", pr), ("m_ssr", 2), "rep"], ["m_kr"], out=kr[:, 0, :], in0=self.PS[pr][:, 0:32], scalar=ssr[:, 2:3],
                           in1=KG[:, 64:96], op0=ALU.mult, op1=ALU.mult)
                    if gi >= 2:
                        self.rope(kr[:], 1, 8, tab, gi, t1k[:], t2k[:], ["m_kr"], "mk")
                    self.v("dve", "tensor_copy", ["m_kr", "m_kn"], ["m_kn"], out=kn[:, :, 64:96], in_=kr[:].broadcast_to([128, 4, 32]))
                    self.v("dve", "tensor_copy", ["m_kn"], ["m_knb"], out=knb[:], in_=kn[:])
                    pt = 3
                    for h in range(4):
                        self.mm(self.PS[pt][0:96, h * 128:(h + 1) * 128], knb[:, h, :], self.I_b, rd=["m_knb", "cstb"], wr=[("ps", pt)])
                    self.act(kT[0:96, :, gi * 128:(gi + 1) * 128], self.PS[pt][0:96, 0:512].rearrange("p (h t) -> p h t", h=4), AF.Identity,
                             rd=[("ps", pt)], wr=[("m_kT", gi)])

                self.P.interleave([q_body, k_body])
        self.release(m2)
        PT = [self.A(f"m_PT{i}", [128, 512], BF16) for i in range(3)]
        otok = self.A("m_otok", [128, 4, 256], BF16)
        rs = self.A("m_rs", [128, 4], F32)
        mixT = self.A("m_mixT", [128, 2, 512], BF16)
        scale = 96.0 ** -0.5
        for sti in range(5):
            if sti == 0 and not need_ctx:
                continue
            t0, n = ST[sti]
            nq = n // 128
            kts = [0, 1] if sti == 0 else list(range(NT))
            items = [(h, ki, kt) for h in range(4) for ki, kt in enumerate(kts)]

            def s_stage(it):
                h, ki, kt = it
                pa = self.rot("m_s", [0, 1])
                self.mm(self.PS[pa][:, 0:n], kT[0:96, h, kt * 128:(kt + 1) * 128], qT[0:96, h, t0:t0 + n], rd=["m_kT", "m_qT"], wr=[("ps", pa)])
                return pa

            pa_next = s_stage(items[0])
            for idx, (h, ki, kt) in enumerate(items):
                pa = pa_next
                if idx + 1 < len(items):
                    pa_next = s_stage(items[idx + 1])
                pt = self.rot("m_pt", [0, 1, 2])
                self.act(PT[pt][:, 0:n], self.PS[pa][:, 0:n], AF.Exp, rd=[("ps", pa)], wr=[("m_PT", pt)], scale=scale)
                for q in range(nq):
                    self.mm(self.PS[2 + q][:, 0:65], PT[pt][:, q * 128:(q + 1) * 128], vaug[:, kt, h, 0:65], start=(ki == 0), stop=(ki == len(kts) - 1),
                            rd=[("m_PT", pt), "m_vaug"], wr=[("ps", 2 + q)])
                if ki == len(kts) - 1:
                    for q in range(nq):
                        self.v("dve", "reciprocal", [("ps", 2 + q)], [("m_rs", q)], out=rs[:, q:q + 1], in_=self.PS[2 + q][:, 64:65])
                        self.act(otok[:, q, h * 64:(h + 1) * 64], self.PS[2 + q][:, 0:64], AF.Identity, rd=[("ps", 2 + q), ("m_rs", q)], wr=[("m_otok", q)], scale=rs[:, q:q + 1])
            for q in range(nq):
                self.tok_to_mixT(otok[:, q, :], ("m_otok", q), mixT, "m_mixT", q, s, 1, t0 // 128 + q)
            self.out_proj(l, s, sti, mixT, "m_mixT", n)
        self.release(m)

    def build(self):
        cfg = self.cfg
        self.prologue()
        phases = cfg.get("phases", ["ffn1", "a", "m", "w", "s", "ffn2"])
        for s in range(self.n_seq):
            self.load_seq(s)
            for l in range(self.n_layers):
                need_ctx = (l == 0)
                self.dma("sp", self.rep[:], self.rep_d[:, l, :], [], ["rep"], "rep")
                if "ffn1" in phases:
                    self.ffn(l, 0, s, False)
                if any(p in phases for p in "amws"):
                    mph = self.mark()
                    self.mixer_phase_begin(l, s)
                    if "a" in phases:
                        self.mixer_mlstm(l, s, need_ctx)
                    if "m" in phases:
                        self.mixer_mla(l, s, need_ctx)
                    if "w" in phases:
                        self.mixer_swa(l, s, need_ctx)
                    if "s" in phases:
                        self.mixer_ssd(l, s, need_ctx)
                    self.release(mph)
                if "ffn2" in phases:
                    self.ffn(l, 2, s, not need_ctx)
                self.P.barrier()
            self.store_seq(s)
        self.stats = self.P.emit()
        return self.nc


def make_inputs_for_core(inp, b0, n_seq, consts):
    cst, ropeM, ropeS = consts
    c_rows = np.concatenate([np.asarray(inp["c"], np.float32)[b0:b0 + n_seq], np.zeros((2 - n_seq, D), np.float32),
                             np.asarray(inp["c_ctx"], np.float32)[None]], 0)
    m = {
        "x": np.ascontiguousarray(inp["x"][b0:b0 + n_seq]), "ctx": np.ascontiguousarray(inp["ctx"][b0:b0 + n_seq]),
        "cst": cst, "ropeM": ropeM, "ropeS": ropeS,
    }
    m["cT"] = np.ascontiguousarray(np.moveaxis(_fm(c_rows), 1, 2))
    return m


_SHARED = ("w_mod", "ffn1_wi", "ffn2_wi", "ffn1_wo", "ffn2_wo", "w_in", "w_out", "mla_wq_b", "mla_wkv_b")


def run(inputs, cfg, core_batches):
    consts = _const_tables()
    pp = _pack_pp(inputs); rep = _pack_rep(inputs)
    shared = {k: np.ascontiguousarray(np.asarray(inputs[k], np.float32)) for k in _SHARED}
    shared["pp"] = pp; shared["rep"] = rep
    b = Builder(cfg)
    nc = b.build()
    in_maps = []
    for b0 in core_batches:
        mm_ = make_inputs_for_core(inputs, b0, b.n_seq, consts)
        mm_.update(shared)
        in_maps.append(mm_)
    res = run_bass_kernel_spmd(nc, in_maps, core_ids=list(range(len(core_batches))))
    return res, b


def kernel(**inputs):
    cfg = dict(n_seq=2, n_layers=2)
    res, b = run(inputs, cfg, [2 * i for i in range(8)])
    out = np.concatenate([np.asarray(r["out"], np.float32) for r in res.results], axis=0)
    return out
```

```python
import math
import numpy as np
import concourse.bass as bass
import concourse.mybir as mybir
from concourse.bass_utils import run_bass_kernel_spmd

F32 = mybir.dt.float32
BF16 = mybir.dt.bfloat16
AF = mybir.ActivationFunctionType
ALU = mybir.AluOpType
AX = mybir.AxisListType

D = 1024; KC = 8; T = 2304; TCX = 256; TL = 2048; NT = 18; FF = 2816; FC = 22
ST = [(0, 256), (256, 512), (768, 512), (1280, 512), (1792, 512)]
EPS = 1e-6
NEG = -30000.0
SEM_LIMIT = 30000
NPP = 246
NREP = 872
R_GATEB, R_ONORM, R_QG, R_KG, R_SQG, R_SKG, R_SINK, R_DTB, R_ALOG, R_SSDD, R_SNORM = 0, 16, 272, 368, 464, 528, 592, 596, 604, 612, 616
P_BMOD, P_NORM, P_QN, P_KVN, P_CW, P_CB = 0, 144, 192, 196, 198, 238
SEG_A, SEG_M, SEG_W, SEG_S = 0, 1040, 1456, 1968


class _Op:
    __slots__ = ("eng", "fn", "deps", "need_sig", "sig", "is_dma", "semkey", "idx", "vc")

    def __init__(self, eng, fn, is_dma=False, semkey=None):
        self.eng = eng; self.fn = fn; self.deps = []; self.need_sig = False; self.sig = None
        self.is_dma = is_dma; self.semkey = semkey


class Prog:
    ENGS = ("pe", "act", "dve", "pool", "sp")

    def __init__(self, nc):
        self.nc = nc
        self.ops = []
        self.state = {}
        self.final_dmas = []
        self.last_op = {}
        self.dmas_open = []
        self.pending_barrier = {}
        self.capture = None

    def _eng(self, name):
        nc = self.nc
        return {"pe": nc.tensor, "act": nc.scalar, "dve": nc.vector, "pool": nc.gpsimd, "sp": nc.sync}[name]

    @staticmethod
    def _norm(k):
        if not isinstance(k, tuple):
            return (k, None)
        if len(k) == 2:
            return k
        return (k[0], tuple(k[1:]))

    def _cells(self, key, create):
        name, sub = key
        st = self.state.setdefault(name, {})
        if sub is None:
            if create and None not in st:
                st[None] = [None, []]
            return list(st.values())
        cells = []
        if None in st:
            cells.append(st[None])
        if sub not in st and create:
            st[sub] = [None, []]
        if sub in st:
            cells.append(st[sub])
        return cells

    def _add(self, op, rd, wr):
        rd = [self._norm(k) for k in rd]
        wr = [self._norm(k) for k in wr]
        deps = {}

        def conflict(o):
            return o.is_dma or op.is_dma or o.eng != op.eng or op.eng != "pe"

        for k in rd:
            for cell in self._cells(k, True):
                w = cell[0]
                if w is not None and w is not op:
                    deps[id(w)] = w
                if k[0] == "ps":
                    for r in cell[1]:
                        if r is not op and r.eng != op.eng:
                            deps[id(r)] = r
        for k in wr:
            for cell in self._cells(k, True):
                w = cell[0]
                if w is not None and w is not op and conflict(w):
                    deps[id(w)] = w
                for r in cell[1]:
                    if r is not op and conflict(r):
                        deps[id(r)] = r
        if op.eng in self.pending_barrier:
            for o in self.pending_barrier.pop(op.eng):
                deps[id(o)] = o
        op.deps = list(deps.values())
        for d_ in op.deps:
            d_.need_sig = True
        for k in wr:
            name, sub = k
            st = self.state[name]
            if sub is None:
                for s_ in list(st.keys()):
                    if s_ is not None:
                        del st[s_]
                st[None] = [op, []]
            else:
                st[sub] = [op, []]
        for k in rd:
            if k in wr:
                continue
            name, sub = k
            st = self.state[name]
            if sub is None:
                st[None][1].append(op)
                for s_, cell in st.items():
                    if s_ is not None:
                        cell[1].append(op)
            else:
                st[sub][1].append(op)
        op.idx = len(self.ops)
        self.ops.append(op)
        if op.is_dma:
            self.dmas_open.append(op)
        else:
            self.last_op[op.eng] = op
        return op

    def op(self, eng, fn, rd=(), wr=()):
        if self.capture is not None:
            self.capture.append(("op", eng, fn, tuple(rd), tuple(wr)))
            return None
        return self._add(_Op(eng, fn), tuple(rd), tuple(wr))

    def interleave(self, bodies):
        lists = []
        for body in bodies:
            self.capture = []
            body()
            lists.append(self.capture)
            self.capture = None
        idx = [0] * len(lists)
        while any(idx[i] < len(lists[i]) for i in range(len(lists))):
            for i, lst in enumerate(lists):
                if idx[i] < len(lst):
                    it = lst[idx[i]]; idx[i] += 1
                    if it[0] == "op":
                        self.op(*it[1:])
                    else:
                        self.dma(*it[1:])

    def dma(self, eng, fn, rd=(), wr=(), semkey=None, final=False):
        if self.capture is not None:
            self.capture.append(("dma", eng, fn, tuple(rd), tuple(wr), semkey, final))
            return None
        o = _Op(eng, fn, is_dma=True, semkey=semkey)
        o.need_sig = True
        self._add(o, tuple(rd), tuple(wr))
        if final:
            self.final_dmas.append(o)
        return o

    def barrier(self):
        lasts = list(self.last_op.values()) + list(self.dmas_open)
        self.dmas_open = []
        for e in self.ENGS:
            self.pending_barrier[e] = list(self.pending_barrier.get(e, [])) + lasts
        self.state = {}

    def emit(self):
        nc = self.nc
        sem_count = [0]

        def new_sem(nm):
            sem_count[0] += 1
            return nc.alloc_semaphore(f"s_{nm}_{sem_count[0]}")

        eng_sem = {}; eng_cnt = {}; dma_sem = {}; dma_cnt = {}
        for o in self.ops:
            if o.is_dma:
                k = o.semkey
                if k not in dma_sem or dma_cnt[k] + 16 > SEM_LIMIT:
                    dma_sem[k] = new_sem("d"); dma_cnt[k] = 0
                dma_cnt[k] += 16
                o.sig = (dma_sem[k], dma_cnt[k])
            elif o.need_sig:
                e = o.eng
                if e not in eng_sem or eng_cnt[e] + 1 > SEM_LIMIT:
                    eng_sem[e] = new_sem(e); eng_cnt[e] = 0
                eng_cnt[e] += 1
                o.sig = (eng_sem[e], eng_cnt[e])
        waited = {e: {} for e in self.ENGS}
        nwait = 0
        sem_by_id = {}
        for o in self.ops:
            eng = self._eng(o.eng)
            wt = waited[o.eng]
            for d_ in sorted(o.deps, key=lambda x: -x.idx):
                sem, val = d_.sig
                key = id(sem)
                if wt.get(key, 0) >= val:
                    continue
                eng.wait_ge(sem, val)
                nwait += 1
                wt[key] = val
                for k2, v2 in d_.vc.items():
                    if wt.get(k2, 0) < v2:
                        wt[k2] = v2
            ins = o.fn(eng)
            if o.sig is not None:
                ins.then_inc(o.sig[0], 16 if o.is_dma else 1)
            if o.need_sig:
                vc = dict(wt)
                if o.is_dma:
                    pass
                else:
                    vc[id(o.sig[0])] = max(vc.get(id(o.sig[0]), 0), o.sig[1])
                o.vc = vc
        for o in self.final_dmas:
            sem, val = o.sig
            if waited["sp"].get(id(sem), 0) < val:
                waited["sp"][id(sem)] = val
                nc.sync.wait_ge(sem, val)
        self.stats = dict(n_ops=len(self.ops), n_wait=nwait, n_sem=sem_count[0], eng_cnt=dict(eng_cnt), dma_max=max(dma_cnt.values()) if dma_cnt else 0)
        return self.stats


def _const_tables():
    r = np.arange(128)[:, None]; c = np.arange(128)[None, :]
    cst = np.zeros((128, 8, 128), np.float32)
    cst[:, 0] = (r == c)
    cst[:, 1] = 1.0
    cst[:, 2] = (r <= c)
    cst[:, 3] = (r >= c)
    cst[:, 4] = (r > c)
    cst[:, 5] = (r < c)
    cst[:, 6] = NEG * (r > c)
    cst[:, 7] = NEG * (r < c)

    def rope(rot):
        t = np.arange(TL)
        row = (t // 64).astype(np.float32); col = (t % 64).astype(np.float32)
        nf = rot // 4
        inv = (10000.0 ** (-np.arange(nf, dtype=np.float32) / nf)).astype(np.float32)
        ar = row[:, None] * inv; ac = col[:, None] * inv
        ang = np.concatenate([ar, ar, ac, ac], -1).astype(np.float32)
        cos = np.cos(ang).astype(np.float32); sin = np.sin(ang).astype(np.float32)
        sgn = np.concatenate([-np.ones(nf), np.ones(nf), -np.ones(nf), np.ones(nf)]).astype(np.float32)
        tab = np.stack([cos, sin * sgn], 1)
        return np.ascontiguousarray(tab.reshape(16, 128, 2, rot).transpose(1, 0, 2, 3))
    return cst, rope(32), rope(64)


def _fm(v):
    v = np.asarray(v, np.float32)
    n = v.shape[-1] // 128
    return np.moveaxis(v.reshape(v.shape[:-1] + (n, 128)), -1, 0)


def _pack_pp(inp):
    pp = np.zeros((128, NPP), np.float32)
    pp[:, P_BMOD:P_BMOD + 144] = _fm(inp["b_mod"].reshape(2, 9, 1024)).reshape(128, 144)
    norms = np.stack([inp["ffn1_norm"], inp["mix_norm"], inp["ffn2_norm"]], 0)
    pp[:, P_NORM:P_NORM + 48] = _fm(norms).reshape(128, 48)
    pp[:, P_QN:P_QN + 4] = _fm(inp["mla_q_norm"]).reshape(128, 4)
    pp[:, P_KVN:P_KVN + 2] = _fm(inp["mla_kv_norm"]).reshape(128, 2)
    cw = np.asarray(inp["ssd_conv_w"], np.float32)
    pp[:, P_CW:P_CW + 40] = np.moveaxis(_fm(cw), -2, -1).reshape(128, 40)
    pp[:, P_CB:P_CB + 8] = _fm(inp["ssd_conv_b"]).reshape(128, 8)
    return pp


def _pack_rep(inp):
    rows = []
    for l in range(2):
        rows.append(np.concatenate([np.asarray(inp[k], np.float32)[l].reshape(-1) for k in (
            "mlstm_gate_b", "mlstm_out_norm", "mla_q_gain", "mla_k_gain", "swa_q_gain", "swa_k_gain", "swa_sink",
            "ssd_dt_bias", "ssd_a_log", "ssd_d", "ssd_norm")]))
    rep = np.stack(rows, 0)
    assert rep.shape == (2, NREP)
    return np.ascontiguousarray(np.broadcast_to(rep[None], (128, 2, NREP))).astype(np.float32)


class Builder:
    def __init__(self, cfg):
        self.cfg = cfg
        self.n_seq = cfg.get("n_seq", 2)
        self.n_layers = cfg.get("n_layers", 2)
        self.dbg = cfg.get("dbg", False)
        nc = self.nc = bass.Bass("TRN2", target_bir_lowering=False)
        self.P = Prog(nc)
        self.uid = 0
        ns = self.n_seq

        def din(name, shape):
            return nc.dram_tensor(name, list(shape), F32, kind="ExternalInput").ap()
        self.x = din("x", [ns, TL, D]); self.ctx = din("ctx", [ns, TCX, D])
        self.cT = din("cT", [128, 8, 3])
        self.w_mod = din("w_mod", [2, D, 9 * D])
        self.wi = [din("ffn1_wi", [2, D, 2 * FF]), din("ffn2_wi", [2, D, 2 * FF])]
        self.wo = [din("ffn1_wo", [2, FF, D]), din("ffn2_wo", [2, FF, D])]
        self.w_in = din("w_in", [2, D, 2744]); self.w_out = din("w_out", [2, D, D])
        self.wq_b = din("mla_wq_b", [2, 256, 384]); self.wkv_b = din("mla_wkv_b", [2, 128, 512])
        self.pp_d = din("pp", [128, NPP]); self.rep_d = din("rep", [128, 2, NREP])
        self.cst_d = din("cst", [128, 8, 128]); self.ropeM_d = din("ropeM", [128, 16, 2, 32]); self.ropeS_d = din("ropeS", [128, 16, 2, 64])
        self.out = nc.dram_tensor("out", [ns, TL, D], F32, kind="ExternalOutput").ap()
        if self.dbg:
            self.dbg_out = nc.dram_tensor("dbg", [ns, 4, T, 256], BF16, kind="ExternalOutput").ap()
        self.hT = nc.alloc_sbuf_tensor("sb_hT", [128, KC, T], F32)
        self.cst = nc.alloc_sbuf_tensor("sb_cst", [128, 8, 128], F32)
        self.cstb = nc.alloc_sbuf_tensor("sb_cstb", [128, 8, 128], BF16)
        self.pp = nc.alloc_sbuf_tensor("sb_pp", [128, NPP], F32)
        self.rep = nc.alloc_sbuf_tensor("sb_rep", [128, NREP], F32)
        self.modT = nc.alloc_sbuf_tensor("sb_modT", [128, 2, 9, KC, 3], F32)
        self.DER = nc.alloc_sbuf_tensor("sb_DER", [128, 2, 3, 3, KC, 3], F32)
        self.eps_t = nc.alloc_sbuf_tensor("sb_eps_t", [128, 4], F32)
        rem = nc.sbuf_bytes_remaining
        self.arena_size = (rem - 64) // 32 * 32
        self.arena_t = nc.alloc_sbuf_tensor("arena", [128, self.arena_size // 4], F32)
        self.arena_base = nc.lookup_mloc(self.arena_t).addr
        self.acur = 0
        self.PS = [nc.alloc_psum_tensor(f"psb{i}", [128, 512], F32) for i in range(8)]
        self.PSB = [p.bitcast(BF16) for p in self.PS]
        self.rotc = {}
        self.last_rg = {}
        c = self.cst
        self.I_f, self.ONES_f, self.TRI_F, self.TRI_B, self.U_F, self.U_B, self.NEG_F, self.NEG_B = [c[:, i, :] for i in range(8)]
        cb = self.cstb
        self.I_b, self.ONES_b, self.TRI_Fb, self.TRI_Bb, self.U_Fb, self.U_Bb, self.NEG_Fb, self.NEG_Bb = [cb[:, i, :] for i in range(8)]

    def A(self, name, shape, dtype):
        sz = (2 if dtype == BF16 else 4) * int(np.prod(shape[1:]))
        off = (self.acur + 31) // 32 * 32
        assert off + sz <= self.arena_size, f"arena overflow {name}: {off + sz} > {self.arena_size}"
        self.acur = off + sz
        self.uid += 1
        return self.nc.alloc_sbuf_tensor_at(f"{name}_{self.uid}", list(shape), dtype, offset=self.arena_base + off)

    def mark(self):
        return self.acur

    def release(self, m):
        self.P.barrier()
        self.acur = m

    def rot(self, name, banks):
        i = self.rotc.get(name, 0)
        self.rotc[name] = i + 1
        return banks[i % len(banks)]

    def mm(self, out, lhsT, rhs, start=True, stop=True, rd=(), wr=()):
        rd = list(rd); wr = list(wr)
        bank = [k[1] for k in wr if isinstance(k, tuple) and k[0] == "ps"][0]
        rg = (lhsT.base_partition(), lhsT.partition_size())
        prev = self.last_rg.get(bank)
        if prev is not None and (prev[0] + prev[1] <= rg[0] or rg[0] + rg[1] <= prev[0]):
            rd.append(("pe_rg", bank))
        wr.append(("pe_rg", bank))
        self.last_rg[bank] = rg
        return self.P.op("pe", lambda e: e.matmul(out, lhsT=lhsT, rhs=rhs, start=start, stop=stop), rd, wr)

    def tr(self, out, in_, ident, rd=(), wr=()):
        return self.P.op("pe", lambda e: e.transpose(out, in_, ident), rd, wr)

    def act(self, out, in_, func, rd=(), wr=(), bias=None, scale=None):
        kw = {}
        if bias is not None:
            kw["bias"] = bias
        if scale is not None:
            kw["scale"] = scale
        return self.P.op("act", lambda e: e.activation(out=out, in_=in_, func=func, **kw), rd, wr)

    def v(self, eng, meth, rd, wr, **kw):
        return self.P.op(eng, lambda e: getattr(e, meth)(**kw), rd, wr)

    def dma(self, eng, out, in_, rd, wr, semkey, final=False):
        return self.P.dma(eng, lambda e: e.dma_start(out=out, in_=in_), rd, wr, semkey, final)

    def ln_exp(self, out, in_, bias_ap, scale, rd, wr, tmpkey, tmp):
        self.act(tmp, in_, AF.Ln, rd=rd, wr=[tmpkey], bias=bias_ap)
        self.act(out, tmp, AF.Exp, rd=[tmpkey], wr=wr, scale=scale)

    def prologue(self):
        P = self.P
        self.dma("sp", self.cst[:], self.cst_d, [], ["cst"], "cst")
        self.dma("pool", self.cstb[:], self.cst_d, [], ["cstb"], "cstb0")
        self.dma("sp", self.pp[:], self.pp_d, [], ["pp"], "pp")
        self.v("dve", "memset", [], ["eps_t"], ap=self.eps_t[:, 0:1], constant=float(D * EPS))
        self.v("dve", "memset", [], ["eps_t"], ap=self.eps_t[:, 1:2], constant=float(EPS))
        self.v("dve", "memset", [], ["eps_t"], ap=self.eps_t[:, 2:3], constant=1.0)
        self.v("dve", "memset", [], ["eps_t"], ap=self.eps_t[:, 3:4], constant=0.0)
        m = self.mark()
        scT = self.A("scT", [128, 8, 3], F32)
        cTs = self.A("cTs", [128, 8, 3], F32)
        self.dma("sp", cTs[:], self.cT, [], ["cTs"], "cTs")
        self.act(scT[:], cTs[:], AF.Silu, rd=["cTs"], wr=["scT"])
        wmb = [self.A(f"wmb{i}", [128, 8, 1024], F32) for i in range(2)]
        for l in range(self.n_layers):
            for i in range(9):
                sl = (l * 9 + i) % 2
                src = self.w_mod[l].rearrange("(kc p) c -> p kc c", p=128)[:, :, i * 1024:(i + 1) * 1024]
                self.dma("sp", wmb[sl][:, 0:4, :], src[:, 0:4, :], [], [(f"wmb{sl}", 0)], ("wmb", sl, 0))
                self.dma("sp", wmb[sl][:, 4:8, :], src[:, 4:8, :], [], [(f"wmb{sl}", 1)], ("wmb", sl, 1))
                pb = self.rot("mod", [0, 1])
                ps = self.PS[pb]
                for kc in range(8):
                    for k2 in range(8):
                        self.mm(ps[:, kc * 3:(kc + 1) * 3], wmb[sl][:, k2, kc * 128:(kc + 1) * 128], scT[:, k2, :],
                                start=(k2 == 0), stop=(k2 == 7), rd=[f"wmb{sl}", "scT"], wr=[("ps", pb)])
                bm = self.pp[:, P_BMOD + (l * 9 + i) * 8: P_BMOD + (l * 9 + i + 1) * 8].unsqueeze(2).broadcast_to([128, 8, 3])
                self.v("dve", "tensor_tensor", [("ps", pb), "pp"], [("modT", l)],
                       out=self.modT[:, l, i, :, :], in0=ps[:, 0:24].rearrange("p (k r) -> p k r", r=3), in1=bm, op=ALU.add)
            for w_, (i_sh, i_sc, i_g, gmul) in enumerate([(0, 1, 2, 0.5), (3, 4, 5, 1.0), (6, 7, 8, 0.5)]):
                nrm = self.pp[:, P_NORM + (w_ * 2 + l) * 8: P_NORM + (w_ * 2 + l + 1) * 8].unsqueeze(2).broadcast_to([128, 8, 3])
                self.v("dve", "scalar_tensor_tensor", [("modT", l), "pp"], [("DER", l)],
                       out=self.DER[:, l, w_, 0, :, :], in0=self.modT[:, l, i_sc, :, :], scalar=1.0, in1=nrm, op0=ALU.add, op1=ALU.mult)
                self.v("dve", "tensor_scalar", [("DER", l)], [("DER", l)],
                       out=self.DER[:, l, w_, 0, :, :], in0=self.DER[:, l, w_, 0, :, :], scalar1=32.0, scalar2=None, op0=ALU.mult)
                self.v("dve", "tensor_copy", [("modT", l)], [("DER", l)], out=self.DER[:, l, w_, 1, :, :], in_=self.modT[:, l, i_sh, :, :])
                self.v("dve", "tensor_scalar", [("modT", l)], [("DER", l)],
                       out=self.DER[:, l, w_, 2, :, :], in0=self.modT[:, l, i_g, :, :], scalar1=float(gmul), scalar2=None, op0=ALU.mult)
        self.release(m)

    def load_seq(self, s):
        m = self.mark()
        xin = [self.A(f"xin{i}", [128, D], F32) for i in range(2)]
        for i in range(NT):
            sl = i % 2
            src = self.ctx[s, i * 128:(i + 1) * 128, :] if i < 2 else self.x[s, (i - 2) * 128:(i - 1) * 128, :]
            self.dma("sp", xin[sl][:], src, [], [("xin", sl)], ("xin", sl))
            for half in range(2):
                pb = self.rot("ld", [0, 1, 2, 3])
                for q in range(4):
                    kc = half * 4 + q
                    self.tr(self.PS[pb][:, q * 128:(q + 1) * 128], xin[sl][:, kc * 128:(kc + 1) * 128], self.I_f,
                            rd=[("xin", sl), "cst"], wr=[("ps", pb)])
                eng = "act" if half == 0 else "dve"
                dst = self.hT[:, half * 4:(half + 1) * 4, i * 128:(i + 1) * 128]
                srcp = self.PS[pb][:, :].rearrange("p (q t) -> p q t", q=4)
                if eng == "act":
                    self.act(dst, srcp, AF.Identity, rd=[("ps", pb)], wr=[("hT", i)])
                else:
                    self.v("dve", "tensor_copy", [("ps", pb)], [("hT", i)], out=dst, in_=srcp)
        self.release(m)

    def store_seq(self, s):
        m = self.mark()
        ob = [self.A(f"ob{i}", [128, D], F32) for i in range(2)]
        for i in range(2, NT):
            sl = i % 2
            for half in range(2):
                pb = self.rot("ld", [0, 1, 2, 3])
                for q in range(4):
                    kc = half * 4 + q
                    self.tr(self.PS[pb][:, q * 128:(q + 1) * 128], self.hT[:, kc, i * 128:(i + 1) * 128], self.I_f,
                            rd=[("hT", i), "cst"], wr=[("ps", pb)])
                dst = ob[sl][:, half * 512:(half + 1) * 512]
                if half == 0:
                    self.act(dst, self.PS[pb][:, :], AF.Identity, rd=[("ps", pb)], wr=[("ob", sl)])
                else:
                    self.v("dve", "tensor_copy", [("ps", pb)], [("ob", sl)], out=dst, in_=self.PS[pb][:, :])
            self.dma("sp", self.out[s, (i - 2) * 128:(i - 1) * 128, :], ob[sl][:], [("ob", sl)], [], ("ob", sl), final=True)
        self.release(m)

    def alloc_norm_tmps(self):
        self.n_sq = self.A("n_sq", [128, KC, 512], BF16)
        self.n_rstd = self.A("n_rstd", [128, 512], F32)
        self.n_ln = self.A("n_ln", [128, 512], F32)
        self.n_tmp = [self.A(f"n_tmp{i}", [128, 512], F32) for i in range(2)]

    def norm_mod(self, l, which, sti, s, xn, xnkey):
        t0, n = ST[sti]
        r = 2 if sti == 0 else s
        tiles = [("hT", t0 // 128 + j) for j in range(n // 128)]
        self.act(self.n_sq[:, :, 0:n], self.hT[:, :, t0:t0 + n], AF.Square, rd=tiles, wr=["n_sq"])
        ps = self.PS[7]
        for kc in range(KC):
            self.mm(ps[:, 0:n], self.ONES_b, self.n_sq[:, kc, 0:n], start=(kc == 0), stop=(kc == KC - 1),
                    rd=["n_sq", "cstb"], wr=[("ps", 7)])
        self.ln_exp(self.n_rstd[:, 0:n], ps[:, 0:n], self.eps_t[:, 0:1], -0.5, [("ps", 7), "eps_t"], ["n_rstd"], "n_ln", self.n_ln[:, 0:n])
        for kc in range(KC):
            tb = self.rot("ntmp", [0, 1])
            self.v("dve", "scalar_tensor_tensor", tiles + ["n_rstd", ("DER", l)], [("n_tmp", tb)],
                   out=self.n_tmp[tb][:, 0:n], in0=self.hT[:, kc, t0:t0 + n], scalar=self.DER[:, l, which, 0, kc, r:r + 1],
                   in1=self.n_rstd[:, 0:n], op0=ALU.mult, op1=ALU.mult)
            self.act(xn[:, kc, 0:n], self.n_tmp[tb][:, 0:n], AF.Identity, rd=[("n_tmp", tb), ("DER", l)], wr=[xnkey],
                     bias=self.DER[:, l, which, 1, kc, r:r + 1])

    def ffn(self, l, which, s, skip_ctx):
        fi = 0 if which == 0 else 1
        m = self.mark()
        self.alloc_norm_tmps()
        xnp = self.A("xnp", [128, KC, 1280], BF16)
        h1 = self.A("h1", [128, FC, 1280], BF16)
        wib = [self.A(f"wib{i}", [128, 2, KC, 128], BF16) for i in range(2)]
        wob = [self.A(f"wob{i}", [128, FC, 128], BF16) for i in range(2)]
        sg = [self.A(f"sg{i}", [128, 512], F32) for i in range(2)]
        wi_v = self.wi[fi][l].rearrange("(kc p) c -> p kc c", p=128)
        wo_v = self.wo[fi][l].rearrange("(jc p) c -> p jc c", p=128)
        passes = [[0, 1, 2], [3, 4]]
        if skip_ctx:
            passes = [[1, 2], [3, 4]]
        def pass_offs(pas):
            offs = {}
            o = 0
            for sti in pas:
                offs[sti] = o
                o += ST[sti][1]
            return offs

        def emit_norm(pas):
            offs = pass_offs(pas)
            for sti in pas:
                self.norm_mod(l, which, sti, s, xnp[:, :, offs[sti]:offs[sti] + ST[sti][1]], "xnp")

        emit_norm(passes[0])
        for pi, pas in enumerate(passes):
            offs = pass_offs(pas)
            for j in range(FC):
                sl = self.rot("wib", [0, 1])
                self.dma("pool", wib[sl][:, 0, :, :], wi_v[:, :, j * 128:(j + 1) * 128], [], [(f"wib{sl}", 0)], ("wib", sl, 0))
                self.dma("pool", wib[sl][:, 1, :, :], wi_v[:, :, FF + j * 128:FF + (j + 1) * 128], [], [(f"wib{sl}", 1)], ("wib", sl, 1))
                for sti in pas:
                    n = ST[sti][1]; o = offs[sti]
                    pg = self.rot("ffg", [0, 1]); pu = self.rot("ffu", [2, 3])
                    for gu, pb in ((0, pg), (1, pu)):
                        for kc in range(KC):
                            self.mm(self.PS[pb][:, 0:n], wib[sl][:, gu, kc, :], xnp[:, kc, o:o + n], start=(kc == 0), stop=(kc == KC - 1),
                                    rd=[(f"wib{sl}", gu), ("xnp", sti)], wr=[("ps", pb)])
                    sb = self.rot("sg", [0, 1])
                    self.act(sg[sb][:, 0:n], self.PS[pg][:, 0:n], AF.Silu, rd=[("ps", pg)], wr=[("sg", sb)])
                    self.v("dve", "tensor_tensor", [("sg", sb), ("ps", pu)], [("h1", j)],
                           out=h1[:, j, o:o + n], in0=sg[sb][:, 0:n], in1=self.PS[pu][:, 0:n], op=ALU.mult)
            if pi + 1 < len(passes):
                emit_norm(passes[pi + 1])
            for mo in range(KC):
                sl = self.rot("wob", [0, 1])
                self.dma("pool", wob[sl][:], wo_v[:, :, mo * 128:(mo + 1) * 128], [], [("wob", sl)], ("wob", sl))
                for sti in pas:
                    t0, n = ST[sti]; o = offs[sti]
                    r = 2 if sti == 0 else s
                    pb = self.rot("ffo", [4, 5])
                    for j in range(FC):
                        self.mm(self.PS[pb][:, 0:n], wob[sl][:, j, :], h1[:, j, o:o + n], start=(j == 0), stop=(j == FC - 1),
                                rd=[("wob", sl), ("h1", j)], wr=[("ps", pb)])
                    tiles = [("hT", t0 // 128 + q) for q in range(n // 128)]
                    self.v("dve", "scalar_tensor_tensor", [("ps", pb), ("DER", l)] + tiles, tiles,
                           out=self.hT[:, mo, t0:t0 + n], in0=self.PS[pb][:, 0:n], scalar=self.DER[:, l, which, 2, mo, r:r + 1],
                           in1=self.hT[:, mo, t0:t0 + n], op0=ALU.mult, op1=ALU.add)
        self.release(m)

    def mixer_phase_begin(self, l, s):
        self.xn2 = self.A("xn2", [128, KC, T], BF16)
        m = self.mark()
        self.alloc_norm_tmps()
        for sti in range(5):
            t0, n = ST[sti]
            self.norm_mod(l, 1, sti, s, self.xn2[:, :, t0:t0 + n], ("xn2", sti))
        self.release(m)

    def mixer_common_alloc(self, l, seg0, ncols, mixer_idx, pad=None, defer_wmix=False):
        self.woutm = self.A("woutm", [128, 2, D], BF16)
        srco = self.w_out[l].rearrange("(c p) n -> p c n", p=128)[:, mixer_idx * 2:(mixer_idx + 1) * 2, :]
        self.dma("pool", self.woutm[:], srco, [], ["woutm"], "woutm")
        if not defer_wmix:
            self.alloc_wmix(l, seg0, ncols, pad)

    def alloc_wmix(self, l, seg0, ncols, pad=None):
        self.wmix = self.A("wmix", [128, KC, pad or ncols], BF16)
        src = self.w_in[l].rearrange("(kc p) c -> p kc c", p=128)[:, :, seg0:seg0 + ncols]
        self.dma("pool", self.wmix[:, 0:4, 0:ncols], src[:, 0:4, :], [], [("wmix", 0)], "wmix0")
        self.dma("pool", self.wmix[:, 4:8, 0:ncols], src[:, 4:8, :], [], [("wmix", 1)], "wmix1")

    def load_gate_w(self, l, col0, name):
        wg = self.A(name, [128, KC, 256], BF16)
        src = self.w_in[l].rearrange("(kc p) c -> p kc c", p=128)[:, :, col0:col0 + 256]
        self.dma("pool", wg[:], src, [], [name], name)
        return wg

    def proj_fm(self, col0, m_rows, t0, n, pb):
        for kc in range(KC):
            self.mm(self.PS[pb][0:m_rows, 0:n], self.wmix[:, kc, col0:col0 + m_rows], self.xn2[:, kc, t0:t0 + n], start=(kc == 0), stop=(kc == KC - 1),
                    rd=["wmix", "xn2"], wr=[("ps", pb)])

    def proj_tm(self, col0, ncol, gi, pb, pcol0=0, w=None, wkey="wmix"):
        w = self.wmix if w is None else w
        for kc in range(KC):
            self.mm(self.PS[pb][:, pcol0:pcol0 + ncol], self.xn2[:, kc, gi * 128:(gi + 1) * 128], w[:, kc, col0:col0 + ncol],
                    start=(kc == 0), stop=(kc == KC - 1), rd=[wkey, "xn2"], wr=[("ps", pb)])

    def out_proj(self, l, s, sti, mixT, mixkey, n):
        t0, _ = ST[sti]
        r = 2 if sti == 0 else s
        tiles = [("hT", t0 // 128 + q) for q in range(n // 128)]
        for mo in range(KC):
            pb = self.rot("op", [6, 7])
            for c in range(2):
                self.mm(self.PS[pb][:, 0:n], self.woutm[:, c, mo * 128:(mo + 1) * 128], mixT[:, c, 0:n], start=(c == 0), stop=(c == 1),
                        rd=["woutm", mixkey], wr=[("ps", pb)])
            self.v("dve", "scalar_tensor_tensor", [("ps", pb), ("DER", l)] + tiles, tiles,
                   out=self.hT[:, mo, t0:t0 + n], in0=self.PS[pb][:, 0:n], scalar=self.DER[:, l, 1, 2, mo, r:r + 1],
                   in1=self.hT[:, mo, t0:t0 + n], op0=ALU.mult, op1=ALU.add)

    def tok_to_mixT(self, otok_ap, otkey, mixT, mixkey, q, s, mixer_idx, gtile, pb=6):
        if self.dbg:
            self.dma("sp", self.dbg_out[s, mixer_idx, gtile * 128:(gtile + 1) * 128, :], otok_ap, [otkey], [], ("dbg", mixer_idx, gtile % 4))
        for c in range(2):
            self.mm(self.PS[pb][:, c * 128:(c + 1) * 128], otok_ap[:, c * 128:(c + 1) * 128], self.I_b, rd=[otkey, "cstb"], wr=[("ps", pb)])
        self.act(mixT[:, :, q * 128:(q + 1) * 128], self.PS[pb][:, 0:256].rearrange("p (c t) -> p c t", c=2), AF.Identity,
                 rd=[("ps", pb)], wr=[(mixkey, q)])

    def linattn_alloc(self, Pv):
        self.la_sets = []
        for si in range(2):
            L = {"Pv": Pv, "id": si}
            L["A"] = self.A(f"laA{si}", [128, 4, 2, 128], BF16)
            L["E"] = self.A(f"laE{si}", [128, 4, 128], F32)
            L["G"] = self.A(f"laG{si}", [128, 4, 128], F32)
            L["W"] = [self.A(f"laW{si}_{i}", [128, 128], BF16) for i in range(4)]
            L["tc"] = self.A(f"latc{si}", [128, 4, Pv], F32)
            L["ydir"] = self.A(f"laydir{si}", [128, 4, Pv], F32)
            L["vs"] = self.A(f"lavs{si}", [128, 4, 80], BF16)
            L["S"] = self.A(f"laS{si}", [64, 4, Pv], F32)
            L["Sbf"] = self.A(f"laSbf{si}", [128, 4, 80], BF16)
            self.la_sets.append(L)
        self.la_gat = self.A("la_gat", [128, NT, 2, 8], F32)
        self.la_etot = self.A("la_etot", [128, NT, 2, 4], F32)
        self.la_ecum = self.A("la_ecum", [128, NT, 2, 4], F32)
        self.la_dec = self.A("la_dec", [128, NT, 2, 4], F32)

    def split_gates(self, a_all, akey, pfx):
        n = NT * 8
        sp = {}
        af = a_all[:].rearrange("p c k -> p (c k)")
        r1 = self.A(pfx + "_r1", [128, n], F32); r2 = self.A(pfx + "_r2", [128, n], F32)
        for nm in ("hi", "mid", "lo"):
            sp[nm + "_b"] = self.A(f"{pfx}_{nm}b", [128, NT, 8], BF16)
            if nm != "lo":
                sp[nm + "_f"] = self.A(f"{pfx}_{nm}f", [128, NT, 8], F32)
        fl = lambda t: t[:].rearrange("p c k -> p (c k)")
        kk = pfx + "_split"
        self.v("dve", "tensor_copy", [akey], [kk], out=fl(sp["hi_b"]), in_=af)
        self.v("dve", "tensor_copy", [kk], [kk], out=fl(sp["hi_f"]), in_=fl(sp["hi_b"]))
        self.v("dve", "tensor_tensor", [akey, kk], [kk], out=r1[:], in0=af, in1=fl(sp["hi_f"]), op=ALU.subtract)
        self.v("dve", "tensor_copy", [kk], [kk], out=fl(sp["mid_b"]), in_=r1[:])
        self.v("dve", "tensor_copy", [kk], [kk], out=fl(sp["mid_f"]), in_=fl(sp["mid_b"]))
        self.v("dve", "tensor_tensor", [kk], [kk], out=r2[:], in0=r1[:], in1=fl(sp["mid_f"]), op=ALU.subtract)
        self.v("dve", "tensor_copy", [kk], [kk], out=fl(sp["lo_b"]), in_=r2[:])
        hm = self.A(pfx + "_hm", [128, NT, 8, 2], F32)
        self.v("dve", "tensor_copy", [kk], [kk], out=hm[:, :, :, 0], in_=sp["hi_f"][:])
        self.v("dve", "tensor_copy", [kk], [kk], out=hm[:, :, :, 1], in_=sp["mid_f"][:])
        sp["hm"] = hm
        sp["key"] = kk
        return sp

    def linattn_gates(self, sp, inp_all, akey):
        skey = sp["key"]
        for c in range(NT):
            for d in range(2):
                TRI = self.TRI_Fb if d == 0 else self.TRI_Bb
                parts = [sp[nm + "_b"][:, c, d * 4:(d + 1) * 4] for nm in ("hi", "mid", "lo")]
                for lhs, off in ((self.ONES_b, 0), (TRI, 4)):
                    col = c * 16 + d * 8 + off
                    for pi, pa in enumerate(parts):
                        self.mm(self.PS[5][:, col:col + 4], lhs, pa, start=(pi == 0), stop=(pi == 2), rd=["cstb", skey], wr=[("ps", 5)])
        gat = self.la_gat
        self.v("dve", "tensor_copy", [("ps", 5)], ["la_gat"], out=gat[:].rearrange("p c d k -> p (c d k)"), in_=self.PS[5][:, 0:NT * 16])
        g3 = gat[:].rearrange("p c d k -> p (c d) k")
        fl = lambda t: t[:].rearrange("p c d k -> p (c d) k")
        self.act(fl(self.la_etot), g3[:, :, 0:4], AF.Exp, rd=["la_gat"], wr=["la_etot"])
        self.act(fl(self.la_ecum), g3[:, :, 4:8], AF.Exp, rd=["la_gat"], wr=["la_ecum"])
        self.v("dve", "tensor_tensor", ["la_gat"], ["la_dec"], out=fl(self.la_dec), in0=g3[:, :, 0:4], in1=g3[:, :, 4:8], op=ALU.subtract)
        self.act(fl(self.la_dec), fl(self.la_dec), AF.Exp, rd=["la_dec"], wr=["la_dec"])
        self.v("dve", "tensor_tensor", ["la_dec", akey], ["la_dec"], out=fl(self.la_dec), in0=fl(self.la_dec),
               in1=inp_all[:].rearrange("p c (d k) -> p (c d) k", d=2), op=ALU.mult)

    def linattn_init(self, L):
        si = L["id"]
        self.v("dve", "memset", [], [f"laS{si}"], ap=L["S"][:], constant=0.0)
        self.v("dve", "memset", [], [f"laSbf{si}"], ap=L["Sbf"][:], constant=0.0)

    def linattn_step(self, L, d, c, want_out, qT, kT, ktok, vtok, gidx, qbase, sp, inp_all, akey, on_out, prep_chunk=None):
        Pv = L["Pv"]; si = L["id"]
        K = lambda nm, j=None: (f"la{nm}{si}", j) if j is not None else f"la{nm}{si}"
        TRI = self.TRI_Fb if d == 0 else self.TRI_Bb
        U = self.U_Fb if d == 0 else self.U_Bb
        NEGm = self.NEG_Fb if d == 0 else self.NEG_Bb
        S, Sbf = L["S"], L["Sbf"]
        skey = sp["key"]
        bA, bB, bC = (0, 1, 2) if si == 0 else (3, 4, 7)
        Dreg = lambda jj: self.PS[bA][:, jj * 128:(jj + 1) * 128]
        Greg = lambda jj: self.PS[bA][:, 256 + jj * 128:256 + (jj + 1) * 128]
        Yreg = lambda j: self.PS[bB][:, j * Pv:(j + 1) * Pv]
        Creg = lambda j: self.PS[bC][:, j * Pv:(j + 1) * Pv]
        Sbank = lambda j: bB if j < 2 else bC
        Sreg = lambda j: self.PS[Sbank(j)][0:64, 260 + (j % 2) * Pv:260 + (j % 2 + 1) * Pv]
        if prep_chunk is not None:
            prep_chunk(d, c)
            yield
        if want_out:
            gs = sorted(set(gidx(j) for j in range(4)))
            jrep = {g: [j for j in range(4) if gidx(j) == g][0] for g in gs}
            for g0 in range(0, len(gs), 2):
                for g in gs[g0:g0 + 2]:
                    self.mm(Greg(g % 2), kT(c, jrep[g]), qT(c, jrep[g]), rd=["laqk"], wr=[("ps", bA)])
                ng = len(gs[g0:g0 + 2])
                self.act(L["G"][:, g0:g0 + ng, :], self.PS[bA][:, 256:256 + ng * 128].rearrange("p (g t) -> p g t", g=ng), AF.Identity,
                         rd=[("ps", bA)], wr=[K("G", g) for g in gs[g0:g0 + ng]])
                yield
            self.v("dve", "tensor_tensor", ["cstb", skey], [K("A")], out=L["A"][:],
                   in0=TRI.unsqueeze(1).unsqueeze(1).broadcast_to([128, 4, 2, 128]),
                   in1=sp["hm"][:, c, d * 4:(d + 1) * 4, :].unsqueeze(3).broadcast_to([128, 4, 2, 128]), op=ALU.mult)
            yield
            for hp in range(2):
                for j in (2 * hp, 2 * hp + 1):
                    o = Dreg(j % 2)
                    self.mm(o, U, L["A"][:, j, 0, :], start=True, stop=False, rd=["cstb", K("A")], wr=[("ps", bA)])
                    self.mm(o, U, L["A"][:, j, 1, :], start=False, stop=False, rd=["cstb", K("A")], wr=[("ps", bA)])
                    self.mm(o, self.I_b, NEGm, start=False, stop=True, rd=["cstb"], wr=[("ps", bA)])
                self.act(L["E"][:, 2 * hp:2 * hp + 2, :], self.PS[bA][:, 0:256].rearrange("p (g t) -> p g t", g=2), AF.Exp,
                         rd=[("ps", bA)], wr=[K("E", 2 * hp), K("E", 2 * hp + 1)])
                yield
            for j in range(4):
                self.v("dve", "scalar_tensor_tensor", [K("E", j), K("G", gidx(j)), akey], [K("W", j)], out=L["W"][j][:], in0=L["E"][:, j, :],
                       scalar=inp_all[:, c, d * 4 + j:d * 4 + j + 1], in1=L["G"][:, gidx(j), :], op0=ALU.mult, op1=ALU.mult)
            yield
            for j in range(4):
                self.mm(Yreg(j), L["W"][j][:], vtok(c, j), rd=[K("W", j), "lav"], wr=[("ps", bB)])
            for j in range(4):
                b0 = qbase(j)
                self.mm(Creg(j), qT(c, j), Sbf[b0:b0 + 64, j, 0:Pv], rd=["laqk", K("Sbf")], wr=[("ps", bC)])
            yield
            for j in range(4):
                self.act(L["tc"][:, j, :], Creg(j), AF.Identity, rd=[("ps", bC), "la_ecum"], wr=[K("tc", j)], scale=self.la_ecum[:, c, d, j:j + 1])
            yield
            self.v("dve", "tensor_tensor", [K("tc", j) for j in range(4)] + [("ps", bB)], [K("ydir", j) for j in range(4)], out=L["ydir"][:],
                   in0=L["tc"][:], in1=self.PS[bB][:, 0:4 * Pv].rearrange("p (j v) -> p j v", j=4), op=ALU.add)
            yield
            on_out(c, d, L["ydir"], [K("ydir", j) for j in range(4)])
            yield
        for j in range(4):
            self.act(L["vs"][:, j, 0:Pv], vtok(c, j), AF.Identity, rd=["lav", "la_dec"], wr=[K("vs", j)], scale=self.la_dec[:, c, d, j:j + 1])
        yield
        for j in range(4):
            self.mm(Sreg(j), ktok(d, c, j), L["vs"][:, j, 0:Pv], rd=["laktok%d" % d, K("vs", j)], wr=[("ps", Sbank(j))])
        yield
        for j in range(4):
            self.v("dve", "scalar_tensor_tensor", [K("S"), "la_etot", ("ps", Sbank(j))], [K("S")], out=S[:, j, :], in0=S[:, j, :],
                   scalar=self.la_etot[0:64, c, d, j:j + 1], in1=Sreg(j), op0=ALU.mult, op1=ALU.add)
        yield
        for b0 in (0, 64):
            hs = [j for j in range(4) if qbase(j) == b0]
            j0, st = hs[0], hs[1] - hs[0]
            self.act(Sbf[b0:b0 + 64, j0:j0 + st + 1:st, 0:Pv], S[:, j0:j0 + st + 1:st, :], AF.Identity, rd=[K("S")], wr=[K("Sbf")])

    def linattn_run(self, skip_out, **args):
        orders = [list(range(NT)), [1, 0] + list(range(NT - 1, 1, -1))]
        for L in self.la_sets:
            self.linattn_init(L)
        for i in range(NT):
            gens = [self.linattn_step(self.la_sets[d], d, orders[d][i], orders[d][i] not in skip_out, **args) for d in range(2)]
            while gens:
                for g in list(gens):
                    try:
                        next(g)
                    except StopIteration:
                        gens.remove(g)

    def mixer_mlstm(self, l, s, need_ctx):
        m = self.mark()
        self.mixer_common_alloc(l, SEG_A, 1040, 0, defer_wmix=True)
        wog = self.load_gate_w(l, SEG_A + 768, "a_wog")
        qT = self.A("a_qT", [128, 2, T], BF16); kT = self.A("a_kT", [128, 2, T], BF16)
        vaug = self.A("a_vaug", [128, NT, 4, 80], BF16)
        a_all = self.A("a_a", [128, NT, 8], F32); inp_all = self.A("a_inp", [128, NT, 8], F32)
        hsum = self.A("a_hsum", [128, NT, 256], F32 if self.cfg.get("acc_f32") else BF16)
        gt = self.A("a_gt", [128, 16], F32); gt2 = self.A("a_gt2", [128, 2, 4], F32)
        m_w = self.mark()
        self.alloc_wmix(l, SEG_A, 1040)
        self.v("dve", "memset", [], ["a_vaug"], ap=vaug[:], constant=1.0)
        for sti in range(5):
            t0, n = ST[sti]
            for c in range(2):
                pb = self.rot("pj", [0, 1, 2, 3])
                self.proj_fm(c * 128, 128, t0, n, pb)
                self.act(qT[:, c, t0:t0 + n], self.PS[pb][:, 0:n], AF.Identity, rd=[("ps", pb)], wr=["laqk"])
                pb = self.rot("pj", [0, 1, 2, 3])
                self.proj_fm(256 + c * 128, 128, t0, n, pb)
                self.act(kT[:, c, t0:t0 + n], self.PS[pb][:, 0:n], AF.Identity, rd=[("ps", pb)], wr=["laqk"], scale=0.125)
            for tt in range(n // 128):
                gi = t0 // 128 + tt
                pb = self.rot("pj", [0, 1, 2, 3])
                self.proj_tm(512, 256, gi, pb)
                self.v("dve", "tensor_copy", [("ps", pb), "a_vaug"], ["lav"], out=vaug[:, gi, :, 0:64],
                       in_=self.PS[pb][:, 0:256].rearrange("p (h d) -> p h d", h=4))
                pb = self.rot("pj", [0, 1, 2, 3])
                self.proj_tm(1024, 16, gi, pb)
                self.v("dve", "tensor_tensor", [("ps", pb), "rep"], ["a_gt"], out=gt[:], in0=self.PS[pb][:, 0:16], in1=self.rep[:, R_GATEB:R_GATEB + 16], op=ALU.add)
                g4 = gt[:].rearrange("p (k h) -> p k h", k=4)
                self.act(inp_all[:, gi, :].rearrange("p (k h) -> p k h", k=2), g4[:, 0:4:2, :], AF.Exp, rd=["a_gt"], wr=["a_gates"])
                self.act(gt2[:], g4[:, 1:4:2, :], AF.Exp, rd=["a_gt"], wr=["a_gt2"], scale=-1.0)
                self.act(gt2[:], gt2[:], AF.Ln, rd=["a_gt2"], wr=["a_gt2"], bias=self.eps_t[:, 2:3])
                self.v("dve", "tensor_scalar", ["a_gt2"], ["a_gates"], out=a_all[:, gi, :].rearrange("p (k h) -> p k h", k=2), in0=gt2[:], scalar1=-1.0, scalar2=None, op0=ALU.mult)
        self.release(m_w)
        m_la = self.mark()
        self.linattn_alloc(65)
        sp = self.split_gates(a_all, "a_gates", "a_sp")
        self.linattn_gates(sp, inp_all, "a_gates")
        hd = [self.A(f"a_hd{i}", [128, 4], F32) for i in range(2)]; hr = [self.A(f"a_hr{i}", [128, 4], F32) for i in range(2)]
        hdir = [self.A(f"a_hdir{i}", [128, 4, 64], F32) for i in range(2)]
        ktmp = [self.A(f"a_ktmp{i}", [128, 256], BF16) for i in range(2)]
        visited = set()

        def prep_chunk(d, c):
            for b_ in range(2):
                self.mm(self.PS[6][:, (d * 2 + b_) * 128:(d * 2 + b_ + 1) * 128], kT[:, b_, c * 128:(c + 1) * 128], self.I_b, rd=["laqk", "cstb"], wr=[("ps", 6)])
            self.act(ktmp[d][:], self.PS[6][:, d * 256:(d + 1) * 256], AF.Identity, rd=[("ps", 6)], wr=["laktok%d" % d])

        def on_out(c, d, ydir, ykeys):
            self.act(hd[d][:], ydir[:, :, 64], AF.Abs, rd=ykeys, wr=[f"a_hd{d}"])
            self.v("dve", "tensor_scalar", [f"a_hd{d}"], [f"a_hd{d}"], out=hd[d][:], in0=hd[d][:], scalar1=1.0, scalar2=None, op0=ALU.max)
            self.v("dve", "reciprocal", [f"a_hd{d}"], [f"a_hr{d}"], out=hr[d][:], in_=hd[d][:])
            hs = hsum[:, c, :].rearrange("p (h d) -> p h d", h=4)
            rb = hr[d][:].unsqueeze(2).broadcast_to([128, 4, 64])
            if c not in visited:
                visited.add(c)
                self.v("dve", "tensor_tensor", ykeys + [f"a_hr{d}"], [("a_hsum", c)], out=hs, in0=ydir[:, :, 0:64], in1=rb, op=ALU.mult)
            else:
                self.v("dve", "tensor_tensor", ykeys + [f"a_hr{d}"], [f"a_hdir{d}"], out=hdir[d][:], in0=ydir[:, :, 0:64], in1=rb, op=ALU.mult)
                self.v("dve", "tensor_tensor", [f"a_hdir{d}", ("a_hsum", c)], [("a_hsum", c)], out=hs, in0=hs, in1=hdir[d][:], op=ALU.add)

        skip = set() if need_ctx else {0, 1}
        args = dict(qT=lambda c, j: qT[(j % 2) * 64:(j % 2) * 64 + 64, j // 2, c * 128:(c + 1) * 128],
                    kT=lambda c, j: kT[(j % 2) * 64:(j % 2) * 64 + 64, j // 2, c * 128:(c + 1) * 128],
                    ktok=lambda d, c, j: ktmp[d][:, j * 64:(j + 1) * 64], vtok=lambda c, j: vaug[:, c, j, 0:65],
                    gidx=lambda j: j, qbase=lambda j: (j % 2) * 64, sp=sp, inp_all=inp_all, akey="a_gates", on_out=on_out, prep_chunk=prep_chunk)
        self.linattn_run(skip, **args)
        self.release(m_la)
        F = []
        for sl in range(2):
            F.append(dict(sq=self.A(f"a_sq{sl}", [128, 256], F32), ss=self.A(f"a_ss{sl}", [128, 4], F32), lnb=self.A(f"a_lnb{sl}", [128, 4], F32),
                          sig=self.A(f"a_sig{sl}", [128, 256], F32), hn=self.A(f"a_hn{sl}", [128, 256], F32), otok=self.A(f"a_otok{sl}", [128, 256], BF16)))
        mixT = self.A("a_mixT", [128, 2, 512], BF16)

        def fin_tile(gi, tt, sl):
            f = F[sl]; k = lambda nm: f"a_{nm}{sl}"
            pb = sl
            self.proj_tm(0, 256, gi, pb, w=wog, wkey="a_wog")
            self.act(f["sig"][:], self.PS[pb][:, 0:256], AF.Sigmoid, rd=[("ps", pb)], wr=[k("sig")])
            self.act(f["sq"][:], hsum[:, gi, :], AF.Square, rd=[("a_hsum", gi)], wr=[k("sq")])
            self.v("dve", "tensor_reduce", [k("sq")], [k("ss")], out=f["ss"][:], in_=f["sq"][:].rearrange("p (h d) -> p h d", h=4), axis=AX.X, op=ALU.add)
            self.v("dve", "tensor_scalar", [k("ss")], [k("ss")], out=f["ss"][:], in0=f["ss"][:], scalar1=1.0 / 64, scalar2=None, op0=ALU.mult)
            self.ln_exp(f["ss"][:], f["ss"][:], self.eps_t[:, 1:2], -0.5, [k("ss"), "eps_t"], [k("ss")], k("lnb"), f["lnb"][:])
            self.v("dve", "tensor_tensor", [("a_hsum", gi), k("ss")], [k("hn")], out=f["hn"][:].rearrange("p (h d) -> p h d", h=4),
                   in0=hsum[:, gi, :].rearrange("p (h d) -> p h d", h=4), in1=f["ss"][:].unsqueeze(2).broadcast_to([128, 4, 64]), op=ALU.mult)
            self.v("dve", "tensor_tensor", [k("hn"), "rep"], [k("hn")], out=f["hn"][:], in0=f["hn"][:], in1=self.rep[:, R_ONORM:R_ONORM + 256], op=ALU.mult)
            self.v("dve", "tensor_tensor", [k("hn"), k("sig")], [k("otok")], out=f["otok"][:], in0=f["hn"][:], in1=f["sig"][:], op=ALU.mult)
            self.tok_to_mixT(f["otok"][:], k("otok"), mixT, "a_mixT", tt, s, 0, gi, pb=6 + sl)

        for sti in range(5):
            if sti == 0 and not need_ctx:
                continue
            t0, n = ST[sti]
            for tp_ in range(0, n // 128, 2):
                self.P.interleave([lambda tt=tt: fin_tile(t0 // 128 + tt, tt, tt % 2) for tt in (tp_, tp_ + 1)])
            self.out_proj(l, s, sti, mixT, "a_mixT", n)
        self.release(m)

    def mixer_ssd(self, l, s, need_ctx):
        m = self.mark()
        self.mixer_common_alloc(l, SEG_S, 776, 3, defer_wmix=True)
        wz = self.load_gate_w(l, SEG_S, "s_wz")
        xbcT = self.A("s_xbcT", [128, 4, T], BF16)
        a_all = self.A("s_a", [128, NT, 8], F32); inp_all = self.A("s_inp", [128, NT, 8], F32)
        Arep = self.A("s_Arep", [128, 8], F32); dtt = self.A("s_dtt", [128, 8], F32)
        self.act(Arep[:], self.rep[:, R_ALOG:R_ALOG + 8], AF.Exp, rd=["rep"], wr=["s_Arep"])
        self.v("dve", "tensor_scalar", ["s_Arep"], ["s_Arep"], out=Arep[:], in0=Arep[:], scalar1=-1.0, scalar2=None, op0=ALU.mult)
        m2 = self.mark()
        self.alloc_wmix(l, SEG_S, 776, pad=784)
        pre_c = self.A("s_prec", [128, 4, TCX + 4], F32); pre_l = self.A("s_prel", [128, 4, TL + 4], F32)
        acc = self.A("s_acc", [128, TL], F32)
        for pre, n_ in ((pre_c, TCX), (pre_l, TL)):
            self.v("dve", "memset", [], ["s_pre"], ap=pre[:, :, 0:2], constant=0.0)
            self.v("dve", "memset", [], ["s_pre"], ap=pre[:, :, n_ + 2:n_ + 4], constant=0.0)
        for sti in range(5):
            t0, n = ST[sti]
            for c in range(4):
                pb = self.rot("pj", [0, 1, 2, 3])
                self.proj_fm(256 + c * 128, 128, t0, n, pb)
                dst = pre_c[:, c, 2:2 + n] if sti == 0 else pre_l[:, c, 2 + t0 - TCX:2 + t0 - TCX + n]
                self.act(dst, self.PS[pb][:, 0:n], AF.Identity, rd=[("ps", pb), "s_pre"], wr=[("s_pre2", sti, c)])
            for tt in range(n // 128):
                gi = t0 // 128 + tt
                pb = self.rot("pj", [0, 1, 2, 3])
                self.proj_tm(768, 8, gi, pb)
                self.v("dve", "tensor_tensor", [("ps", pb), "rep"], ["s_dtt"], out=dtt[:], in0=self.PS[pb][:, 0:8], in1=self.rep[:, R_DTB:R_DTB + 8], op=ALU.add)
                self.act(dtt[:], dtt[:], AF.Exp, rd=["s_dtt"], wr=["s_dtt"])
                self.act(inp_all[:, gi, :], dtt[:], AF.Ln, rd=["s_dtt"], wr=["s_gates"], bias=self.eps_t[:, 2:3])
                self.v("dve", "tensor_tensor", ["s_gates", "s_Arep"], ["s_gates"], out=a_all[:, gi, :], in0=inp_all[:, gi, :], in1=Arep[:], op=ALU.mult)
        for pre, n_, tb, stis in ((pre_c, TCX, 0, [0]), (pre_l, TL, TCX, [1, 2, 3, 4])):
            for c in range(4):
                rdp = [("s_pre2", sti, c) for sti in stis] + ["s_pre", "pp"]
                cw = lambda k: self.pp[:, P_CW + (l * 4 + c) * 5 + k: P_CW + (l * 4 + c) * 5 + k + 1]
                self.v("dve", "tensor_scalar", rdp, ["s_acc"], out=acc[:, 0:n_], in0=pre[:, c, 0:n_], scalar1=cw(0),
                       scalar2=self.pp[:, P_CB + l * 4 + c:P_CB + l * 4 + c + 1], op0=ALU.mult, op1=ALU.add)
                for k in range(1, 5):
                    self.v("dve", "scalar_tensor_tensor", rdp + ["s_acc"], ["s_acc"], out=acc[:, 0:n_], in0=pre[:, c, k:k + n_], scalar=cw(k),
                           in1=acc[:, 0:n_], op0=ALU.mult, op1=ALU.add)
                self.act(xbcT[:, c, tb:tb + n_], acc[:, 0:n_], AF.Silu, rd=["s_acc"], wr=["s_xbcT", "laqk"])
        self.release(m2)
        sstage = self.cfg.get("s_stage", 9)
        if sstage < 2:
            self.release(m)
            return
        xtok = self.A("s_xtok", [128, NT, 256], BF16); btok = self.A("s_btok", [128, NT, 128], BF16)
        yacc = self.A("s_yacc", [128, NT, 256], F32 if self.cfg.get("acc_f32") else BF16)
        m_la = self.mark()
        self.linattn_alloc(64)
        for gi in range(NT):
            pb = self.rot("pj", [0, 1, 2, 3])
            for c in range(3):
                self.mm(self.PS[pb][:, c * 128:(c + 1) * 128], xbcT[:, c, gi * 128:(gi + 1) * 128], self.I_b, rd=["s_xbcT", "cstb"], wr=[("ps", pb)])
            self.act(xtok[:, gi, :], self.PS[pb][:, 0:256], AF.Identity, rd=[("ps", pb)], wr=["lav"])
            self.act(btok[:, gi, :], self.PS[pb][:, 256:384], AF.Identity, rd=[("ps", pb)], wr=["laktok0", "laktok1"])

        if sstage < 3:
            self.release(m)
            return

        sp = self.split_gates(a_all, "s_gates", "s_sp")
        self.linattn_gates(sp, inp_all, "s_gates")
        visited = set()

        def on_out(c, d, ydir, ykeys):
            ys = yacc[:, c, :].rearrange("p (h d) -> p h d", h=4)
            if c not in visited:
                visited.add(c)
                self.v("dve", "tensor_copy", ykeys, [("s_yacc", c)], out=ys, in_=ydir[:])
            else:
                self.v("dve", "tensor_tensor", ykeys + [("s_yacc", c)], [("s_yacc", c)], out=ys, in0=ys, in1=ydir[:], op=ALU.add)

        skip = set() if need_ctx else {0, 1}
        args = dict(qT=lambda c, j: xbcT[(j // 2) * 64:(j // 2) * 64 + 64, 3, c * 128:(c + 1) * 128],
                    kT=lambda c, j: xbcT[(j // 2) * 64:(j // 2) * 64 + 64, 2, c * 128:(c + 1) * 128],
                    ktok=lambda d, c, j: btok[:, c, (j // 2) * 64:(j // 2) * 64 + 64], vtok=lambda c, j: xtok[:, c, j * 64:(j + 1) * 64],
                    gidx=lambda j: j // 2, qbase=lambda j: (j // 2) * 64, sp=sp, inp_all=inp_all, akey="s_gates", on_out=on_out)
        self.linattn_run(skip, **args)
        self.release(m_la)
        F = []
        for sl in range(2):
            F.append(dict(sz=self.A(f"s_sz{sl}", [128, 256], F32), yy=self.A(f"s_yy{sl}", [128, 256], F32), sq=self.A(f"s_sq{sl}", [128, 256], F32),
                          ss=self.A(f"s_ss{sl}", [128, 2], F32), lnb=self.A(f"s_lnb{sl}", [128, 2], F32), otok=self.A(f"s_otok{sl}", [128, 256], BF16)))
        mixT = self.A("s_mixT", [128, 2, 512], BF16)

        def fin_tile(gi, tt, sl):
            f = F[sl]; k = lambda nm: f"s_{nm}{sl}"
            pb = sl
            yy = f["yy"]
            self.proj_tm(0, 256, gi, pb, w=wz, wkey="s_wz")
            self.act(f["sz"][:], self.PS[pb][:, 0:256], AF.Silu, rd=[("ps", pb)], wr=[k("sz")])
            y4 = yy[:].rearrange("p (h d) -> p h d", h=4)
            self.v("dve", "tensor_tensor", ["lav", "rep"], [k("yy")], out=y4, in0=xtok[:, gi, :].rearrange("p (h d) -> p h d", h=4),
                   in1=self.rep[:, R_SSDD:R_SSDD + 4].unsqueeze(2).broadcast_to([128, 4, 64]), op=ALU.mult)
            self.v("dve", "tensor_tensor", [k("yy"), ("s_yacc", gi)], [k("yy")], out=yy[:], in0=yy[:], in1=yacc[:, gi, :], op=ALU.add)
            self.v("dve", "tensor_tensor", [k("yy"), k("sz")], [k("yy")], out=yy[:], in0=yy[:], in1=f["sz"][:], op=ALU.mult)
            self.act(f["sq"][:], yy[:], AF.Square, rd=[k("yy")], wr=[k("sq")])
            self.v("dve", "tensor_reduce", [k("sq")], [k("ss")], out=f["ss"][:], in_=f["sq"][:].rearrange("p (g d) -> p g d", g=2), axis=AX.X, op=ALU.add)
            self.v("dve", "tensor_scalar", [k("ss")], [k("ss")], out=f["ss"][:], in0=f["ss"][:], scalar1=1.0 / 128, scalar2=None, op0=ALU.mult)
            self.ln_exp(f["ss"][:], f["ss"][:], self.eps_t[:, 1:2], -0.5, [k("ss"), "eps_t"], [k("ss")], k("lnb"), f["lnb"][:])
            self.v("dve", "tensor_tensor", [k("yy"), k("ss")], [k("yy")], out=yy[:].rearrange("p (g d) -> p g d", g=2),
                   in0=yy[:].rearrange("p (g d) -> p g d", g=2), in1=f["ss"][:].unsqueeze(2).broadcast_to([128, 2, 128]), op=ALU.mult)
            self.v("dve", "tensor_tensor", [k("yy"), "rep"], [k("otok")], out=f["otok"][:], in0=yy[:], in1=self.rep[:, R_SNORM:R_SNORM + 256], op=ALU.mult)
            self.tok_to_mixT(f["otok"][:], k("otok"), mixT, "s_mixT", tt, s, 3, gi, pb=6 + sl)

        for sti in range(5):
            if sti == 0 and not need_ctx:
                continue
            t0, n = ST[sti]
            for tp_ in range(0, n // 128, 2):
                self.P.interleave([lambda tt=tt: fin_tile(t0 // 128 + tt, tt, tt % 2) for tt in (tp_, tp_ + 1)])
            self.out_proj(l, s, sti, mixT, "s_mixT", n)
        self.release(m)

    def rope(self, x3, H, q, tab, gi, tmp1, tmp2, rd, key):
        lt = gi - 2
        cos = tab[:, lt, 0, :].unsqueeze(1).broadcast_to([128, H, 4 * q])
        xv = x3.rearrange("p h (a b d) -> p h a b d", a=2, b=2)
        tv = tmp2.rearrange("p h (a b d) -> p h a b d", a=2, b=2)
        sv = tab[:, lt, 1, :].rearrange("p (a b d) -> p a b d", a=2, b=2)
        for b_ in range(2):
            self.v("dve", "tensor_tensor", rd + ["rope"], [key + "_t2"], out=tv[:, :, :, b_, :], in0=xv[:, :, :, 1 - b_, :],
                   in1=sv[:, :, b_, :].unsqueeze(1).broadcast_to([128, H, 2, q]), op=ALU.mult)
        self.v("dve", "tensor_tensor", rd + ["rope"], [key + "_t1"], out=tmp1, in0=x3, in1=cos, op=ALU.mult)
        self.v("dve", "tensor_tensor", [key + "_t1", key + "_t2"] + rd, rd, out=x3, in0=tmp1, in1=tmp2, op=ALU.add)

    def mixer_swa(self, l, s, need_ctx):
        m = self.mark()
        self.mixer_common_alloc(l, SEG_W, 512, 2)
        qTs = self.A("w_qT", [128, 2, T], BF16)
        kTs = self.A("w_kT", [128, T], BF16)
        vaug = self.A("w_vaug", [128, NT, 2, 80], BF16)
        tab = self.A("w_rope", [128, 16, 2, 64], F32)
        self.dma("sp", tab[:], self.ropeS_d, [], ["rope"], "rope")
        gain6 = self.A("w_gain6", [128, 6, 64], F32)
        self.v("dve", "tensor_copy", ["rep"], ["w_gain6"], out=gain6[:, 0:4, :], in_=self.rep[:, R_SQG:R_SQG + 64].unsqueeze(1).broadcast_to([128, 4, 64]))
        self.v("dve", "tensor_copy", ["rep", "w_gain6"], ["w_gain6"], out=gain6[:, 4:6, :], in_=self.rep[:, R_SKG:R_SKG + 64].unsqueeze(1).broadcast_to([128, 2, 64]))
        esink = self.A("w_esink", [128, 4], F32)
        self.act(esink[:], self.rep[:, R_SINK:R_SINK + 4], AF.Exp, rd=["rep"], wr=["w_esink"])
        self.v("dve", "memset", [], ["w_vaug"], ap=vaug[:], constant=1.0)
        W_ = []
        for sl in range(2):
            W_.append(dict(sq=self.A(f"w_sq{sl}", [128, 384], F32), ss=self.A(f"w_ss{sl}", [128, 6], F32), lnb=self.A(f"w_lnb{sl}", [128, 6], F32),
                           qk=self.A(f"w_qk{sl}", [128, 6, 64], F32), t1=self.A(f"w_t1{sl}", [128, 6, 64], F32), t2=self.A(f"w_t2{sl}", [128, 6, 64], F32),
                           qkb=self.A(f"w_qkb{sl}", [128, 384], BF16)))

        def prep_tile(gi, sl):
            f = W_[sl]; k = lambda nm: f"w_{nm}{sl}"
            sq, ss, lnb, qk, qkb = f["sq"], f["ss"], f["lnb"], f["qk"], f["qkb"]
            pb = 2 * sl
            self.proj_tm(0, 512, gi, pb)
            ps = self.PS[pb]
            self.act(sq[:], ps[:, 0:384], AF.Square, rd=[("ps", pb)], wr=[k("sq")])
            self.v("dve", "tensor_reduce", [k("sq")], [k("ss")], out=ss[:], in_=sq[:].rearrange("p (h d) -> p h d", h=6), axis=AX.X, op=ALU.add)
            self.v("dve", "tensor_scalar", [k("ss")], [k("ss")], out=ss[:], in0=ss[:], scalar1=1.0 / 64, scalar2=None, op0=ALU.mult)
            self.ln_exp(ss[:], ss[:], self.eps_t[:, 1:2], -0.5, [k("ss"), "eps_t"], [k("ss")], k("lnb"), lnb[:])
            self.v("dve", "tensor_tensor", [("ps", pb), k("ss")], [k("qk")], out=qk[:], in0=ps[:, 0:384].rearrange("p (h d) -> p h d", h=6),
                   in1=ss[:].unsqueeze(2).broadcast_to([128, 6, 64]), op=ALU.mult)
            self.v("dve", "tensor_tensor", [k("qk"), "w_gain6"], [k("qk")], out=qk[:], in0=qk[:], in1=gain6[:], op=ALU.mult)
            self.v("dve", "tensor_copy", [("ps", pb), "w_vaug"], [("w_vaug", gi)], out=vaug[:, gi, :, 0:64], in_=ps[:, 384:512].rearrange("p (h d) -> p h d", h=2))
            if gi >= 2:
                self.rope(qk[:], 6, 16, tab, gi, f["t1"][:], f["t2"][:], [k("qk")], k("r"))
            self.v("dve", "tensor_copy", [k("qk")], [k("qkb")], out=qkb[:, 0:256].rearrange("p (b a d) -> p a b d", b=2, a=2),
                   in_=qk[:, 0:4, :].rearrange("p (a b) d -> p a b d", a=2))
            self.v("dve", "tensor_copy", [k("qk")], [k("qkb")], out=qkb[:, 256:384], in_=qk[:, 4:6, :].rearrange("p h d -> p (h d)"))
            pt = 2 * sl + 1
            for c in range(3):
                self.mm(self.PS[pt][:, c * 128:(c + 1) * 128], qkb[:, c * 128:(c + 1) * 128], self.I_b, rd=[k("qkb"), "cstb"], wr=[("ps", pt)])
            self.act(qTs[:, :, gi * 128:(gi + 1) * 128], self.PS[pt][:, 0:256].rearrange("p (c t) -> p c t", c=2), AF.Identity, rd=[("ps", pt)], wr=[("w_qT", gi)])
            self.act(kTs[:, gi * 128:(gi + 1) * 128], self.PS[pt][:, 256:384], AF.Identity, rd=[("ps", pt)], wr=[("w_kT", gi)])

        for g0 in range(0, NT, 2):
            self.P.interleave([lambda gi=gi: prep_tile(gi, gi % 2) for gi in (g0, g0 + 1)])
        PT = [self.A(f"w_PT{i}", [128, 5, 128], BF16) for i in range(2)]
        otok = [self.A(f"w_otok{i}", [128, 256], BF16) for i in range(2)]
        den = self.A("w_den", [128, 4], F32)
        mixT = self.A("w_mixT", [128, 2, 512], BF16)
        for sti in range(5):
            if sti == 0 and not need_ctx:
                continue
            t0, n = ST[sti]
            def keys_of(gi):
                if gi < 2:
                    return [(0, None), (1, None)]
                keys = []
                if gi - 1 >= 2:
                    keys.append((gi - 1, self.NEG_Bb))
                keys.append((gi, None))
                if gi + 1 < NT:
                    keys.append((gi + 1, self.NEG_Fb))
                return keys + [(0, None), (1, None)]

            items = [(tt, h) for tt in range(n // 128) for h in range(4)]

            def s_stage(it):
                tt, h = it
                gi = t0 // 128 + tt
                keys = keys_of(gi)
                a_, b_ = h // 2, h % 2
                q_ap = qTs[a_ * 64:(a_ + 1) * 64, b_, gi * 128:(gi + 1) * 128]
                pa = self.rot("w_s", [0, 1]); pbk = self.rot("w_s2", [2, 3])
                for ki, (kt, msk) in enumerate(keys):
                    pbank = pa if ki < 4 else pbk
                    col = (ki % 4) * 128
                    self.mm(self.PS[pbank][:, col:col + 128], kTs[a_ * 64:(a_ + 1) * 64, kt * 128:(kt + 1) * 128], q_ap,
                            start=True, stop=(msk is None), rd=["w_kT", "w_qT"], wr=[("ps", pbank)])
                    if msk is not None:
                        self.mm(self.PS[pbank][:, col:col + 128], self.I_b, msk, start=False, stop=True, rd=["cstb"], wr=[("ps", pbank)])
                return pa, pbk

            banks_next = s_stage(items[0])
            ob = 0
            for idx, (tt, h) in enumerate(items):
                gi = t0 // 128 + tt
                keys = keys_of(gi)
                a_ = h // 2
                pa, pbk = banks_next
                if idx + 1 < len(items):
                    banks_next = s_stage(items[idx + 1])
                if h == 0:
                    ob = self.rot("w_otok", [0, 1])
                pto = self.rot("w_pt", [0, 1])
                nk = len(keys)
                n1 = min(nk, 4)
                self.act(PT[pto][:, 0:n1, :], self.PS[pa][:, 0:n1 * 128].rearrange("p (k t) -> p k t", k=n1), AF.Exp, rd=[("ps", pa)], wr=[("w_PT", pto)], scale=0.125)
                if nk > 4:
                    self.act(PT[pto][:, 4:5, :], self.PS[pbk][:, 0:128].rearrange("p (k t) -> p k t", k=1), AF.Exp, rd=[("ps", pbk)], wr=[("w_PT", pto)], scale=0.125)
                po = self.rot("w_o", [4, 5])
                for ki, (kt, msk) in enumerate(keys):
                    self.mm(self.PS[po][:, 0:65], PT[pto][:, ki, :], vaug[:, kt, a_, 0:65], start=(ki == 0), stop=(ki == nk - 1),
                            rd=[("w_PT", pto), "w_vaug"], wr=[("ps", po)])
                self.v("dve", "tensor_tensor", [("ps", po), "w_esink"], [("w_den", h)], out=den[:, h:h + 1], in0=self.PS[po][:, 64:65], in1=esink[:, h:h + 1], op=ALU.add)
                self.v("dve", "reciprocal", [("w_den", h)], [("w_den", h)], out=den[:, h:h + 1], in_=den[:, h:h + 1])
                self.act(otok[ob][:, h * 64:(h + 1) * 64], self.PS[po][:, 0:64], AF.Identity, rd=[("ps", po), ("w_den", h)], wr=[("w_otok", ob)], scale=den[:, h:h + 1])
                if h == 3:
                    self.tok_to_mixT(otok[ob][:], ("w_otok", ob), mixT, "w_mixT", tt, s, 2, gi)
            self.out_proj(l, s, sti, mixT, "w_mixT", n)
        self.release(m)

    def mixer_mla(self, l, s, need_ctx):
        m = self.mark()
        self.mixer_common_alloc(l, SEG_M, 416, 1)
        qT = self.A("m_qT", [128, 4, T], BF16); kT = self.A("m_kT", [128, 4, T], BF16)
        vaug = self.A("m_vaug", [128, NT, 4, 80], BF16)
        wqb = self.A("m_wqb", [128, 2, 384], BF16); wkvb = self.A("m_wkvb", [128, 512], BF16)
        self.dma("pool", wqb[:], self.wq_b[l].rearrange("(c p) n -> p c n", p=128), [], ["m_wqb"], "m_wqb")
        self.dma("pool", wkvb[:], self.wkv_b[l], [], ["m_wkvb"], "m_wkvb")
        self.v("dve", "memset", [], ["m_vaug"], ap=vaug[:], constant=1.0)
        gq16 = self.A("m_gq16", [128, 2], F32); gkv = self.A("m_gkv", [128, 1], F32)
        self.v("dve", "tensor_scalar", ["pp"], ["m_g"], out=gq16[:], in0=self.pp[:, P_QN + l * 2:P_QN + l * 2 + 2], scalar1=16.0, scalar2=None, op0=ALU.mult)
        self.v("dve", "tensor_scalar", ["pp"], ["m_g"], out=gkv[:], in0=self.pp[:, P_KVN + l:P_KVN + l + 1], scalar1=float(math.sqrt(128.0)), scalar2=None, op0=ALU.mult)
        epsq = self.A("m_epsq", [128, 2], F32)
        self.v("dve", "memset", [], ["m_epsq"], ap=epsq[:, 0:1], constant=float(256 * EPS))
        self.v("dve", "memset", [], ["m_epsq"], ap=epsq[:, 1:2], constant=float(128 * EPS))
        m2 = self.mark()
        tab = self.A("m_rope", [128, 16, 2, 32], F32)
        self.dma("sp", tab[:], self.ropeM_d, [], ["rope"], "rope")
        aq = self.A("m_aq", [128, 2, 256], F32); aqsq = self.A("m_aqsq", [128, 2, 256], BF16); aqn = self.A("m_aqn", [128, 2, 256], BF16)
        akv = self.A("m_akv", [128, 256], F32); akvsq = self.A("m_akvsq", [128, 256], BF16); akvn = self.A("m_akvn", [128, 256], BF16)
        rs1 = self.A("m_rs1", [128, 256], F32); rs2 = self.A("m_rs2", [128, 256], F32); lnt = self.A("m_lnt", [128, 256], F32)
        sqq = self.A("m_sqq", [128, 4, 96], F32); ssq = self.A("m_ssq", [128, 12], F32); lnq = self.A("m_lnq", [128, 12], F32)
        sqk = self.A("m_sqk", [128, 4, 64], F32)
        qn = self.A("m_qn", [128, 4, 96], F32); kn = self.A("m_kn", [128, 4, 96], F32)
        qnb = self.A("m_qnb", [128, 4, 96], BF16); knb = self.A("m_knb", [128, 4, 96], BF16)
        t1 = self.A("m_t1", [128, 4, 32], F32); t2 = self.A("m_t2", [128, 4, 32], F32)
        kr = self.A("m_kr", [128, 1, 32], F32); sqr = self.A("m_sqr", [128, 32], F32)
        qr = self.A("m_qr", [128, 4, 32], F32); ssr = self.A("m_ssr", [128, 4], F32)
        ssk = self.A("m_ssk", [128, 4], F32); lnk = self.A("m_lnk", [128, 4], F32)
        t1k = self.A("m_t1k", [128, 1, 32], F32); t2k = self.A("m_t2k", [128, 1, 32], F32)
        QG = self.rep[:, R_QG:R_QG + 96]; KG = self.rep[:, R_KG:R_KG + 96]
        for piece in range(T // 256):
            t0, n = piece * 256, 256
            for c in range(2):
                pb = self.rot("pj", [0, 1, 2, 3])
                self.proj_fm(c * 128, 128, t0, n, pb)
                self.act(aq[:, c, 0:n], self.PS[pb][:, 0:n], AF.Identity, rd=[("ps", pb)], wr=["m_aq"])
            self.act(aqsq[:, :, 0:n], aq[:, :, 0:n], AF.Square, rd=["m_aq"], wr=["m_aqsq"])
            pb = self.rot("pj", [0, 1, 2, 3])
            for c in range(2):
                self.mm(self.PS[pb][:, 0:n], self.ONES_b, aqsq[:, c, 0:n], start=(c == 0), stop=(c == 1), rd=["m_aqsq", "cstb"], wr=[("ps", pb)])
            self.ln_exp(rs1[:, 0:n], self.PS[pb][:, 0:n], epsq[:, 0:1], -0.5, [("ps", pb), "m_epsq"], ["m_rs1"], "m_lnt", lnt[:, 0:n])
            for c in range(2):
                self.v("dve", "scalar_tensor_tensor", ["m_aq", "m_rs1", "m_g"], ["m_aqn"], out=aqn[:, c, 0:n], in0=aq[:, c, 0:n], scalar=gq16[:, c:c + 1],
                       in1=rs1[:, 0:n], op0=ALU.mult, op1=ALU.mult)
            pb = self.rot("pj", [0, 1, 2, 3])
            self.proj_fm(256, 128, t0, n, pb)
            self.act(akv[:, 0:n], self.PS[pb][:, 0:n], AF.Identity, rd=[("ps", pb)], wr=["m_akv"])
            self.act(akvsq[:, 0:n], akv[:, 0:n], AF.Square, rd=["m_akv"], wr=["m_akvsq"])
            pb = self.rot("pj", [0, 1, 2, 3])
            self.mm(self.PS[pb][:, 0:n], self.ONES_b, akvsq[:, 0:n], rd=["m_akvsq", "cstb"], wr=[("ps", pb)])
            self.ln_exp(rs2[:, 0:n], self.PS[pb][:, 0:n], epsq[:, 1:2], -0.5, [("ps", pb), "m_epsq"], ["m_rs2"], "m_lnt", lnt[:, 0:n])
            self.v("dve", "scalar_tensor_tensor", ["m_akv", "m_rs2", "m_g"], ["m_akvn"], out=akvn[:, 0:n], in0=akv[:, 0:n], scalar=gkv[:, 0:1],
                   in1=rs2[:, 0:n], op0=ALU.mult, op1=ALU.mult)
            for tt in range(n // 128):
                gi = t0 // 128 + tt
                tsl = slice(tt * 128, (tt + 1) * 128)

                def q_body(gi=gi, tsl=tsl):
                    pq = 0
                    for c in range(2):
                        self.mm(self.PS[pq][:, 0:384], aqn[:, c, tsl], wqb[:, c, :], start=(c == 0), stop=(c == 1), rd=["m_aqn", "m_wqb"], wr=[("ps", pq)])
                    psq = self.PS[pq][:, 0:384].rearrange("p (h d) -> p h d", h=4)
                    self.act(sqq[:], psq, AF.Square, rd=[("ps", pq)], wr=["m_sqq"])
                    self.v("dve", "tensor_reduce", ["m_sqq"], ["m_ssq"], out=ssq[:, 0:4], in_=sqq[:, :, 0:64], axis=AX.X, op=ALU.add)
                    self.v("dve", "tensor_reduce", ["m_sqq"], ["m_ssq"], out=ssq[:, 4:8], in_=sqq[:, :, 64:96], axis=AX.X, op=ALU.add)
                    self.v("dve", "tensor_scalar", ["m_ssq"], ["m_ssq"], out=ssq[:, 0:4], in0=ssq[:, 0:4], scalar1=1.0 / 64, scalar2=None, op0=ALU.mult)
                    self.v("dve", "tensor_scalar", ["m_ssq"], ["m_ssq"], out=ssq[:, 4:8], in0=ssq[:, 4:8], scalar1=1.0 / 32, scalar2=None, op0=ALU.mult)
                    self.ln_exp(ssq[:, 0:8], ssq[:, 0:8], self.eps_t[:, 1:2], -0.5, ["m_ssq", "eps_t"], ["m_ssq"], "m_lnq", lnq[:, 0:8])
                    self.v("dve", "tensor_tensor", [("ps", pq), "m_ssq"], ["m_qn"], out=qn[:, :, 0:64], in0=psq[:, :, 0:64], in1=ssq[:, 0:4].unsqueeze(2).broadcast_to([128, 4, 64]), op=ALU.mult)
                    self.v("dve", "tensor_tensor", ["m_qn", "rep"], ["m_qn"], out=qn[:, :, 0:64], in0=qn[:, :, 0:64], in1=QG[:, 0:64].unsqueeze(1).broadcast_to([128, 4, 64]), op=ALU.mult)
                    self.v("dve", "tensor_tensor", [("ps", pq), "m_ssq"], ["m_qr"], out=qr[:], in0=psq[:, :, 64:96], in1=ssq[:, 4:8].unsqueeze(2).broadcast_to([128, 4, 32]), op=ALU.mult)
                    self.v("dve", "tensor_tensor", ["m_qr", "rep"], ["m_qr"], out=qr[:], in0=qr[:], in1=QG[:, 64:96].unsqueeze(1).broadcast_to([128, 4, 32]), op=ALU.mult)
                    if gi >= 2:
                        self.rope(qr[:], 4, 8, tab, gi, t1[:], t2[:], ["m_qr"], "mq")
                    self.v("dve", "tensor_copy", ["m_qr", "m_qn"], ["m_qn"], out=qn[:, :, 64:96], in_=qr[:])
                    self.v("dve", "tensor_copy", ["m_qn"], ["m_qnb"], out=qnb[:], in_=qn[:])
                    pt = 2
                    for h in range(4):
                        self.mm(self.PS[pt][0:96, h * 128:(h + 1) * 128], qnb[:, h, :], self.I_b, rd=["m_qnb", "cstb"], wr=[("ps", pt)])
                    self.act(qT[0:96, :, gi * 128:(gi + 1) * 128], self.PS[pt][0:96, 0:512].rearrange("p (h t) -> p h t", h=4), AF.Identity,
                             rd=[("ps", pt)], wr=[("m_qT", gi)])

                def k_body(gi=gi, tsl=tsl):
                    pk = 1
                    self.mm(self.PS[pk][:, 0:512], akvn[:, tsl], wkvb[:], rd=["m_akvn", "m_wkvb"], wr=[("ps", pk)])
                    psk = self.PS[pk][:, 0:512].rearrange("p (h d) -> p h d", h=4)
                    self.act(sqk[:], psk[:, :, 0:64], AF.Square, rd=[("ps", pk)], wr=["m_sqk"])
                    self.v("dve", "tensor_reduce", ["m_sqk"], ["m_ssk"], out=ssk[:, 0:4], in_=sqk[:], axis=AX.X, op=ALU.add)
                    self.v("dve", "tensor_copy", [("ps", pk), "m_vaug"], [("m_vaug", gi)], out=vaug[:, gi, :, 0:64], in_=psk[:, :, 64:128])
                    self.v("dve", "tensor_scalar", ["m_ssk"], ["m_ssk"], out=ssk[:, 0:4], in0=ssk[:, 0:4], scalar1=1.0 / 64, scalar2=None, op0=ALU.mult)
                    self.ln_exp(ssk[:, 0:4], ssk[:, 0:4], self.eps_t[:, 1:2], -0.5, ["m_ssk", "eps_t"], ["m_ssk"], "m_lnk", lnk[:, 0:4])
                    self.v("dve", "tensor_tensor", [("ps", pk), "m_ssk"], ["m_kn"], out=kn[:, :, 0:64], in0=psk[:, :, 0:64], in1=ssk[:, 0:4].unsqueeze(2).broadcast_to([128, 4, 64]), op=ALU.mult)
                    self.v("dve", "tensor_tensor", ["m_kn", "rep"], ["m_kn"], out=kn[:, :, 0:64], in0=kn[:, :, 0:64], in1=KG[:, 0:64].unsqueeze(1).broadcast_to([128, 4, 64]), op=ALU.mult)
                    pr = 3
                    self.proj_tm(384, 32, gi, pr)
                    self.act(sqr[:], self.PS[pr][:, 0:32], AF.Square, rd=[("ps", pr)], wr=["m_sqr"])
                    self.v("dve", "tensor_reduce", ["m_sqr"], [("m_ssr", 0)], out=ssr[:, 0:1], in_=sqr[:], axis=AX.X, op=ALU.add)
                    self.v("dve", "tensor_scalar", [("m_ssr", 0)], [("m_ssr", 0)], out=ssr[:, 0:1], in0=ssr[:, 0:1], scalar1=1.0 / 32, scalar2=None, op0=ALU.mult)
                    self.ln_exp(ssr[:, 2:3], ssr[:, 0:1], self.eps_t[:, 1:2], -0.5, [("m_ssr", 0), "eps_t"], [("m_ssr", 2)], ("m_ssr", 1), ssr[:, 1:2])
                    self.v("dve", "scalar_tensor_tensor", [("ps", pr), ("m_ssr", 2), "rep"], ["m_kr"], out=kr[:, 0, :], in0=self.PS[pr][:, 0:32], scalar=ssr[:, 2:3],
                           in1=KG[:, 64:96], op0=ALU.mult, op1=ALU.mult)
                    if gi >= 2:
                        self.rope(kr[:], 1, 8, tab, gi, t1k[:], t2k[:], ["m_kr"], "mk")
                    self.v("dve", "tensor_copy", ["m_kr", "m_kn"], ["m_kn"], out=kn[:, :, 64:96], in_=kr[:].broadcast_to([128, 4, 32]))
                    self.v("dve", "tensor_copy", ["m_kn"], ["m_knb"], out=knb[:], in_=kn[:])
                    pt = 3
                    for h in range(4):
                        self.mm(self.PS[pt][0:96, h * 128:(h + 1) * 128], knb[:, h, :], self.I_b, rd=["m_knb", "cstb"], wr=[("ps", pt)])
                    self.act(kT[0:96, :, gi * 128:(gi + 1) * 128], self.PS[pt][0:96, 0:512].rearrange("p (h t) -> p h t", h=4), AF.Identity,
                             rd=[("ps", pt)], wr=[("m_kT", gi)])

                self.P.interleave([q_body, k_body])
        self.release(m2)
        PT = [self.A(f"m_PT{i}", [128, 512], BF16) for i in range(3)]
        otok = self.A("m_otok", [128, 4, 256], BF16)
        rs = self.A("m_rs", [128, 4], F32)
        mixT = self.A("m_mixT", [128, 2, 512], BF16)
        scale = 96.0 ** -0.5
        for sti in range(5):
            if sti == 0 and not need_ctx:
                continue
            t0, n = ST[sti]
            nq = n // 128
            kts = [0, 1] if sti == 0 else list(range(NT))
            items = [(h, ki, kt) for h in range(4) for ki, kt in enumerate(kts)]

            def s_stage(it):
                h, ki, kt = it
                pa = self.rot("m_s", [0, 1])
                self.mm(self.PS[pa][:, 0:n], kT[0:96, h, kt * 128:(kt + 1) * 128], qT[0:96, h, t0:t0 + n], rd=["m_kT", "m_qT"], wr=[("ps", pa)])
                return pa

            pa_next = s_stage(items[0])
            for idx, (h, ki, kt) in enumerate(items):
                pa = pa_next
                if idx + 1 < len(items):
                    pa_next = s_stage(items[idx + 1])
                pt = self.rot("m_pt", [0, 1, 2])
                self.act(PT[pt][:, 0:n], self.PS[pa][:, 0:n], AF.Exp, rd=[("ps", pa)], wr=[("m_PT", pt)], scale=scale)
                for q in range(nq):
                    self.mm(self.PS[2 + q][:, 0:65], PT[pt][:, q * 128:(q + 1) * 128], vaug[:, kt, h, 0:65], start=(ki == 0), stop=(ki == len(kts) - 1),
                            rd=[("m_PT", pt), "m_vaug"], wr=[("ps", 2 + q)])
                if ki == len(kts) - 1:
                    for q in range(nq):
                        self.v("dve", "reciprocal", [("ps", 2 + q)], [("m_rs", q)], out=rs[:, q:q + 1], in_=self.PS[2 + q][:, 64:65])
                        self.act(otok[:, q, h * 64:(h + 1) * 64], self.PS[2 + q][:, 0:64], AF.Identity, rd=[("ps", 2 + q), ("m_rs", q)], wr=[("m_otok", q)], scale=rs[:, q:q + 1])
            for q in range(nq):
                self.tok_to_mixT(otok[:, q, :], ("m_otok", q), mixT, "m_mixT", q, s, 1, t0 // 128 + q)
            self.out_proj(l, s, sti, mixT, "m_mixT", n)
        self.release(m)

    def build(self):
        cfg = self.cfg
        self.prologue()
        phases = cfg.get("phases", ["ffn1", "a", "m", "w", "s", "ffn2"])
        for s in range(self.n_seq):
            self.load_seq(s)
            for l in range(self.n_layers):
                need_ctx = (l == 0)
                self.dma("sp", self.rep[:], self.rep_d[:, l, :], [], ["rep"], "rep")
                if "ffn1" in phases:
                    self.ffn(l, 0, s, False)
                if any(p in phases for p in "amws"):
                    mph = self.mark()
                    self.mixer_phase_begin(l, s)
                    if "a" in phases:
                        self.mixer_mlstm(l, s, need_ctx)
                    if "m" in phases:
                        self.mixer_mla(l, s, need_ctx)
                    if "w" in phases:
                        self.mixer_swa(l, s, need_ctx)
                    if "s" in phases:
                        self.mixer_ssd(l, s, need_ctx)
                    self.release(mph)
                if "ffn2" in phases:
                    self.ffn(l, 2, s, not need_ctx)
                self.P.barrier()
            self.store_seq(s)
        self.stats = self.P.emit()
        return self.nc


def make_inputs_for_core(inp, b0, n_seq, consts):
    cst, ropeM, ropeS = consts
    c_rows = np.concatenate([np.asarray(inp["c"], np.float32)[b0:b0 + n_seq], np.zeros((2 - n_seq, D), np.float32),
                             np.asarray(inp["c_ctx"], np.float32)[None]], 0)
    m = {
        "x": np.ascontiguousarray(inp["x"][b0:b0 + n_seq]), "ctx": np.ascontiguousarray(inp["ctx"][b0:b0 + n_seq]),
        "cst": cst, "ropeM": ropeM, "ropeS": ropeS,
    }
    m["cT"] = np.ascontiguousarray(np.moveaxis(_fm(c_rows), 1, 2))
    return m


_SHARED = ("w_mod", "ffn1_wi", "ffn2_wi", "ffn1_wo", "ffn2_wo", "w_in", "w_out", "mla_wq_b", "mla_wkv_b")


def run(inputs, cfg, core_batches):
    consts = _const_tables()
    pp = _pack_pp(inputs); rep = _pack_rep(inputs)
    shared = {k: np.ascontiguousarray(np.asarray(inputs[k], np.float32)) for k in _SHARED}
    shared["pp"] = pp; shared["rep"] = rep
    b = Builder(cfg)
    nc = b.build()
    in_maps = []
    for b0 in core_batches:
        mm_ = make_inputs_for_core(inputs, b0, b.n_seq, consts)
        mm_.update(shared)
        in_maps.append(mm_)
    res = run_bass_kernel_spmd(nc, in_maps, core_ids=list(range(len(core_batches))))
    return res, b


def kernel(**inputs):
    cfg = dict(n_seq=2, n_layers=2)
    res, b = run(inputs, cfg, [2 * i for i in range(8)])
    out = np.concatenate([np.asarray(r["out"], np.float32) for r in res.results], axis=0)
    return out
```
